# Optimizing a Trainium2 kernel written in Bass

```python
import jax
import jax.numpy as jnp
from jax import lax
import numpy as np

D_MODEL = 1024
BATCH = 8
SEQ = 4096
DEPTH = 2

GRID_W = 64
CTX_LEN = 256
NORM_EPS = 1e-6
N_MOD = 6

POOL_WINDOWS = (2, 4, 8, 16)
POOL_WIDTH = D_MODEL // 4
POOL_GROUP = POOL_WIDTH // len(POOL_WINDOWS)

MLA_HEADS = 8
MLA_NOPE = 64
MLA_ROPE = 32
MLA_V = 64
MLA_Q_RANK = 384
MLA_KV_RANK = 256
MLA_WIDTH = MLA_HEADS * MLA_V
Q_BLOCK = 128
ROPE_BASE = 10000.0

RWKV_HEAD = 64
RWKV_WIDTH = D_MODEL // 4
RWKV_HEADS = RWKV_WIDTH // RWKV_HEAD
DECAY_RANK = 64
AAA_RANK = 64
GATE_RANK = 128
RWKV_GN_EPS = 64e-5
RWKV_SIZES = (RWKV_WIDTH, RWKV_WIDTH, RWKV_WIDTH, DECAY_RANK, DECAY_RANK, AAA_RANK, AAA_RANK, GATE_RANK)
RWKV_IN = 3 * RWKV_WIDTH + 2 * DECAY_RANK + 2 * AAA_RANK + GATE_RANK

N_BRANCH = 3
IN_SIZES = (POOL_WIDTH, MLA_Q_RANK, MLA_KV_RANK, MLA_ROPE, RWKV_IN, N_BRANCH * D_MODEL)
IN_COLS = POOL_WIDTH + MLA_Q_RANK + MLA_KV_RANK + MLA_ROPE + RWKV_IN + N_BRANCH * D_MODEL
D_FF = 4 * D_MODEL

kernel_name = 'hybrid_pool_mla_rwkv7_dit_block'


def split_cols(z, sizes):
    idx = []
    acc = 0
    for s in sizes[:-1]:
        acc += s
        idx.append(acc)
    return jnp.split(z, idx, axis=-1)


def rms_norm(x, g, eps=NORM_EPS):
    xf = x.astype(jnp.float32)
    y = xf * lax.rsqrt(jnp.mean(xf * xf, axis=-1, keepdims=True) + eps)
    return (y * g).astype(x.dtype)


def modulate(h, shift, scale):
    return h * (1.0 + scale) + shift


def axial_rope_tables(n_tokens):
    rows = n_tokens // GRID_W
    row = jnp.repeat(jnp.arange(rows), GRID_W).astype(jnp.float32)
    col = jnp.tile(jnp.arange(GRID_W), rows).astype(jnp.float32)
    n_freq = MLA_ROPE // 4
    inv_freq = jnp.power(ROPE_BASE, -jnp.arange(n_freq, dtype=jnp.float32) / n_freq)
    ang = jnp.concatenate([row[:, None] * inv_freq, col[:, None] * inv_freq], axis=-1)
    return jnp.cos(ang), jnp.sin(ang)


def apply_rope(x, cos, sin):
    half = MLA_ROPE // 2
    cos = cos.astype(x.dtype)
    sin = sin.astype(x.dtype)
    x1, x2 = x[..., :half], x[..., half:]
    return jnp.concatenate([x1 * cos - x2 * sin, x1 * sin + x2 * cos], axis=-1)


def pool_mixer(u, pool_w, pool_scale):
    B, L, _ = u.shape
    uf = u.astype(jnp.float32)
    csum = jnp.concatenate([jnp.zeros((B, 1, POOL_WIDTH), jnp.float32), jnp.cumsum(uf, axis=1)], axis=1)
    t = jnp.arange(L)
    groups = []
    for gi, win in enumerate(POOL_WINDOWS):
        sl = slice(gi * POOL_GROUP, (gi + 1) * POOL_GROUP)
        lo = jnp.clip(t - win // 2, 0, L)
        hi = jnp.clip(t + win // 2, 0, L)
        total = jnp.take(csum[..., sl], hi, axis=1) - jnp.take(csum[..., sl], lo, axis=1)
        mean = total / (hi - lo).astype(jnp.float32)[:, None]
        groups.append(mean - uf[..., sl])
    pooled = jnp.stack(groups, axis=2).astype(u.dtype)
    y = jnp.einsum('blgc,gcd->blgd', pooled, pool_w).reshape(B, L, POOL_WIDTH)
    return y * pool_scale


def mla_queries(q_c, p):
    B, L, _ = q_c.shape
    q = (rms_norm(q_c, p['mla_q_norm']) @ p['mla_w_uq']).reshape(B, L, MLA_HEADS, MLA_NOPE + MLA_ROPE)
    q_nope = rms_norm(q[..., :MLA_NOPE], p['qk_gain_q'][:MLA_NOPE])
    q_rope = rms_norm(q[..., MLA_NOPE:], p['qk_gain_q'][MLA_NOPE:])
    return q_nope, q_rope


def mla_keys(kv_c, k_r, p):
    B, L, _ = kv_c.shape
    kv = (rms_norm(kv_c, p['mla_kv_norm']) @ p['mla_w_ukv']).reshape(B, L, MLA_HEADS, MLA_NOPE + MLA_V)
    k_nope = rms_norm(kv[..., :MLA_NOPE], p['qk_gain_k'][:MLA_NOPE])
    v = kv[..., MLA_NOPE:]
    k_rope = rms_norm(k_r, p['qk_gain_k'][MLA_NOPE:])
    return k_nope, k_rope, v


def attend(qn, qr, kn, kr, v):
    s = jnp.einsum('bqhd,bkhd->bhqk', qn, kn) + jnp.einsum('bqhd,bkd->bhqk', qr, kr)
    s = s.astype(jnp.float32) * ((MLA_NOPE + MLA_ROPE) ** -0.5)
    prob = jax.nn.softmax(s, axis=-1)
    return jnp.einsum('bhqk,bkhd->bqhd', prob.astype(v.dtype), v)


def latent_attention(qn, qr, kn, kr, v):
    B, L, H, _ = qn.shape
    nb = L // Q_BLOCK

    def blocks(a):
        return jnp.moveaxis(a.reshape((B, nb, Q_BLOCK) + a.shape[2:]), 1, 0)

    out = lax.map(lambda q: attend(q[0], q[1], kn, kr, v), (blocks(qn), blocks(qr)))
    return jnp.moveaxis(out, 0, 1).reshape(B, L, H * MLA_V)


def bidir_shift(z, mu_prev, mu_next):
    zp = jnp.pad(z, ((0, 0), (1, 0), (0, 0)))[:, :-1]
    zn = jnp.pad(z, ((0, 0), (0, 1), (0, 0)))[:, 1:]
    return z + mu_prev * (zp - z) + mu_next * (zn - z)


def rwkv_prepare(z, p):
    B, L, _ = z.shape
    z = bidir_shift(z.astype(jnp.float32), p['rwkv_mu'][0], p['rwkv_mu'][1])
    r, k, v, wd_f, wd_b, ad_f, ad_b, gd = split_cols(z, RWKV_SIZES)

    def heads(t):
        return t.reshape(B, L, RWKV_HEADS, RWKV_HEAD).astype(jnp.float32)

    kk = heads(k * p['rwkv_kk'])
    kk = kk * lax.rsqrt(jnp.maximum(jnp.sum(kk * kk, axis=-1, keepdims=True), 1e-24))
    dirs = []
    for d, (wd, ad) in enumerate(((wd_f, ad_f), (wd_b, ad_b))):
        w_log = -jax.nn.softplus(-(p['rwkv_w0'][d] + jnp.tanh(wd) @ p['rwkv_w2'][d])) - 0.5
        decay = jnp.exp(-jnp.exp(w_log.astype(jnp.float32)))
        a = jax.nn.sigmoid(p['rwkv_a0'][d] + ad @ p['rwkv_a2'][d])
        k_d = k * (1.0 + (a - 1.0) * p['rwkv_ka'][d])
        a_h = heads(a)
        dirs.append((heads(decay), heads(k_d), -kk, kk * a_h))
    return heads(r), heads(v), gd, dirs


def rwkv7_scan(s0, r, decay, k, a_vec, b_vec, v, reverse, emit):
    def step(S, inp):
        r_t, w_t, k_t, a_t, b_t, v_t = inp
        sa = jnp.einsum('bhvk,bhk->bhv', S, a_t)
        S = S * w_t[:, :, None, :] + sa[..., None] * b_t[:, :, None, :] + v_t[..., None] * k_t[:, :, None, :]
        y = jnp.einsum('bhvk,bhk->bhv', S, r_t) if emit else None
        return S, y

    xs = tuple(jnp.moveaxis(t, 1, 0) for t in (r, decay, k, a_vec, b_vec, v))
    S, ys = lax.scan(step, s0, xs, reverse=reverse)
    return S, (jnp.moveaxis(ys, 0, 1) if emit else None)


def rwkv_readout(y, r, v, k_f, k_b, gd, p):
    B, L = y.shape[:2]
    mu = jnp.mean(y, axis=-1, keepdims=True)
    var = jnp.mean(jnp.square(y - mu), axis=-1, keepdims=True)
    yn = ((y - mu) * lax.rsqrt(var + RWKV_GN_EPS)).reshape(B, L, RWKV_WIDTH) * p['rwkv_ln_w'] + p['rwkv_ln_b']
    bonus = jnp.sum(r * (0.5 * (k_f + k_b)) * p['rwkv_rk'], axis=-1, keepdims=True) * v
    g = jax.nn.sigmoid(gd) @ p['rwkv_g2']
    return (yn + bonus.reshape(B, L, RWKV_WIDTH)) * g


def merge_branches(o_pool, o_mla, o_rwkv, gate_cols, p):
    g_pool, g_mla, g_rwkv = jnp.split(jax.nn.sigmoid(gate_cols), N_BRANCH, axis=-1)
    m = (g_pool * (o_pool @ p['w_br_pool'])
         + g_mla * (o_mla @ p['w_br_mla'])
         + g_rwkv * (o_rwkv @ p['w_br_rwkv']))
    return m @ p['w_o']


def mixer_sublayer(h_l, h_c, p, cos, sin, need_ctx):
    B, L, _ = h_l.shape
    Lc = h_c.shape[1]
    pool_l, qc_l, kvc_l, kr_l, rw_l, gate_l = split_cols(h_l @ p['w_in'], IN_SIZES)
    pool_c, qc_c, kvc_c, kr_c, rw_c, gate_c = split_cols(h_c @ p['w_in'], IN_SIZES)

    o_pool_l = pool_mixer(pool_l, p['pool_w'], p['pool_scale'])

    kn_c, krn_c, v_c = mla_keys(kvc_c, kr_c, p)
    kn_l, krn_l, v_l = mla_keys(kvc_l, kr_l, p)
    krn_l = apply_rope(krn_l, cos, sin)
    qn_l, qr_l = mla_queries(qc_l, p)
    qr_l = apply_rope(qr_l, cos[:, None, :], sin[:, None, :])
    o_mla_l = latent_attention(qn_l, qr_l,
                               jnp.concatenate([kn_l, kn_c], axis=1),
                               jnp.concatenate([krn_l, krn_c], axis=1),
                               jnp.concatenate([v_l, v_c], axis=1))

    r_c, vr_c, gd_c, dirs_c = rwkv_prepare(rw_c, p)
    r_l, vr_l, gd_l, dirs_l = rwkv_prepare(rw_l, p)
    s0 = jnp.zeros((B, RWKV_HEADS, RWKV_HEAD, RWKV_HEAD), jnp.float32)
    ys_l = []
    ys_c = []
    for d in range(2):
        rev = d == 1
        s_ctx, yc = rwkv7_scan(s0, r_c, *dirs_c[d], vr_c, rev, need_ctx)
        _, yl = rwkv7_scan(s_ctx, r_l, *dirs_l[d], vr_l, rev, True)
        ys_l.append(yl)
        ys_c.append(yc)
    o_rwkv_l = rwkv_readout(ys_l[0] + ys_l[1], r_l, vr_l, dirs_l[0][1], dirs_l[1][1], gd_l, p)

    out_l = merge_branches(o_pool_l, o_mla_l, o_rwkv_l, gate_l, p)
    if not need_ctx:
        return out_l, None

    o_pool_c = pool_mixer(pool_c, p['pool_w'], p['pool_scale'])
    qn_c, qr_c = mla_queries(qc_c, p)
    o_mla_c = attend(qn_c, qr_c, kn_c, krn_c, v_c).reshape(B, Lc, MLA_WIDTH)
    o_rwkv_c = rwkv_readout(ys_c[0] + ys_c[1], r_c, vr_c, dirs_c[0][1], dirs_c[1][1], gd_c, p)
    out_c = merge_branches(o_pool_c, o_mla_c, o_rwkv_c, gate_c, p)
    return out_l, out_c


def sq_relu_mlp(h, w1, w2):
    return jnp.square(jax.nn.relu(h @ w1)) @ w2


def setup_inputs(seed: int = 0) -> dict:
    key = jax.random.key(seed)
    keys = iter(jax.random.split(key, 48))

    def nrm(shape, scale):
        return scale * jax.random.normal(next(keys), shape, jnp.float32)

    D = D_MODEL
    return {
        'x': nrm((BATCH, SEQ, D), 1.0),
        'c': nrm((BATCH, D), 1.0),
        'ctx': nrm((BATCH, CTX_LEN, D), 1.0),
        'c_ctx': nrm((D,), 1.0),
        'norm1_g': 1.0 + nrm((DEPTH, D), 0.05),
        'norm2_g': 1.0 + nrm((DEPTH, D), 0.05),
        'w_ada': nrm((DEPTH, D, N_MOD * D), D ** -0.5),
        'b_ada': nrm((DEPTH, N_MOD * D), 0.02),
        'w_in': nrm((DEPTH, D, IN_COLS), D ** -0.5),
        'pool_w': nrm((DEPTH, len(POOL_WINDOWS), POOL_GROUP, POOL_GROUP), POOL_GROUP ** -0.5),
        'pool_scale': 1.0 + nrm((DEPTH, POOL_WIDTH), 0.1),
        'mla_q_norm': 1.0 + nrm((DEPTH, MLA_Q_RANK), 0.05),
        'mla_w_uq': nrm((DEPTH, MLA_Q_RANK, MLA_HEADS * (MLA_NOPE + MLA_ROPE)), MLA_Q_RANK ** -0.5),
        'mla_kv_norm': 1.0 + nrm((DEPTH, MLA_KV_RANK), 0.05),
        'mla_w_ukv': nrm((DEPTH, MLA_KV_RANK, MLA_HEADS * (MLA_NOPE + MLA_V)), MLA_KV_RANK ** -0.5),
        'qk_gain_q': 1.0 + nrm((DEPTH, MLA_NOPE + MLA_ROPE), 0.05),
        'qk_gain_k': 1.0 + nrm((DEPTH, MLA_NOPE + MLA_ROPE), 0.05),
        'rwkv_mu': jax.random.uniform(next(keys), (DEPTH, 2, RWKV_IN), jnp.float32, 0.0, 0.5),
        'rwkv_w0': jnp.linspace(-6.0, -1.0, RWKV_WIDTH, dtype=jnp.float32) + nrm((DEPTH, 2, RWKV_WIDTH), 0.3),
        'rwkv_w2': nrm((DEPTH, 2, DECAY_RANK, RWKV_WIDTH), 0.1 * DECAY_RANK ** -0.5),
        'rwkv_a0': nrm((DEPTH, 2, RWKV_WIDTH), 0.1),
        'rwkv_a2': nrm((DEPTH, 2, AAA_RANK, RWKV_WIDTH), 0.5 * AAA_RANK ** -0.5),
        'rwkv_ka': 1.0 + nrm((DEPTH, 2, RWKV_WIDTH), 0.05),
        'rwkv_kk': 0.85 + nrm((DEPTH, RWKV_WIDTH), 0.05),
        'rwkv_rk': nrm((DEPTH, RWKV_HEADS, RWKV_HEAD), 0.1),
        'rwkv_g2': nrm((DEPTH, GATE_RANK, RWKV_WIDTH), GATE_RANK ** -0.5),
        'rwkv_ln_w': 1.0 + nrm((DEPTH, RWKV_WIDTH), 0.05),
        'rwkv_ln_b': nrm((DEPTH, RWKV_WIDTH), 0.02),
        'w_br_pool': nrm((DEPTH, POOL_WIDTH, D), POOL_WIDTH ** -0.5),
        'w_br_mla': nrm((DEPTH, MLA_WIDTH, D), MLA_WIDTH ** -0.5),
        'w_br_rwkv': nrm((DEPTH, RWKV_WIDTH, D), RWKV_WIDTH ** -0.5),
        'w_o': nrm((DEPTH, D, D), D ** -0.5),
        'mlp_w1': nrm((DEPTH, D, D_FF), D ** -0.5),
        'mlp_w2': nrm((DEPTH, D_FF, D), D_FF ** -0.5),
    }


def reference(x, c, ctx, c_ctx, norm1_g, norm2_g, w_ada, b_ada, w_in, pool_w, pool_scale,
              mla_q_norm, mla_w_uq, mla_kv_norm, mla_w_ukv, qk_gain_q, qk_gain_k,
              rwkv_mu, rwkv_w0, rwkv_w2, rwkv_a0, rwkv_a2, rwkv_ka, rwkv_kk, rwkv_rk, rwkv_g2,
              rwkv_ln_w, rwkv_ln_b, w_br_pool, w_br_mla, w_br_rwkv, w_o, mlp_w1, mlp_w2):
    B, L, D = x.shape
    cos, sin = axial_rope_tables(L)
    silu_c = jax.nn.silu(c)
    silu_cc = jax.nn.silu(c_ctx)
    for i in range(DEPTH):
        need_ctx = i < DEPTH - 1
        p = {
            'w_in': w_in[i], 'pool_w': pool_w[i], 'pool_scale': pool_scale[i],
            'mla_q_norm': mla_q_norm[i], 'mla_w_uq': mla_w_uq[i],
            'mla_kv_norm': mla_kv_norm[i], 'mla_w_ukv': mla_w_ukv[i],
            'qk_gain_q': qk_gain_q[i], 'qk_gain_k': qk_gain_k[i],
            'rwkv_mu': rwkv_mu[i], 'rwkv_w0': rwkv_w0[i], 'rwkv_w2': rwkv_w2[i],
            'rwkv_a0': rwkv_a0[i], 'rwkv_a2': rwkv_a2[i], 'rwkv_ka': rwkv_ka[i],
            'rwkv_kk': rwkv_kk[i], 'rwkv_rk': rwkv_rk[i], 'rwkv_g2': rwkv_g2[i],
            'rwkv_ln_w': rwkv_ln_w[i], 'rwkv_ln_b': rwkv_ln_b[i],
            'w_br_pool': w_br_pool[i], 'w_br_mla': w_br_mla[i], 'w_br_rwkv': w_br_rwkv[i],
            'w_o': w_o[i],
        }
        mod_l = (silu_c @ w_ada[i] + b_ada[i]).reshape(B, 1, N_MOD, D)
        sh1, sc1, g1, sh2, sc2, g2 = (mod_l[:, :, j] for j in range(N_MOD))
        mod_c = (silu_cc @ w_ada[i] + b_ada[i]).reshape(N_MOD, D)
        csh1, csc1, cg1, csh2, csc2, cg2 = (mod_c[j] for j in range(N_MOD))

        h_l = modulate(rms_norm(x, norm1_g[i]), sh1, sc1)
        h_c = modulate(rms_norm(ctx, norm1_g[i]), csh1, csc1)
        o_l, o_c = mixer_sublayer(h_l, h_c, p, cos, sin, need_ctx)
        x = x + g1 * o_l
        x = x + g2 * sq_relu_mlp(modulate(rms_norm(x, norm2_g[i]), sh2, sc2), mlp_w1[i], mlp_w2[i])
        if need_ctx:
            ctx = ctx + cg1 * o_c
            ctx = ctx + cg2 * sq_relu_mlp(modulate(rms_norm(ctx, norm2_g[i]), csh2, csc2), mlp_w1[i], mlp_w2[i])
    return x
```

```python
import numpy as np
import concourse.bass as bass
import concourse.mybir as mybir

F32 = mybir.dt.float32
BF16 = mybir.dt.bfloat16
AF = mybir.ActivationFunctionType
ALU = mybir.AluOpType
AX = mybir.AxisListType


class Tk:
    __slots__ = ("name", "t", "w", "r", "dsem", "dcnt", "psum")

    def __init__(self, name, t=None):
        self.name = name
        self.t = t
        self.w = None
        self.r = {}
        self.dsem = None
        self.dcnt = 0
        self.psum = False

    def __getitem__(self, k):
        return self.t[k]


class Kern:
    EP = 16000

    def __init__(self, nc, es):
        self.nc = nc
        self.es = es
        self.H = {"pe": nc.tensor, "act": nc.scalar, "dve": nc.vector, "pool": nc.gpsimd, "sp": nc.sync}
        self.q = {e: [] for e in self.H}
        self.known = {e: {} for e in self.H}
        self.nsem = 0
        self.dsems = []
        self.dtks = []
        self.lastc = {e: -1 for e in self.H}

    def sb(self, name, shape, dt, stack=None):
        self.uid = getattr(self, "uid", 0) + 1
        name = f"{name}_{self.uid}"
        t = (stack or self.es).enter_context(self.nc.sbuf_tensor(name, list(shape), dt))
        tk = Tk(name, t)
        return tk

    def ps(self, name, shape, dt=F32, stack=None):
        self.uid = getattr(self, "uid", 0) + 1
        name = f"{name}_{self.uid}"
        t = (stack or self.es).enter_context(self.nc.psum_tensor(name, list(shape), dt))
        tk = Tk(name, t)
        tk.psum = True
        return tk

    def dram(self, name, shape, dt, kind="Internal"):
        t = self.nc.dram_tensor(name, list(shape), dt, kind=kind).ap()
        return Tk(name, t)

    def newsem(self, name):
        self.nsem += 1
        return self.es.enter_context(self.nc.semaphore(name))

    def _need(self, e, tok, waits, same_ok):
        if tok is None:
            return
        if tok[0] == "e":
            _, f, idx = tok
            if f == e and (same_ok or e in ("pe", "sp")):
                return
            if self.known[e].get(f, -1) >= idx:
                return
            self.known[e][f] = idx
            waits.append(tok)
            self.q[f][idx][2] = True
        else:
            _, sid, val = tok
            if self.known[e].get(("d", sid), 0) >= val:
                return
            self.known[e][("d", sid)] = val
            waits.append(tok)

    def _deps(self, e, reads, writes, nowaw=False):
        waits = []
        for t in reads:
            self._need(e, t.w, waits, False)
            if t.psum:
                for tok in t.r.values():
                    self._need(e, tok, waits, True)
        for t in writes:
            if not (nowaw and t.w is not None and t.w[0] == "d"):
                self._need(e, t.w, waits, True)
            for tok in t.r.values():
                self._need(e, tok, waits, True)
        return waits

    def ins(self, e, reads, writes, _f, *args, _selfwait=None, **kwargs):
        fn = lambda: _f(*args, **kwargs)
        waits = self._deps(e, reads, writes)
        if _selfwait is not None and self.known[e].get(e, -1) < _selfwait[2]:
            self.known[e][e] = _selfwait[2]
            waits.append(_selfwait)
            self.q[e][_selfwait[2]][2] = True
        idx = len(self.q[e])
        self.q[e].append([fn, waits, False, None])
        self.lastc[e] = idx
        tok = ("e", e, idx)
        for t in reads:
            t.r[e] = tok
        for t in writes:
            t.w = tok
            t.r = {}
        return tok

    def dma(self, e, out_ap, in_ap, reads, write, nowaw=True, **kw):
        waits = self._deps(e, reads, [write], nowaw=nowaw)
        if write.dsem is None:
            free = getattr(self, "dfree", None)
            if free is None:
                free = self.dfree = []
                self.dcount = {}
            if free:
                write.dsem = free.pop()
            else:
                write.dsem = len(self.dsems)
                self.dsems.append(self.newsem(f"d{len(self.dsems)}"))
                self.dcount[write.dsem] = 0
            self.dtks.append(write)
        self.dcount[write.dsem] += 16
        write.dcnt = self.dcount[write.dsem]
        tok = ("d", write.dsem, write.dcnt)
        H = self.H[e]
        self.q[e].append([lambda: H.dma_start(out=out_ap, in_=in_ap, **kw), waits, False, tok])
        for t in reads:
            t.r[("d", write.dsem)] = tok
        write.w = tok
        return tok

    def barrier(self):
        last = dict(self.lastc)
        for e in self.H:
            waits = []
            for f in self.H:
                if f != e and last[f] >= 0:
                    self._need(e, ("e", f, last[f]), waits, False)
            for tk in self.dtks:
                self._need(e, ("d", tk.dsem, tk.dcnt), waits, False)
            self.q[e].append([None, waits, False, None])
        for tk in self.dtks:
            self.dfree.append(tk.dsem)
            tk.dsem = None
        self.dtks = []

    def emit(self):
        nc = self.nc
        sigmap = {}
        for e, lst in self.q.items():
            cnt = 0
            m = {}
            sems = []
            for idx, rec in enumerate(lst):
                if rec[2]:
                    ep, v = divmod(cnt, self.EP)
                    if ep >= len(sems):
                        sems.append(self.newsem(f"e_{e}_{ep}"))
                    m[idx] = (sems[ep], v + 1)
                    cnt += 1
            sigmap[e] = m
        for e, lst in self.q.items():
            H = self.H[e]
            for idx, rec in enumerate(lst):
                fn, waits, sig, dtok = rec
                for tok in waits:
                    if tok[0] == "e":
                        s, v = sigmap[tok[1]][tok[2]]
                        H.wait_ge(s, v)
                    else:
                        H.wait_ge(self.dsems[tok[1]], tok[2])
                if fn is None:
                    assert not sig
                    continue
                ins = fn()
                if dtok is not None:
                    ins.then_inc(self.dsems[dtok[1]], 16)
                    assert not sig
                elif sig:
                    ins.then_inc(sigmap[e][idx][0], 1)
        print("emitted:", {e: len(l) for e, l in self.q.items()}, "sems", self.nsem)


from concourse.bass_utils import run_bass_kernel_spmd
import contextlib
import ml_dtypes

T_ALL = 4352
LC = 256
EPS = 1e-6


EXTRA_INPUTS = [
    ("bands", [128, 20, 128]), ("pw_bd", [2, 2, 128, 128]), ("pscale", [128, 2, 2]),
    ("w_uq", [2, 384, 768]), ("w_ukv_k", [2, 256, 512]), ("w_ukv_v", [2, 256, 512]),
    ("Cq", [128, 128]), ("Prot", [128, 128]), ("gq96", [128, 2]), ("gk96", [128, 2]),
    ("qng", [128, 2, 3]), ("kvng", [128, 2, 2]), ("cs", [32, 2, 4096]),
    ("r_mu", [128, 2, 9, 2]), ("r_kk", [128, 2, 2]), ("r_w0", [128, 2, 2, 2]), ("r_a0", [128, 2, 2, 2]), ("r_ka", [128, 2, 2, 2]),
    ("r_rk", [128, 2, 2]), ("r_w2", [2, 2, 128, 256]), ("r_a2", [2, 2, 128, 256]), ("r_g2", [2, 128, 256]),
    ("r_lnwb", [2, 128, 256]), ("r_lnbb", [2, 128, 256]), ("blk64", [128, 128]), ("ind2", [128, 2]),
    ("masks", [128, 4, 128]), ("rmask", [128, 512]), ("lvmask", [128, 14, 128]),
]


def tiles_of(n, last_layer=False):
    ts = [] if last_layer else [(0, LC, 1)]
    t = LC
    while t < T_ALL:
        ts.append((t, n, 0))
        t += n
    return ts


class Ctx:
    pass


def build(nc, es, dbg=None, stop_after=None):
    K = Kern(nc, es)
    C = Ctx()
    C.rdbg = (dbg or {}).get("rdbg", 9)
    C.pdbg = (dbg or {}).get("pdbg", 9)
    C.sdbg = (dbg or {}).get("sdbg", 9)
    C.bdbg = (dbg or {}).get("bdbg", 9)
    C.maxfly = (dbg or {}).get("maxfly", 4)
    C.ustop = (dbg or {}).get("ustop", 0)
    C.onebank = (dbg or {}).get("onebank", False)
    C.K = K
    C.nc = nc
    dbg = dbg or {}
    ext = lambda name, shape, dt=F32: K.dram(name, shape, dt, kind="ExternalInput")
    I = {}
    I["xin"] = ext("xin", [T_ALL, 1024])
    I["ccT"] = ext("ccT", [128, 16])
    I["w_ada"] = ext("w_ada", [2, 1024, 6144])
    I["badaT"] = ext("badaT", [128, 96])
    I["ng"] = ext("ng", [128, 32])
    I["w_in"] = ext("w_in", [2, 1024, 5152])
    I["w_br"] = ext("w_br", [2, 1024, 1024])
    I["w_o"] = ext("w_o", [2, 1024, 1024])
    I["mlp_w1"] = ext("mlp_w1", [2, 1024, 4096])
    I["mlp_w2"] = ext("mlp_w2", [2, 4096, 1024])
    I["ident"] = ext("ident", [128, 128])
    for nm, shp in EXTRA_INPUTS:
        I[nm] = ext(nm, shp)
    C.I = I
    out = K.dram("out", [4096, 1024], F32, kind="ExternalOutput")
    C.out = out
    douts = dbg.get("outs", ())
    scr = lambda name, shape, dt: K.dram(name, shape, dt, kind=("ExternalOutput" if name in douts else "Internal"))
    C.scr = scr
    C.xT = [scr("xT0", [1024, T_ALL], F32), scr("xT1", [1024, T_ALL], F32)]
    C.zT = scr("zT", [15, 128, T_ALL], F32)
    C.u_d = scr("u_d", [T_ALL, 256], BF16)
    C.obr = scr("obr", [1024, T_ALL], BF16)
    C.moddbg = scr("moddbg", [128, 192], F32)
    C.gtok_d = scr("gtok", [T_ALL, 256], BF16)
    if "obr_in" in dbg:
        C.obr_in = ext("obr_in", [1024, T_ALL])
    C.ident = K.sb("ident_s", [128, 128], F32)
    C.identb = K.sb("identb_s", [128, 128], BF16)
    C.ones = K.sb("ones_s", [128, 128], F32)
    C.epsb = K.sb("epsb_s", [128, 1], F32)
    C.gneps = K.sb("gneps_s", [128, 1], F32)
    C.mod = K.sb("mod_s", [128, 2, 48, 2], F32)
    C.gm = K.sb("gm_s", [128, 2, 2, 8, 2], F32)
    C.ng = K.sb("ng_s", [128, 2, 2, 8], F32)
    K.dma("sp", C.ident[:], I["ident"][:, :], [I["ident"]], C.ident)
    K.dma("sp", C.ng[:].rearrange("p a b c -> p (a b c)"), I["ng"][:, :], [I["ng"]], C.ng)
    K.ins("dve", [], [C.ones], nc.vector.memset, C.ones[:], 1.0)
    K.ins("dve", [], [C.epsb], nc.vector.memset, C.epsb[:], EPS)
    K.ins("dve", [], [C.gneps], nc.vector.memset, C.gneps[:], 64e-5)
    K.ins("dve", [C.ident], [C.identb], nc.vector.tensor_copy, out=C.identb[:], in_=C.ident[:])

    SKIP = dbg.get("skip_pre", False)
    if not SKIP:
        phase_M(C)
    K.barrier()
    if "moddbg" in douts:
        K.dma("sp", C.moddbg[:, :], C.mod[:].rearrange("p l m s -> p (l m s)"), [C.mod], C.moddbg)
    if not SKIP:
        phase_T0(C)
    K.barrier()
    cur = 0
    for li in range(2):
        last = li == 1
        if not SKIP:
            phase_A(C, li, cur)
        K.barrier()
        if stop_after == ("A", li):
            break
        if "obr_in" in dbg:
            phase_dbg_obr(C)
            K.barrier()
        else:
            mix = dbg.get("mix", ("P", "L", "R"))
            if "P" in mix:
                phase_P(C, li)
                K.barrier()
            if "L" in mix:
                phase_MLA(C, li)
                K.barrier()
            if "R" in mix:
                phase_R(C, li)
                K.barrier()
        if stop_after == ("X", li):
            break
        phase_C1(C, li, cur)
        K.barrier()
        cur ^= 1
        phase_C2(C, li, cur)
        K.barrier()
        cur ^= 1
        if stop_after == ("C", li):
            break
    K.barrier()
    K.emit()
    K.C = C
    return K


def phase_dbg_obr(C):
    K, nc = C.K, C.nc
    with contextlib.ExitStack() as st:
        a = K.sb("dbo_a", [128, 8, 512], F32, st)
        b = K.sb("dbo_b", [128, 8, 512], BF16, st)
        for t0 in range(0, T_ALL, 512):
            n = min(512, T_ALL - t0)
            K.dma("sp", a[:, :, :n], C.obr_in[:, t0:t0 + n].rearrange("(c p) t -> p c t", p=128), [C.obr_in], a)
            K.ins("dve", [a], [b], nc.vector.tensor_copy, out=b[:, :, :n], in_=a[:, :, :n])
            K.dma("sp", C.obr[:, t0:t0 + n].rearrange("(c p) t -> p c t", p=128), b[:, :, :n], [b], C.obr)


def phase_M(C):
    K, nc, I = C.K, C.nc, C.I
    with contextlib.ExitStack() as st:
        cc = K.sb("m_cc", [128, 8, 2], F32, st)
        sc = K.sb("m_sc", [128, 8, 2], F32, st)
        bada = K.sb("m_bada", [128, 2, 48], F32, st)
        wb = [K.sb(f"m_w{j}", [128, 8, 768], F32, st) for j in range(2)]
        psMs = [K.ps(f"m_ps{j}", [128, 512], F32, st) for j in range(4)]
        mraw = K.sb("m_raw", [128, 48, 2], F32, st)
        K.dma("sp", cc[:].rearrange("p k s -> p (k s)"), I["ccT"][:, :], [I["ccT"]], cc)
        K.dma("sp", bada[:].rearrange("p l m -> p (l m)"), I["badaT"][:, :], [I["badaT"]], bada)
        K.ins("act", [cc], [sc], nc.scalar.activation, out=sc[:], in_=cc[:], func=AF.Silu)
        n = 0
        for li in range(2):
            for blk in range(8):
                w = wb[n % 2]
                n += 1
                for kc in range(8):
                    K.dma("sp" if kc % 2 == 0 else "act", w[:, kc, :], I["w_ada"][li, kc * 128:(kc + 1) * 128, blk * 768:(blk + 1) * 768], [I["w_ada"]], w)
                for m6 in range(6):
                    m = blk * 6 + m6
                    pm = psMs[m % 4]
                    for kc in range(8):
                        K.ins("pe", [w, sc], [pm], nc.tensor.matmul, pm[:, 0:2], lhsT=w[:, kc, m6 * 128:(m6 + 1) * 128], rhs=sc[:, kc, :], start=(kc == 0), stop=(kc == 7))
                    K.ins("dve", [pm], [mraw], nc.vector.tensor_copy, out=mraw[:, m, :], in_=pm[:, 0:2])
            K.ins("dve", [mraw, bada], [C.mod], nc.vector.tensor_tensor, out=C.mod[:, li, :, :], in0=mraw[:], in1=bada[:, li, :].unsqueeze(2).broadcast_to([128, 48, 2]), op=ALU.add)
            for which in range(2):
                K.ins("dve", [C.mod, C.ng], [C.gm], nc.vector.scalar_tensor_tensor,
                    out=C.gm[:, which, li, :, :], in0=C.mod[:, li, 8 + 24 * which:16 + 24 * which, :], scalar=1.0,
                    in1=C.ng[:, which, li, :].unsqueeze(2).broadcast_to([128, 8, 2]), op0=ALU.add, op1=ALU.mult)


def phase_T0(C):
    K, nc, I = C.K, C.nc, C.I
    with contextlib.ExitStack() as st:
        xin = [K.sb(f"t0_x{j}", [128, 4, 1024], F32, st) for j in range(2)]
        xo = [K.sb(f"t0_o{j}", [128, 8, 512], F32, st) for j in range(2)]
        pss = [K.ps(f"t0_p{j}", [128, 512], F32, st) for j in range(4)]
        npz = 0
        for it, t0 in enumerate(range(0, T_ALL, 512)):
            n = min(512, T_ALL - t0)
            nj = n // 128
            xi = xin[it % 2]
            o = xo[it % 2]
            K.dma("sp", xi[:, :nj, :], I["xin"][t0:t0 + n, :].rearrange("(j p) d -> p j d", p=128), [I["xin"]], xi)
            for c in range(8):
                p = pss[npz % 4]
                npz += 1
                for j in range(nj):
                    K.ins("pe", [xi, C.ident], [p], nc.tensor.transpose, out=p[:, j * 128:(j + 1) * 128], in_=xi[:, j, c * 128:(c + 1) * 128], identity=C.ident[:])
                if c % 2 == 0:
                    K.ins("dve", [p], [o], nc.vector.tensor_copy, out=o[:, c, :n], in_=p[:, :n])
                else:
                    K.ins("act", [p], [o], nc.scalar.copy, out=o[:, c, :n], in_=p[:, :n])
            K.dma("act", C.xT[0][:, t0:t0 + n].rearrange("(c p) t -> p c t", p=128), o[:, :, :n], [o], C.xT[0])


def compute_h_g(C, xt, wk, hT, psS, rstd, which, li, s, n):
    K, nc = C.K, C.nc
    K.ins("act", [xt], [wk], nc.scalar.activation, out=wk[:, :, :n], in_=xt[:, :, :n], func=AF.Square)
    yield
    for c in range(8):
        K.ins("pe", [C.ones, wk], [psS], nc.tensor.matmul, psS[:, :n], lhsT=C.ones[:], rhs=wk[:, c, :n], start=(c == 0), stop=(c == 7))
        if c % 4 == 3:
            yield
    K.ins("act", [psS, C.epsb], [rstd], nc.scalar.activation, out=rstd[:, :n], in_=psS[:, :n], func=AF.Ln, scale=1.0 / 1024.0, bias=C.epsb[:, 0:1])
    yield
    K.ins("act", [rstd], [rstd], nc.scalar.activation, out=rstd[:, :n], in_=rstd[:, :n], func=AF.Exp, scale=-0.5)
    yield
    shb = 0 if which == 0 else 24
    for c in range(8):
        K.ins("dve", [xt, C.gm, rstd], [wk], nc.vector.scalar_tensor_tensor, out=wk[:, c, :n], in0=xt[:, c, :n], scalar=C.gm[:, which, li, c, s:s + 1], in1=rstd[:, :n], op0=ALU.mult, op1=ALU.mult)
        yield
        K.ins("act", [wk, C.mod], [hT], nc.scalar.activation, out=hT[:, c, :n], in_=wk[:, c, :n], func=AF.Identity, bias=C.mod[:, li, shb + c, s:s + 1], scale=1.0)
        yield


def compute_h(C, xt, wk, hT, psS, rstd, which, li, s, n):
    for _ in compute_h_g(C, xt, wk, hT, psS, rstd, which, li, s, n):
        pass


def rr(*gens):
    gens = [g for g in gens if g is not None]
    while gens:
        for g in list(gens):
            try:
                next(g)
            except StopIteration:
                gens.remove(g)


def pipeline_tiles(ntiles, load_fn, h_fn, main_fn):
    load_fn(0)
    for _ in h_fn(0):
        pass
    for i in range(ntiles):
        nxt = None
        if i + 1 < ntiles:
            load_fn(i + 1)
            nxt = h_fn(i + 1)
        rr(main_fn(i), nxt)


def load_w_bf16(C, dst, src_ap_fn, nk, src_tk):
    K = C.K
    for kc in range(nk):
        K.dma("pool", dst[:, kc, :], src_ap_fn(kc), [src_tk], dst)


A_CHUNKS = [(256 + 128 * m, 128) for m in range(3)] + [(640, 128), (768, 128), (896, 32)] + [(928 + 128 * m, 128) for m in range(9)]


def phase_A(C, li, cur):
    K, nc, I = C.K, C.nc, C.I
    with contextlib.ExitStack() as st:
        wA = K.sb("a_w", [128, 8, 2080], BF16, st)
        load_w_bf16(C, wA, lambda kc: I["w_in"][li, kc * 128:(kc + 1) * 128, 0:2080], 8, I["w_in"])
        xts = [K.sb(f"a_x{j}", [128, 8, 512], F32, st) for j in range(2)]
        wk = K.sb("a_wk", [128, 8, 512], F32, st)
        hTs = [K.sb(f"a_h{j}", [128, 8, 512], BF16, st) for j in range(2)]
        rstd = K.sb("a_rstd", [128, 512], F32, st)
        stg = [K.sb(f"a_stg{j}", [128, 15, 512], F32, st) for j in range(1)]
        ustg = K.sb("a_ustg", [128, 4, 256], BF16, st)
        psS = K.ps("a_psS", [128, 512], F32, st)
        psP = [K.ps(f"a_psP{j}", [128, 512], F32, st) for j in range(4)]
        psU = [K.ps(f"a_psU{j}", [128, 512], F32, st) for j in range(2)]
        tiles = tiles_of(512)
        cntp = {"npp": 0}

        def load_fn(i):
            t0, n, s = tiles[i]
            xt = xts[i % 2]
            K.dma("sp", xt[:, :, :n], C.xT[cur][:, t0:t0 + n].rearrange("(c p) t -> p c t", p=128), [C.xT[cur]], xt)

        def h_fn(i):
            t0, n, s = tiles[i]
            return compute_h_g(C, xts[i % 2], wk, hTs[i % 2], psS, rstd, 0, li, s, n)

        def main_fn(i):
            t0, n, s = tiles[i]
            hT = hTs[i % 2]
            sg = stg[0]
            for mi, (c0, M) in enumerate(A_CHUNKS):
                p = psP[cntp["npp"] % 4]
                cntp["npp"] += 1
                for kc in range(8):
                    K.ins("pe", [wA, hT], [p], nc.tensor.matmul, p[:M, :n], lhsT=wA[:, kc, c0:c0 + M], rhs=hT[:, kc, :n], start=(kc == 0), stop=(kc == 7))
                if mi % 2 == 0:
                    K.ins("dve", [p], [sg], nc.vector.tensor_copy, out=sg[:M, mi, :n], in_=p[:M, :n])
                else:
                    K.ins("act", [p], [sg], nc.scalar.copy, out=sg[:M, mi, :n], in_=p[:M, :n])
                yield
            K.dma("sp", C.zT[:, :, t0:t0 + n].rearrange("m p t -> p m t"), sg[:, :, :n], [sg], C.zT)
            for j in range(n // 128):
                p = psU[j % 2]
                for kc in range(8):
                    K.ins("pe", [wA, hT], [p], nc.tensor.matmul, p[:, 0:256], lhsT=hT[:, kc, j * 128:(j + 1) * 128], rhs=wA[:, kc, 0:256], start=(kc == 0), stop=(kc == 7))
                K.ins("act", [p], [ustg], nc.scalar.copy, out=ustg[:, j, :], in_=p[:, 0:256])
                yield
            K.dma("sp", C.u_d[t0:t0 + n, :].rearrange("(j p) c -> p j c", p=128), ustg[:, :n // 128, :], [ustg], C.u_d)

        pipeline_tiles(len(tiles), load_fn, h_fn, main_fn)


BR_K = [(0, 2), (2, 4), (6, 2)]


def phase_C1(C, li, cur):
    K, nc, I = C.K, C.nc, C.I
    last = li == 1
    with contextlib.ExitStack() as st:
        wG = K.sb("c1_wg", [128, 8, 3072], BF16, st)
        wB = K.sb("c1_wb", [128, 8, 1024], BF16, st)
        wO = K.sb("c1_wo", [128, 8, 1024], BF16, st)
        load_w_bf16(C, wG, lambda kc: I["w_in"][li, kc * 128:(kc + 1) * 128, 2080:5152], 8, I["w_in"])
        load_w_bf16(C, wB, lambda kc: I["w_br"][li, kc * 128:(kc + 1) * 128, :], 8, I["w_br"])
        load_w_bf16(C, wO, lambda kc: I["w_o"][li, kc * 128:(kc + 1) * 128, :], 8, I["w_o"])
        xts = [K.sb(f"c1_x{j}", [128, 8, 512], F32, st) for j in range(2)]
        wk = K.sb("c1_wk", [128, 8, 512], F32, st)
        hTs = [K.sb(f"c1_h{j}", [128, 8, 512], BF16, st) for j in range(2)]
        obs = [K.sb(f"c1_ob{j}", [128, 8, 512], BF16, st) for j in range(2)]
        mT = K.sb("c1_m", [128, 8, 512], BF16, st)
        rstd = K.sb("c1_rstd", [128, 512], F32, st)
        sgs = [K.sb(f"c1_sg{j}", [128, 512], BF16, st) for j in range(3)]
        tt_ = [K.sb(f"c1_t{j}", [128, 512], F32, st) for j in range(3)]
        psS = K.ps("c1_psS", [128, 512], F32, st)
        psG = [K.ps(f"c1_psG{j}", [128, 512], F32, st) for j in range(3)]
        psB = [K.ps(f"c1_psB{j}", [128, 512], F32, st) for j in range(3)]
        psO = K.ps("c1_psO", [128, 512], F32, st)
        tiles = tiles_of(512, last)

        def load_fn(i):
            t0, n, s = tiles[i]
            K.dma("sp", xts[i % 2][:, :, :n], C.xT[cur][:, t0:t0 + n].rearrange("(c p) t -> p c t", p=128), [C.xT[cur]], xts[i % 2])
            K.dma("sp", obs[i % 2][:, :, :n], C.obr[:, t0:t0 + n].rearrange("(c p) t -> p c t", p=128), [C.obr], obs[i % 2])

        def h_fn(i):
            t0, n, s = tiles[i]
            return compute_h_g(C, xts[i % 2], wk, hTs[i % 2], psS, rstd, 0, li, s, n)

        def main_fn(i):
            t0, n, s = tiles[i]
            xt, hT, ob = xts[i % 2], hTs[i % 2], obs[i % 2]
            for mo in range(8):
                for b in range(3):
                    pg, pb = psG[b], psB[b]
                    gc = (b * 8 + mo) * 128
                    for kc in range(8):
                        K.ins("pe", [wG, hT], [pg], nc.tensor.matmul, pg[:, :n], lhsT=wG[:, kc, gc:gc + 128], rhs=hT[:, kc, :n], start=(kc == 0), stop=(kc == 7))
                    k0, nk = BR_K[b]
                    for kk in range(nk):
                        K.ins("pe", [wB, ob], [pb], nc.tensor.matmul, pb[:, :n], lhsT=wB[:, k0 + kk, mo * 128:(mo + 1) * 128], rhs=ob[:, k0 + kk, :n], start=(kk == 0), stop=(kk == nk - 1))
                    K.ins("act", [pg], [sgs[b]], nc.scalar.activation, out=sgs[b][:, :n], in_=pg[:, :n], func=AF.Sigmoid)
                    K.ins("dve", [pb, sgs[b]], [tt_[b]], nc.vector.tensor_tensor, out=tt_[b][:, :n], in0=pb[:, :n], in1=sgs[b][:, :n], op=ALU.mult)
                    yield
                K.ins("pool", [tt_[0], tt_[1]], [tt_[0]], nc.gpsimd.tensor_tensor, out=tt_[0][:, :n], in0=tt_[0][:, :n], in1=tt_[1][:, :n], op=ALU.add)
                K.ins("pool", [tt_[0], tt_[2]], [mT], nc.gpsimd.tensor_tensor, out=mT[:, mo, :n], in0=tt_[0][:, :n], in1=tt_[2][:, :n], op=ALU.add)
            for mo in range(8):
                for kc in range(8):
                    K.ins("pe", [wO, mT], [psO], nc.tensor.matmul, psO[:, :n], lhsT=wO[:, kc, mo * 128:(mo + 1) * 128], rhs=mT[:, kc, :n], start=(kc == 0), stop=(kc == 7))
                K.ins("dve", [psO, C.mod, xt], [xt], nc.vector.scalar_tensor_tensor, out=xt[:, mo, :n], in0=psO[:, :n], scalar=C.mod[:, li, 16 + mo, s:s + 1], in1=xt[:, mo, :n], op0=ALU.mult, op1=ALU.add)
                yield
            K.dma("act", C.xT[cur ^ 1][:, t0:t0 + n].rearrange("(c p) t -> p c t", p=128), xt[:, :, :n], [xt], C.xT[cur ^ 1])

        pipeline_tiles(len(tiles), load_fn, h_fn, main_fn)


def phase_C2(C, li, cur):
    K, nc, I = C.K, C.nc, C.I
    last = li == 1
    N = 256
    with contextlib.ExitStack() as st:
        w1 = K.sb("c2_w1", [128, 8, 4096], BF16, st)
        w2 = K.sb("c2_w2", [128, 32, 1024], BF16, st)
        load_w_bf16(C, w1, lambda kc: I["mlp_w1"][li, kc * 128:(kc + 1) * 128, :], 8, I["mlp_w1"])
        load_w_bf16(C, w2, lambda kc: I["mlp_w2"][li, kc * 128:(kc + 1) * 128, :], 32, I["mlp_w2"])
        xts = [K.sb(f"c2_x{j}", [128, 8, N], F32, st) for j in range(2)]
        wk = K.sb("c2_wk", [128, 8, N], F32, st)
        hTs = [K.sb(f"c2_h{j}", [128, 8, N], BF16, st) for j in range(2)]
        uT = K.sb("c2_u", [128, 32, N], BF16, st)
        rstd = K.sb("c2_rstd", [128, N], F32, st)
        rrb = [K.sb(f"c2_r{j}", [128, N], BF16, st) for j in range(3)]
        ostg = [K.sb(f"c2_os{j}", [128, 1024], F32, st) for j in range(2)] if last else None
        psS = K.ps("c2_psS", [128, 512], F32, st)
        ps1 = [K.ps(f"c2_p1{j}", [128, 512], F32, st) for j in range(3)]
        ps2 = [K.ps(f"c2_p2{j}", [128, 512], F32, st) for j in range(2)]
        psT = [K.ps(f"c2_pT{j}", [128, 512], F32, st) for j in range(2)] if last else None
        tiles = tiles_of(N, last)
        cn = {"n1": 0, "n2": 0, "nt": 0}

        def load_fn(i):
            t0, n, s = tiles[i]
            K.dma("sp", xts[i % 2][:, :, :n], C.xT[cur][:, t0:t0 + n].rearrange("(c p) t -> p c t", p=128), [C.xT[cur]], xts[i % 2])

        def h_fn(i):
            t0, n, s = tiles[i]
            return compute_h_g(C, xts[i % 2], wk, hTs[i % 2], psS, rstd, 1, li, s, n)

        def main_fn(i):
            t0, n, s = tiles[i]
            xt, hT = xts[i % 2], hTs[i % 2]
            for f in range(32):
                p = ps1[cn["n1"] % 3]
                r = rrb[cn["n1"] % 3]
                cn["n1"] += 1
                for kc in range(8):
                    K.ins("pe", [w1, hT], [p], nc.tensor.matmul, p[:, :n], lhsT=w1[:, kc, f * 128:(f + 1) * 128], rhs=hT[:, kc, :n], start=(kc == 0), stop=(kc == 7))
                K.ins("act", [p], [r], nc.scalar.activation, out=r[:, :n], in_=p[:, :n], func=AF.Relu)
                K.ins("dve", [r], [uT], nc.vector.tensor_tensor, out=uT[:, f, :n], in0=r[:, :n], in1=r[:, :n], op=ALU.mult)
                if f % 2 == 1:
                    yield
            for mo in range(8):
                p = ps2[cn["n2"] % 2]
                cn["n2"] += 1
                for f in range(32):
                    K.ins("pe", [w2, uT], [p], nc.tensor.matmul, p[:, :n], lhsT=w2[:, f, mo * 128:(mo + 1) * 128], rhs=uT[:, f, :n], start=(f == 0), stop=(f == 31))
                K.ins("dve", [p, C.mod, xt], [xt], nc.vector.scalar_tensor_tensor, out=xt[:, mo, :n], in0=p[:, :n], scalar=C.mod[:, li, 40 + mo, s:s + 1], in1=xt[:, mo, :n], op0=ALU.mult, op1=ALU.add)
                yield
            if not last:
                K.dma("act", C.xT[cur ^ 1][:, t0:t0 + n].rearrange("(c p) t -> p c t", p=128), xt[:, :, :n], [xt], C.xT[cur ^ 1])
            else:
                for j in range(n // 128):
                    og = ostg[cn["nt"] % 2]
                    for half in range(2):
                        p = psT[half]
                        for c4 in range(4):
                            c = half * 4 + c4
                            K.ins("pe", [xt, C.ident], [p], nc.tensor.transpose, out=p[:, c4 * 128:(c4 + 1) * 128], in_=xt[:, c, j * 128:(j + 1) * 128], identity=C.ident[:])
                        if half == 0:
                            K.ins("dve", [p], [og], nc.vector.tensor_copy, out=og[:, 0:512], in_=p[:, :])
                        else:
                            K.ins("act", [p], [og], nc.scalar.copy, out=og[:, 512:1024], in_=p[:, :])
                    cn["nt"] += 1
                    r0 = t0 - LC + j * 128
                    K.dma("act", C.out[r0:r0 + 128, :], og[:, :], [og], C.out)
                    yield

        pipeline_tiles(len(tiles), load_fn, h_fn, main_fn)


def host_inputs(inputs, b):
    f = lambda a: np.ascontiguousarray(a, dtype=np.float32)
    x, c, ctx, c_ctx = inputs["x"], inputs["c"], inputs["ctx"], inputs["c_ctx"]
    m = {}
    m["xin"] = f(np.concatenate([ctx[b], x[b]], axis=0))
    cc = np.stack([c[b], c_ctx], axis=-1)
    m["ccT"] = f(cc.reshape(8, 128, 2).transpose(1, 0, 2).reshape(128, 16))
    m["w_ada"] = f(inputs["w_ada"])
    m["badaT"] = f(inputs["b_ada"].reshape(2, 48, 128).transpose(2, 0, 1).reshape(128, 96))
    ng = np.stack([inputs["norm1_g"], inputs["norm2_g"]], axis=0)
    m["ng"] = f(ng.reshape(2, 2, 8, 128).transpose(3, 0, 1, 2).reshape(128, 32))
    m["w_in"] = f(inputs["w_in"])
    m["w_br"] = f(np.concatenate([inputs["w_br_pool"], inputs["w_br_mla"], inputs["w_br_rwkv"]], axis=1))
    m["w_o"] = f(inputs["w_o"])
    m["mlp_w1"] = f(inputs["mlp_w1"])
    m["mlp_w2"] = f(inputs["mlp_w2"])
    m["ident"] = np.eye(128, dtype=np.float32)
    m.update(host_consts())
    pw = inputs["pool_w"]
    pwbd = np.zeros((2, 2, 128, 128), np.float32)
    for g in range(4):
        a, gi = divmod(g, 2)
        pwbd[:, a, gi * 64:(gi + 1) * 64, gi * 64:(gi + 1) * 64] = pw[:, g]
    m["pw_bd"] = pwbd
    m["pscale"] = f(inputs["pool_scale"].reshape(2, 2, 128).transpose(2, 0, 1))
    m["w_uq"] = f(inputs["mla_w_uq"])
    wkv = inputs["mla_w_ukv"].reshape(2, 256, 8, 128)
    m["w_ukv_k"] = f(wkv[:, :, :, :64].reshape(2, 256, 512))
    m["w_ukv_v"] = f(wkv[:, :, :, 64:].reshape(2, 256, 512))
    g96 = np.zeros((128, 2), np.float32); g96[:96] = inputs["qk_gain_q"].T; m["gq96"] = g96
    g96 = np.zeros((128, 2), np.float32); g96[:96] = inputs["qk_gain_k"].T; m["gk96"] = g96
    m["qng"] = f(inputs["mla_q_norm"].reshape(2, 3, 128).transpose(2, 0, 1))
    m["kvng"] = f(inputs["mla_kv_norm"].reshape(2, 2, 128).transpose(2, 0, 1))
    m["r_mu"] = f(inputs["rwkv_mu"].reshape(2, 2, 9, 128).transpose(3, 0, 2, 1))
    m["r_kk"] = f(inputs["rwkv_kk"].reshape(2, 2, 128).transpose(2, 0, 1))
    for nm in ("w0", "a0", "ka"):
        m["r_" + nm] = f(inputs["rwkv_" + nm].reshape(2, 2, 2, 128).transpose(3, 0, 1, 2))
    m["r_rk"] = f(inputs["rwkv_rk"].reshape(2, 2, 128).transpose(2, 0, 1))
    for nm in ("w2", "a2"):
        z_ = np.zeros((2, 2, 128, 256), np.float32)
        for d_ in range(2):
            z_[:, d_, d_ * 64:(d_ + 1) * 64, :] = inputs["rwkv_" + nm][:, d_]
        m["r_" + nm] = z_
    m["r_g2"] = f(inputs["rwkv_g2"])
    m["r_lnwb"] = f(np.broadcast_to(inputs["rwkv_ln_w"][:, None, :], (2, 128, 256)))
    m["r_lnbb"] = f(np.broadcast_to(inputs["rwkv_ln_b"][:, None, :], (2, 128, 256)))
    return m


_HC = {}


def host_consts():
    if _HC:
        return _HC
    m = _HC
    bands = np.zeros((128, 20, 128), np.float32)
    for g, win in enumerate((2, 4, 8, 16)):
        Ls = 384
        B = np.zeros((Ls, Ls), np.float64)
        for t in range(Ls):
            lo, hi = max(t - win // 2, 0), min(t + win // 2, Ls)
            B[lo:hi, t] = 1.0 / (hi - lo)
            B[t, t] -= 1.0
        bands[:, g * 5 + 0] = B[0:128, 0:128]
        bands[:, g * 5 + 1] = B[128:256, 128:256]
        bands[:, g * 5 + 2] = B[256:384, 256:384]
        bands[:, g * 5 + 3] = B[0:128, 128:256]
        bands[:, g * 5 + 4] = B[256:384, 128:256]
    m["bands"] = bands
    Cq = np.zeros((128, 128), np.float32); Cq[0:64, 0:64] = 1.0 / 64; Cq[64:96, 64:96] = 1.0 / 32
    m["Cq"] = Cq
    Pr = np.zeros((128, 128), np.float32)
    for i in range(16):
        Pr[64 + 16 + i, 64 + i] = 1.0
        Pr[64 + i, 64 + 16 + i] = 1.0
    m["Prot"] = Pr
    tpos = np.arange(4096)
    row = (tpos // 64).astype(np.float32); col = (tpos % 64).astype(np.float32)
    inv = np.power(np.float32(10000.0), -np.arange(8, dtype=np.float32) / np.float32(8)).astype(np.float32)
    ang = np.concatenate([row[:, None] * inv, col[:, None] * inv], axis=-1).astype(np.float32)
    cs = np.zeros((32, 2, 4096), np.float32)
    cs[0:16, 0] = np.cos(ang).T; cs[16:32, 0] = np.cos(ang).T
    cs[0:16, 1] = -np.sin(ang).T; cs[16:32, 1] = np.sin(ang).T
    m["cs"] = cs
    blk = np.zeros((128, 128), np.float32); blk[0:64, 0:64] = 1; blk[64:, 64:] = 1
    m["blk64"] = blk
    ind2 = np.zeros((128, 2), np.float32); ind2[0:64, 0] = 1; ind2[64:, 1] = 1
    m["ind2"] = ind2
    rr, cc_ = np.meshgrid(np.arange(128), np.arange(128), indexing="ij")
    masks = np.stack([rr < cc_, rr <= cc_, rr > cc_, rr >= cc_], axis=1).astype(np.float32)
    m["masks"] = np.ascontiguousarray(masks)
    lv = np.zeros((128, 14, 128), np.float32)
    ii = np.arange(128)
    for li_, mlev in enumerate((1, 2, 4, 8, 16, 32, 64)):
        same = (ii[:, None] // (2 * mlev)) == (ii[None, :] // (2 * mlev))
        MA = same & ((ii[:, None] % (2 * mlev)) >= mlev) & ((ii[None, :] % (2 * mlev)) < mlev)
        lv[:, li_, :] = MA
        lv[:, 7 + li_, :] = MA.T
    m["lvmask"] = lv
    rmask = np.ones((128, 512), np.float32); rmask[:, ::128] = 0
    m["rmask"] = rmask
    return m


def kernel(**inputs):
    nc = bass.Bass("TRN2", target_bir_lowering=False)
    with contextlib.ExitStack() as es:
        build(nc, es)
    in_maps = [host_inputs(inputs, b) for b in range(8)]
    res = run_bass_kernel_spmd(nc, in_maps, core_ids=list(range(8)))
    return np.stack([np.asarray(r["out"], dtype=np.float32) for r in res.results], axis=0)


_PE_SIG = {}


def mm(C, out_tk, out_ap, l_tk, l_ap, r_tk, r_ap, start=True, stop=True):
    sig = (l_ap.base_partition(), l_ap.partition_size())
    prev = _PE_SIG.get(id(out_tk))
    sw = None
    if prev is not None and prev[0] is out_tk and prev[1] != sig and prev[1][1] < 128 and sig[1] < 128:
        sw = prev[2]
    tok = C.K.ins("pe", [l_tk, r_tk], [out_tk], C.nc.tensor.matmul, out_ap, lhsT=l_ap, rhs=r_ap, start=start, stop=stop, _selfwait=sw)
    _PE_SIG[id(out_tk)] = (out_tk, sig, tok)


def cp(C, eng, out_tk, out_ap, in_tk, in_ap):
    if eng == "act":
        C.K.ins("act", [in_tk], [out_tk], C.nc.scalar.copy, out=out_ap, in_=in_ap)
    elif eng == "dve":
        C.K.ins("dve", [in_tk], [out_tk], C.nc.vector.tensor_copy, out=out_ap, in_=in_ap)
    else:
        C.K.ins("pool", [in_tk], [out_tk], C.nc.gpsimd.tensor_copy, out=out_ap, in_=in_ap)


def tt(C, eng, out_tk, out_ap, a_tk, a_ap, b_tk, b_ap, op):
    f = C.nc.vector.tensor_tensor if eng == "dve" else C.nc.gpsimd.tensor_tensor
    C.K.ins(eng, [a_tk, b_tk], [out_tk], f, out=out_ap, in0=a_ap, in1=b_ap, op=op)


def stt(C, out_tk, out_ap, a_tk, a_ap, sc_tk, sc, b_tk, b_ap, op0, op1):
    rd = [a_tk, b_tk] + ([sc_tk] if sc_tk is not None else [])
    C.K.ins("dve", rd, [out_tk], C.nc.vector.scalar_tensor_tensor, out=out_ap, in0=a_ap, scalar=sc, in1=b_ap, op0=op0, op1=op1)


def act(C, out_tk, out_ap, in_tk, in_ap, func, extra=(), **kw):
    C.K.ins("act", [in_tk] + list(extra), [out_tk], C.nc.scalar.activation, out=out_ap, in_=in_ap, func=func, **kw)


def seq_tiles(last_layer, n=512):
    return tiles_of(n, last_layer)


def phase_P(C, li):
    K, nc, I = C.K, C.nc, C.I
    with contextlib.ExitStack() as st:
        uall = K.sb("p_u", [128, 34, 256], BF16, st)
        bands = K.sb("p_bands", [128, 20, 128], BF16, st)
        pw = K.sb("p_pw", [128, 2, 128], BF16, st)
        psc = K.sb("p_sc", [128, 2], F32, st)
        pooled = K.sb("p_pooled", [128, 2, 512], BF16, st)
        ost = K.sb("p_ost", [128, 2, 512], BF16, st)
        psg = [K.ps(f"p_psg{j}", [128, 512], F32, st) for j in range(4)]
        psy = [K.ps(f"p_psy{j}", [128, 512], F32, st) for j in range(2)]
        K.dma("sp", uall[:], C.u_d[:, :].rearrange("(j p) c -> p j c", p=128), [C.u_d], uall)
        K.dma("pool", bands[:], I["bands"][:, :, :], [I["bands"]], bands)
        K.dma("pool", pw[:], I["pw_bd"][li].rearrange("a p d -> p a d"), [I["pw_bd"]], pw)
        K.dma("sp", psc[:], I["pscale"][:, li, :], [I["pscale"]], psc)
        for (t0, n, s) in tiles_of(512):
            j0 = t0 // 128
            first, lastj = (0, 1) if s else (2, 33)
            for a in range(2):
                for gi in range(2):
                    g = 2 * a + gi
                    pg = psg[g]
                    for jj in range(n // 128):
                        j = j0 + jj
                        srcs = []
                        if j > first:
                            srcs.append((j - 1, 3))
                        srcs.append((j, 0 if j == first else (2 if j == lastj else 1)))
                        if j < lastj:
                            srcs.append((j + 1, 4))
                        for si, (sj, kind) in enumerate(srcs):
                            mm(C, pg, pg[:, jj * 128:(jj + 1) * 128], uall, uall[:, sj, a * 128:(a + 1) * 128], bands, bands[:, g * 5 + kind, :], start=(si == 0), stop=(si == len(srcs) - 1))
                    cp(C, "dve" if gi == 0 else "act", pooled, pooled[gi * 64:(gi + 1) * 64, a, :n], pg, pg[gi * 64:(gi + 1) * 64, :n])
                py = psy[a]
                mm(C, py, py[:, :n], pw, pw[:, a, :], pooled, pooled[:, a, :n])
                K.ins("dve", [py, psc], [ost], nc.vector.tensor_scalar, out=ost[:, a, :n], in0=py[:, :n], scalar1=psc[:, a:a + 1], scalar2=None, op0=ALU.mult)
            K.dma("sp", C.obr[0:256, t0:t0 + n].rearrange("(a p) t -> p a t", p=128), ost[:, :, :n], [ost], C.obr)


def rms_feat(C, x, ncn, n, gain, out, psS, rstd, wk, dim):
    K, nc = C.K, C.nc
    act(C, wk, wk[:, :ncn, :n], x, x[:, :ncn, :n], AF.Square)
    for c in range(ncn):
        mm(C, psS, psS[:, :n], C.ones, C.ones[:], wk, wk[:, c, :n], start=(c == 0), stop=(c == ncn - 1))
    act(C, rstd, rstd[:, :n], psS, psS[:, :n], AF.Ln, extra=[C.epsb], scale=1.0 / dim, bias=C.epsb[:, 0:1])
    act(C, rstd, rstd[:, :n], rstd, rstd[:, :n], AF.Exp, scale=-0.5)
    for c in range(ncn):
        stt(C, out, out[:, c, :n], x, x[:, c, :n], gain, gain[:, c:c + 1], rstd, rstd[:, :n], ALU.mult, ALU.mult)


def run_pool(factories, slots, extra=(), extra_steps=1):
    free = list(slots)
    active = []
    extra = list(extra)
    pend = list(factories)
    while pend or active or extra:
        while pend and free:
            sl = free.pop(0)
            active.append((pend.pop(0)(sl), sl))
        for item in list(active):
            try:
                next(item[0])
            except StopIteration:
                active.remove(item)
                free.append(item[1])
        for g in list(extra):
            try:
                for _ in range(extra_steps):
                    next(g)
            except StopIteration:
                extra.remove(g)


def phase_MLA(C, li):
    K, nc, I = C.K, C.nc, C.I
    last = li == 1
    SC = 96.0 ** -0.5
    with contextlib.ExitStack() as st:
        KT = K.sb("l_KT", [128, 8, T_ALL], BF16, st)
        KTh = [Tk(f"l_KT_h{h}", KT.t) for h in range(8)]
        V = K.sb("l_V", [128, 34, 8, 65], BF16, st)
        wuq = K.sb("l_wuq", [128, 3, 768], BF16, st)
        wk_ = K.sb("l_wukvk", [128, 2, 512], BF16, st)
        wv_ = K.sb("l_wukvv", [128, 2, 512], BF16, st)
        Cq = K.sb("l_Cq", [128, 128], F32, st)
        Prot = K.sb("l_Prot", [128, 128], BF16, st)
        gq = K.sb("l_gq", [128, 1], F32, st)
        gk = K.sb("l_gk", [128, 1], F32, st)
        qng = K.sb("l_qng", [128, 3], F32, st)
        kvng = K.sb("l_kvng", [128, 2], F32, st)
        load_w_bf16(C, wuq, lambda kc: I["w_uq"][li, kc * 128:(kc + 1) * 128, :], 3, I["w_uq"])
        load_w_bf16(C, wk_, lambda kc: I["w_ukv_k"][li, kc * 128:(kc + 1) * 128, :], 2, I["w_ukv_k"])
        load_w_bf16(C, wv_, lambda kc: I["w_ukv_v"][li, kc * 128:(kc + 1) * 128, :], 2, I["w_ukv_v"])
        K.dma("sp", Cq[:], I["Cq"][:, :], [I["Cq"]], Cq)
        K.dma("pool", Prot[:], I["Prot"][:, :], [I["Prot"]], Prot)
        gqf = K.sb("l_gqf", [128, 2], F32, st)
        gkf = K.sb("l_gkf", [128, 2], F32, st)
        K.dma("sp", gqf[:], I["gq96"][:, :], [I["gq96"]], gqf)
        K.dma("sp", gkf[:], I["gk96"][:, :], [I["gk96"]], gkf)
        cp(C, "dve", gq, gq[:], gqf, gqf[:, li:li + 1])
        cp(C, "dve", gk, gk[:], gkf, gkf[:, li:li + 1])
        K.dma("sp", qng[:], I["qng"][:, li, :], [I["qng"]], qng)
        K.dma("sp", kvng[:], I["kvng"][:, li, :], [I["kvng"]], kvng)
        xin = K.sb("l_xin", [128, 3, 512], F32, st)
        xkr = K.sb("l_xkr", [128, 512], F32, st)
        xn = K.sb("l_xn", [128, 3, 512], BF16, st)
        wk = K.sb("l_wk", [128, 3, 512], F32, st)
        rstd = K.sb("l_rstd", [128, 512], F32, st)
        cs = K.sb("l_cs", [128, 2, 512], F32, st)
        QTs = [K.sb(f"l_QT{j}", [128, 8, 512], BF16, st) for j in range(2)]
        QTh = [[Tk(f"l_QT{j}_h{h}", QTs[j].t) for h in range(8)] for j in range(2)]
        PT = [K.sb(f"l_PT{j}", [128, 512], BF16, st) for j in range(3)]
        rsb = K.sb("l_rsb", [128, 512], F32, st)
        bcs = K.sb("l_bcs", [128, 512], F32, st)
        ost = K.sb("l_ost", [128, 8, 512], BF16, st)
        psS = K.ps("l_psS", [128, 512], F32, st)
        psSc = [K.ps(f"l_psSc{j}", [128, 512], F32, st) for j in range(2)]
        psO = [K.ps(f"l_psO{j}", [128, 512], F32, st) for j in range(2)]
        psF = K.ps("l_psF", [128, 512], F32, st)
        slots = []
        for j in range(2):
            slots.append(dict(
                raw=K.sb(f"l_raw{j}", [128, 512], F32, st), sq=K.sb(f"l_sq{j}", [128, 512], F32, st),
                r2=K.sb(f"l_r2{j}", [128, 512], F32, st), qn=K.sb(f"l_qn{j}", [128, 512], F32, st),
                qnb=K.sb(f"l_qnb{j}", [128, 512], BF16, st), tmp=K.sb(f"l_tmp{j}", [128, 512], F32, st),
                ps=K.ps(f"l_psH{j}", [128, 512], F32, st)))
        K.ins("dve", [], [V], nc.vector.memset, V[:, :, :, 64:65], 1.0)
        R96 = slice(64, 96)

        def head_norm_g(S_, rows, src_tk, src_ap, gain, n, lhs_ap):
            sq, r2, qn, ps = S_["sq"], S_["r2"], S_["qn"], S_["ps"]
            tt(C, "pool", sq, sq[rows, :n], src_tk, src_ap, src_tk, src_ap, ALU.mult)
            yield
            mm(C, ps, ps[0:96, :n], Cq, lhs_ap, sq, sq[rows, :n])
            yield
            act(C, r2, r2[rows, :n], ps, ps[rows, :n], AF.Ln, extra=[C.epsb], scale=1.0, bias=C.epsb[rows, 0:1])
            yield
            act(C, r2, r2[rows, :n], r2, r2[rows, :n], AF.Exp, scale=-0.5)
            yield
            stt(C, qn, qn[rows, :n], src_tk, src_ap, gain, gain[rows, 0:1], r2, r2[rows, :n], ALU.mult, ALU.mult)
            yield

        def rope_g(S_, rows, n, out_tk, out_ap):
            qn, qnb, tmp, ps = S_["qn"], S_["qnb"], S_["tmp"], S_["ps"]
            cp(C, "dve", qnb, qnb[rows, :n], qn, qn[rows, :n])
            yield
            mm(C, ps, ps[0:96, :n], Prot, Prot[rows, 0:96], qnb, qnb[rows, :n])
            yield
            tt(C, "dve", tmp, tmp[rows, :n], ps, ps[rows, :n], cs, cs[rows, 1, :n], ALU.mult)
            tt(C, "pool", qn, qn[rows, :n], qn, qn[rows, :n], cs, cs[rows, 0, :n], ALU.mult)
            yield
            tt(C, "dve", out_tk, out_ap, qn, qn[rows, :n], tmp, tmp[rows, :n], ALU.add)
            yield

        def k_head_f(h, t0, n):
            def g(S_):
                ps, raw, qn = S_["ps"], S_["raw"], S_["qn"]
                for kc in range(2):
                    mm(C, ps, ps[0:64, :n], wk_, wk_[:, kc, h * 64:(h + 1) * 64], xn, xn[:, kc, :n], start=(kc == 0), stop=(kc == 1))
                yield
                cp(C, "dve", raw, raw[0:64, :n], ps, ps[0:64, :n])
                yield
                yield from head_norm_g(S_, slice(0, 64), raw, raw[0:64, :n], gk, n, Cq[0:64, 0:96])
                cp(C, "pool", KTh[h], KT[0:64, h, t0:t0 + n], qn, qn[0:64, :n])
                yield
            return g

        def k_kr_f(t0, n, s):
            def g(S_):
                qn = S_["qn"]
                yield from head_norm_g(S_, R96, xkr, xkr[R96, :n], gk, n, Cq[64:96, 0:96])
                if not s:
                    yield from rope_g(S_, R96, n, qn, qn[R96, :n])
                K.ins("dve", [qn], KTh, nc.vector.tensor_copy, out=KT[64:96, :, t0:t0 + n], in_=qn[64:96, :n].unsqueeze(1).broadcast_to([32, 8, n]))
                yield
            return g

        for (t0, n, s) in tiles_of(512):
            K.dma("sp", xin[:, 0:2, :n], C.zT[3:5, :, t0:t0 + n].rearrange("m p t -> p m t"), [C.zT], xin)
            K.dma("sp", xkr[64:96, :n], C.zT[5, 0:32, t0:t0 + n], [C.zT], xkr)
            if not s:
                K.dma("sp", cs[64:96, :, :n], I["cs"][:, :, t0 - LC:t0 - LC + n], [I["cs"]], cs)
            rms_feat(C, xin, 2, n, kvng, xn, psS, rstd, wk, 256.0)
            for jj in range(n // 128):
                pa = psSc[jj % 2]
                for kc in range(2):
                    mm(C, pa, pa[:, :], xn, xn[:, kc, jj * 128:(jj + 1) * 128], wv_, wv_[:, kc, :], start=(kc == 0), stop=(kc == 1))
                cp(C, "act", V, V[:, t0 // 128 + jj, :, 0:64], pa, pa[:, :].rearrange("p (h d) -> p h d", d=64))
            run_pool([k_kr_f(t0, n, s)] + [k_head_f(h, t0, n) for h in range(8)], slots)

        def q_head_f(h, n, s, qb):
            def g(S_):
                ps, raw, qn = S_["ps"], S_["raw"], S_["qn"]
                QTt, QT = QTh[qb][h], QTs[qb]
                for kc in range(3):
                    mm(C, ps, ps[0:96, :n], wuq, wuq[:, kc, h * 96:(h + 1) * 96], xn, xn[:, kc, :n], start=(kc == 0), stop=(kc == 2))
                yield
                cp(C, "dve", raw, raw[0:96, :n], ps, ps[0:96, :n])
                yield
                yield from head_norm_g(S_, slice(0, 96), raw, raw[0:96, :n], gq, n, Cq[0:96, 0:96])
                cp(C, "pool", QTt, QT[0:64, h, :n], qn, qn[0:64, :n])
                if not s:
                    yield from rope_g(S_, R96, n, QTt, QT[R96, h, :n])
                else:
                    cp(C, "pool", QTt, QT[R96, h, :n], qn, qn[R96, :n])
                yield
            return g

        def q_prep(t0, n, s, qb):
            K.dma("sp", xin[:, 0:3, :n], C.zT[0:3, :, t0:t0 + n].rearrange("m p t -> p m t"), [C.zT], xin)
            if not s:
                K.dma("sp", cs[64:96, :, :n], I["cs"][:, :, t0 - LC:t0 - LC + n], [I["cs"]], cs)
            rms_feat(C, xin, 3, n, qng, xn, psS, rstd, wk, 384.0)
            return [q_head_f(h, n, s, qb) for h in range(8)]

        cnt = {"npt": 0}

        def attn_g(t0, n, s, qb):
            QT = QTs[qb]
            kts = [0, 1] if s else list(range(34))
            steps = [(h, ki, kt) for h in range(8) for ki, kt in enumerate(kts)]
            slot = {}

            def emit_S(i):
                h, ki, kt = steps[i]
                psc_ = psSc[cnt["npt"] % 2]
                pt = PT[cnt["npt"] % 3]
                cnt["npt"] += 1
                slot[i] = pt
                mm(C, psc_, psc_[:, :n], KTh[h], KT[0:96, h, kt * 128:(kt + 1) * 128], QTh[qb][h], QT[0:96, h, :n])
                act(C, pt, pt[:, :n], psc_, psc_[:, :n], AF.Exp, scale=SC)

            emit_S(0)
            for i, (h, ki, kt) in enumerate(steps):
                if i + 1 < len(steps):
                    emit_S(i + 1)
                po = psO[h % 2]
                pt = slot.pop(i)
                mm(C, po, po[0:65, :n], V, V[:, kt, h, :], pt, pt[:, :n], start=(ki == 0), stop=(ki == len(kts) - 1))
                if ki == len(kts) - 1:
                    K.ins("dve", [po], [rsb], nc.vector.reciprocal, out=rsb[64:65, :n], in_=po[64:65, :n])
                    mm(C, psF, psF[0:64, :n], C.ones, C.ones[64:65, 0:64], rsb, rsb[64:65, :n])
                    cp(C, "dve", bcs, bcs[0:64, :n], psF, psF[0:64, :n])
                    tt(C, "dve", ost, ost[0:64, h, :n], po, po[0:64, :n], bcs, bcs[0:64, :n], ALU.mult)
                yield
            K.dma("sp", C.obr[256:768, t0:t0 + n].rearrange("(h p) t -> p h t", p=64), ost[0:64, :, :n], [ost], C.obr)

        qtiles = tiles_of(512, last)
        run_pool(q_prep(*qtiles[0], 0), slots)
        for i, (t0, n, s) in enumerate(qtiles):
            nxt = q_prep(*qtiles[i + 1], (i + 1) % 2) if i + 1 < len(qtiles) else []
            run_pool(nxt, slots, extra=[attn_g(t0, n, s, i % 2)], extra_steps=4)


def phase_R(C, li):
    K, nc, I = C.K, C.nc, C.I
    last = li == 1
    NCH = 34
    TN = 256
    MAXFLY = min(getattr(C, "maxfly", 6), 6)
    with contextlib.ExitStack() as st:
        C.dbgtk = getattr(C, "dbgtk", {})

        def sbt(name, shape, dt=F32):
            tk = K.sb(f"rw{li}_" + name, shape, dt, st)
            C.dbgtk[f"rw{li}_" + name] = tk
            return tk
        mu = sbt("mu", [128, 9, 2]); omm = sbt("omm", [128, 9])
        kkp = sbt("kkp", [128, 2]); w0 = sbt("w0", [128, 2, 2]); a0 = sbt("a0", [128, 2, 2]); ka = sbt("ka", [128, 2, 2]); omka = sbt("omka", [128, 2, 2])
        rk = sbt("rk", [128, 2]); w2 = sbt("w2", [128, 2, 256], BF16); a2 = sbt("a2", [128, 2, 256], BF16); zb7 = sbt("zb7", [128, TN], BF16)
        g2 = sbt("g2", [128, 256], BF16); lnw = sbt("lnw", [128, 256]); lnb = sbt("lnb", [128, 256])
        blk = sbt("blk", [128, 128]); ind2 = sbt("ind2", [128, 2]); masks = sbt("masks", [128, 4, 128]); rmask = sbt("rmask", [128, 512])
        lvm = sbt("lvm", [128, 14, 128], BF16)
        lvm2 = sbt("lvm2", [128, 14, 128], BF16)
        K.dma("sp", mu[:], I["r_mu"][:, li, :, :], [I["r_mu"]], mu)
        K.dma("sp", kkp[:], I["r_kk"][:, li, :], [I["r_kk"]], kkp)
        K.dma("sp", w0[:], I["r_w0"][:, li, :, :], [I["r_w0"]], w0)
        K.dma("sp", a0[:], I["r_a0"][:, li, :, :], [I["r_a0"]], a0)
        K.dma("sp", ka[:], I["r_ka"][:, li, :, :], [I["r_ka"]], ka)
        K.dma("sp", rk[:], I["r_rk"][:, li, :], [I["r_rk"]], rk)
        for d_ in range(2):
            K.dma("pool", w2[:, d_, :], I["r_w2"][li, d_], [I["r_w2"]], w2)
            K.dma("pool", a2[:, d_, :], I["r_a2"][li, d_], [I["r_a2"]], a2)
        K.dma("pool", g2[:], I["r_g2"][li], [I["r_g2"]], g2)
        K.dma("pool", lvm[:], I["lvmask"][:, :, :], [I["lvmask"]], lvm)
        K.dma("pool", lvm2[:, 0:7, :], I["lvmask"][:, 7:14, :], [I["lvmask"]], lvm2)
        K.dma("pool", lvm2[:, 7:14, :], I["lvmask"][:, 0:7, :], [I["lvmask"]], lvm2)
        K.dma("sp", lnw[:], I["r_lnwb"][li], [I["r_lnwb"]], lnw)
        K.dma("sp", lnb[:], I["r_lnbb"][li], [I["r_lnbb"]], lnb)
        K.dma("sp", blk[:], I["blk64"][:, :], [I["blk64"]], blk)
        K.dma("sp", ind2[:], I["ind2"][:, :], [I["ind2"]], ind2)
        K.dma("sp", masks[:], I["masks"][:, :, :], [I["masks"]], masks)
        K.dma("sp", rmask[:], I["rmask"][:, :], [I["rmask"]], rmask)
        tt(C, "dve", omm, omm[:], mu, mu[:, :, 0], mu, mu[:, :, 1], ALU.add)
        K.ins("dve", [omm], [omm], nc.vector.tensor_scalar, out=omm[:], in0=omm[:], scalar1=-1.0, scalar2=1.0, op0=ALU.mult, op1=ALU.add)
        K.ins("dve", [ka], [omka], nc.vector.tensor_scalar, out=omka[:], in0=ka[:], scalar1=-1.0, scalar2=1.0, op0=ALU.mult, op1=ALU.add)
        Yacc = sbt("Yacc", [128, NCH, 256]); Vtok = sbt("Vtok", [128, NCH, 256], BF16)
        VtokT = [Tk(f"Vtok_t{j}", Vtok.t) for j in range(NCH // 2)]
        bon = sbt("bon", [128, NCH, 4]); gst = sbt("gst", [128, 2, 256], BF16)
        S_ = [sbt(f"S{hp}", [128, 64]) for hp in range(2)]
        Sb_ = [sbt(f"Sb{hp}", [128, 64], BF16) for hp in range(2)]
        zr = sbt("zr", [128, 9, TN + 2]); zs = sbt("zs", [128, 9, TN])
        kkn = sbt("kkn", [128, 2, TN]); e1 = sbt("e1", [128, 2, TN]); e2 = sbt("e2", [128, 2, TN])
        xw = sbt("xw", [128, 2, TN]); aa = sbt("aa", [128, 2, TN]); kd = [sbt(f"kd{d}", [128, 2, TN]) for d in range(2)]
        bb = sbt("bb", [128, 2, TN]); cum = sbt("cum", [128, 2, TN]); tot = sbt("tot", [128, 2, 2])
        th = sbt("th", [128, TN], BF16); sgd = sbt("sgd", [128, TN], BF16)
        opsets = []
        for j in range(3):
            o = {nm: sbt(f"o{j}_" + nm, [128, 2, TN], BF16) for nm in ("at", "rt", "bt", "kt", "bh", "kh")}
            o["WC"] = sbt(f"o{j}_WC", [128, 2, 2])
            opsets.append(o)
        banks = [K.ps(f"r_bank{j}", [128, 512], F32, st) for j in range(8)]
        bsets = []
        for j in range(MAXFLY):
            b = dict(
                AT=sbt(f"u{j}_AT", [128, 2, 512], BF16), Lc=[sbt(f"u{j}_Lc{q}", [128, 2, 2, 128], BF16) for q in range(2)],
                QT=[sbt(f"u{j}_QT{q}", [128, 2, 2, 128], BF16) for q in range(2)], ZZ=sbt(f"u{j}_ZZ", [128, 2, 2, 128], BF16),
                P=sbt(f"u{j}_P", [128, 2, 128], BF16), tok3=sbt(f"u{j}_tok3", [128, 3, 128], BF16), rGH=sbt(f"u{j}_rGH", [128, 2, 128], BF16),
                NX=sbt(f"u{j}_NX", [128, 512], BF16), GHs=sbt(f"u{j}_GHs", [128, 2, 128], BF16), GT=sbt(f"u{j}_GT", [128, 128], BF16), Us=sbt(f"u{j}_Us", [128, 2, 64], BF16),
                ps=(banks[2 + j], banks[2 + j]))
            bsets.append(b)
        psW = [banks[0], banks[1]]
        psQ = [banks[0], banks[1]]
        ytmp = sbt("ytmp", [128, 256]); otok = sbt("otok", [128, 256]); st4 = sbt("st4", [128, 4]); st4b = sbt("st4b", [128, 4])
        obst = sbt("obst", [128, 2, 128], BF16); gl = sbt("gl", [128, 256], BF16)

        def prep(t0, n, s, d, first_pass, oset):
            s0, s1 = (0, LC) if s else (LC, T_ALL)
            lo, hi = max(t0 - 1, s0), min(t0 + n + 1, s1)
            if lo == t0 or hi == t0 + n:
                K.ins("pool", [], [zr], nc.gpsimd.memset, zr[:, :, :], 0.0)
            off = lo - (t0 - 1)
            K.dma("sp", zr[:, :, off:off + hi - lo], C.zT[6:15, :, lo:hi].rearrange("m p t -> p m t"), [C.zT], zr)
            yield
            for c in range(9):
                K.ins("dve", [zr, omm], [zs], nc.vector.tensor_scalar, out=zs[:, c, :n], in0=zr[:, c, 1:n + 1], scalar1=omm[:, c:c + 1], scalar2=None, op0=ALU.mult)
                stt(C, zs, zs[:, c, :n], zr, zr[:, c, 0:n], mu, mu[:, c, 0:1], zs, zs[:, c, :n], ALU.mult, ALU.add)
                stt(C, zs, zs[:, c, :n], zr, zr[:, c, 2:n + 2], mu, mu[:, c, 1:2], zs, zs[:, c, :n], ALU.mult, ALU.add)
                if c % 3 == 2:
                    yield
            r_, k_ = zs[:, 0:2, :n], zs[:, 2:4, :n]
            for hp in range(2):
                K.ins("dve", [zs, kkp], [kkn], nc.vector.tensor_scalar, out=kkn[:, hp, :n], in0=zs[:, 2 + hp, :n], scalar1=kkp[:, hp:hp + 1], scalar2=None, op0=ALU.mult)
            tt(C, "pool", e1, e1[:, :, :n], kkn, kkn[:, :, :n], kkn, kkn[:, :, :n], ALU.mult)
            yield
            for hp in range(2):
                pw_ = psW[hp]
                mm(C, pw_, pw_[:, :n], blk, blk[:], e1, e1[:, hp, :n])
                K.ins("dve", [pw_], [e2], nc.vector.tensor_scalar, out=e2[:, hp, :n], in0=pw_[:, :n], scalar1=1e-24, scalar2=None, op0=ALU.max)
            act(C, e2, e2[:, :, :n], e2, e2[:, :, :n], AF.Ln)
            yield
            act(C, e2, e2[:, :, :n], e2, e2[:, :, :n], AF.Exp, scale=-0.5)
            tt(C, "dve", kkn, kkn[:, :, :n], kkn, kkn[:, :, :n], e2, e2[:, :, :n], ALU.mult)
            yield
            dirs = (0, 1) if first_pass else (d,)
            cp(C, "act", zb7, zb7[:, :n], zs, zs[:, 7, :n])
            yield
            for dd in dirs:
                for hp in range(2):
                    pw_ = psW[hp]
                    mm(C, pw_, pw_[:, :n], a2, a2[:, dd, hp * 128:(hp + 1) * 128], zb7, zb7[:, :n])
                    K.ins("dve", [pw_, a0], [aa], nc.vector.tensor_scalar, out=aa[:, hp, :n], in0=pw_[:, :n], scalar1=a0[:, dd, hp:hp + 1], scalar2=None, op0=ALU.add)
                    act(C, aa, aa[:, hp, :n], aa, aa[:, hp, :n], AF.Sigmoid)
                    K.ins("dve", [aa, ka], [e1], nc.vector.tensor_scalar, out=e1[:, hp, :n], in0=aa[:, hp, :n], scalar1=ka[:, dd, hp:hp + 1], scalar2=None, op0=ALU.mult)
                    K.ins("dve", [e1, omka], [e1], nc.vector.tensor_scalar, out=e1[:, hp, :n], in0=e1[:, hp, :n], scalar1=omka[:, dd, hp:hp + 1], scalar2=None, op0=ALU.add)
                    yield
                tt(C, "dve", kd[dd], kd[dd][:, :, :n], e1, e1[:, :, :n], zs, k_, ALU.mult)
                if dd == d:
                    tt(C, "pool", bb, bb[:, :, :n], kkn, kkn[:, :, :n], aa, aa[:, :, :n], ALU.mult)
            if first_pass:
                tidx = t0 // TN
                tt(C, "pool", e1, e1[:, :, :n], kd[0], kd[0][:, :, :n], kd[1], kd[1][:, :, :n], ALU.add)
                tt(C, "dve", e1, e1[:, :, :n], e1, e1[:, :, :n], zs, r_, ALU.mult)
                for hp in range(2):
                    K.ins("dve", [e1, rk], [e1], nc.vector.tensor_scalar, out=e1[:, hp, :n], in0=e1[:, hp, :n], scalar1=rk[:, hp:hp + 1], scalar2=0.5, op0=ALU.mult, op1=ALU.mult)
                act(C, sgd, sgd[:, :n], zs, zs[:, 8, :n], AF.Sigmoid)
                yield
                for jj in range(n // 128):
                    cidx = t0 // 128 + jj
                    pq = psQ[jj % 2]
                    for hp in range(2):
                        mm(C, pq, pq[:, 256 + 2 * hp:258 + 2 * hp], e1, e1[:, hp, jj * 128:(jj + 1) * 128], ind2, ind2[:, :])
                    mm(C, pq, pq[:, 0:256], sgd, sgd[:, jj * 128:(jj + 1) * 128], g2, g2[:, :])
                    cp(C, "act", gst, gst[:, jj, :], pq, pq[:, 0:256])
                    cp(C, "dve", bon, bon[:, cidx, :], pq, pq[:, 256:260])
                    pv = psW[jj % 2]
                    for hp in range(2):
                        K.ins("pe", [zs, C.ident], [pv], nc.tensor.transpose, out=pv[:, hp * 128:(hp + 1) * 128], in_=zs[:, 4 + hp, jj * 128:(jj + 1) * 128], identity=C.ident[:])
                    cp(C, "act", VtokT[tidx], Vtok[:, cidx, :], pv, pv[:, 0:256])
                    yield
                K.dma("sp", C.gtok_d[t0:t0 + n, :].rearrange("(j p) c -> p j c", p=128), gst[:, :n // 128, :], [gst], C.gtok_d)
            act(C, th, th[:, :n], zs, zs[:, 6, :n], AF.Tanh)
            yield
            for hp in range(2):
                pw_ = psW[hp]
                mm(C, pw_, pw_[:, :n], w2, w2[:, d, hp * 128:(hp + 1) * 128], th, th[:, :n])
                K.ins("dve", [pw_, w0], [xw], nc.vector.tensor_scalar, out=xw[:, hp, :n], in0=pw_[:, :n], scalar1=w0[:, d, hp:hp + 1], scalar2=None, op0=ALU.add)
                act(C, xw, xw[:, hp, :n], xw, xw[:, hp, :n], AF.Sigmoid)
                yield
            K.ins("dve", [xw], [xw], nc.vector.tensor_scalar, out=xw[:, :, :n], in0=xw[:, :, :n], scalar1=-0.6065306597126334, scalar2=None, op0=ALU.mult)
            yield
            nchk = n // 128
            for hp in range(2):
                K.ins("dve", [rmask, xw], [cum], nc.vector.tensor_tensor_scan, out=cum[:, hp, :n], data0=rmask[:, :n], data1=xw[:, hp, :n], initial=0.0, op0=ALU.mult, op1=ALU.add)
            cum4 = cum[:, :, :n].rearrange("p h (c t) -> p h c t", t=128)
            cp(C, "dve", tot, tot[:, :, :nchk], cum, cum4[:, :, :, 127])
            yield
            totb = tot[:, :, :nchk].unsqueeze(3).broadcast_to([128, 2, nchk, 128])
            v4 = lambda tk: tk[:, :, :n].rearrange("p h (c t) -> p h c t", t=128)
            if d == 0:
                tt(C, "pool", e2, e2[:, :, :n], cum, cum[:, :, :n], xw, xw[:, :, :n], ALU.subtract)
                tt(C, "dve", e1, v4(e1), tot, totb, cum, cum4, ALU.subtract)
            else:
                tt(C, "dve", e2, v4(e2), tot, totb, cum, cum4, ALU.subtract)
                tt(C, "pool", e1, e1[:, :, :n], cum, cum[:, :, :n], xw, xw[:, :, :n], ALU.subtract)
                tt(C, "dve", cum, cum[:, :, :n], e2, e2[:, :, :n], xw, xw[:, :, :n], ALU.add)
            ci = cum[:, :, :n]
            WC = oset["WC"]
            act(C, WC, WC[:, :, :nchk], tot, tot[:, :, :nchk], AF.Exp)
            yield
            act(C, e2, e2[:, :, :n], e2, e2[:, :, :n], AF.Exp)
            stt(C, oset["at"], oset["at"][:, :, :n], kkn, kkn[:, :, :n], None, -1.0, e2, e2[:, :, :n], ALU.mult, ALU.mult)
            yield
            act(C, e2, e2[:, :, :n], cum, ci, AF.Exp)
            tt(C, "dve", oset["rt"], oset["rt"][:, :, :n], zs, r_, e2, e2[:, :, :n], ALU.mult)
            yield
            act(C, e2, e2[:, :, :n], cum, ci, AF.Exp, scale=-1.0)
            tt(C, "dve", oset["bt"], oset["bt"][:, :, :n], bb, bb[:, :, :n], e2, e2[:, :, :n], ALU.mult)
            tt(C, "pool", oset["kt"], oset["kt"][:, :, :n], kd[d], kd[d][:, :, :n], e2, e2[:, :, :n], ALU.mult)
            yield
            act(C, e1, e1[:, :, :n], e1, e1[:, :, :n], AF.Exp)
            tt(C, "dve", oset["bh"], oset["bh"][:, :, :n], bb, bb[:, :, :n], e1, e1[:, :, :n], ALU.mult)
            tt(C, "pool", oset["kh"], oset["kh"][:, :, :n], kd[d], kd[d][:, :, :n], e1, e1[:, :, :n], ALU.mult)

        done = {0: 0, 1: 0}

        free_sets = list(range(MAXFLY))

        def unit(cidx, jj, d, hp, emit_y, first_pass, oset, B, myn):
            if getattr(C, "rdbg", 9) < 2:
                done[hp] += 1
                return
            bidx = free_sets.pop(0)
            B = bsets[bidx]
            T_ = slice(jj * 128, (jj + 1) * 128)
            mS = 0 if d == 0 else 2
            at, rt, bt, kt, bh, kh, WC = (oset[nm] for nm in ("at", "rt", "bt", "kt", "bh", "kh", "WC"))
            AT, Lcs, QTs, ZZ, P, tok3, rGH, GHs, GT, Us = (B[nm] for nm in ("AT", "Lc", "QT", "ZZ", "P", "tok3", "rGH", "GHs", "GT", "Us"))
            pA, pB = B["ps"]
            if getattr(C, "onebank", False):
                pB = pA
            S, Sb = S_[hp], Sb_[hp]
            Vt = VtokT[cidx // 2]
            vsl = lambda hh: Vtok[:, cidx, hp * 128 + hh * 64: hp * 128 + hh * 64 + 64]
            mview = masks[:, mS:mS + 2, :].unsqueeze(1).broadcast_to([128, 2, 2, 128])
            for hh in range(2):
                R = slice(hh * 64, hh * 64 + 64)
                pa = pA if hh == 0 else pB
                mm(C, pa, pa[:, 0:128], bt, bt[R, hp, T_], at, at[R, hp, T_])
                mm(C, pa, pa[:, 128:256], bt, bt[R, hp, T_], rt, rt[R, hp, T_])
                mm(C, pa, pa[:, 256:384], kt, kt[R, hp, T_], at, at[R, hp, T_])
                mm(C, pa, pa[:, 384:512], kt, kt[R, hp, T_], rt, rt[R, hp, T_])
                tt(C, "dve", AT, AT[:, hh, :].rearrange("p (a m t) -> p a m t", a=2, m=2), pa, pa[:, :].rearrange("p (a m t) -> p a m t", a=2, m=2), masks, mview, ALU.mult)
            yield
            pn = pA
            for hh in range(2):
                R = slice(hh * 64, hh * 64 + 64)
                mm(C, pn, pn[:, hh * 128:(hh + 1) * 128], at, at[R, hp, T_], bt, bt[R, hp, T_])
                mm(C, pn, pn[:, 256 + hh * 128:384 + hh * 128], bt, bt[R, hp, T_], at, at[R, hp, T_])
            NX = B["NX"]
            cp(C, "act", NX, NX[:, :], pn, pn[:, :])
            yield
            lvD = lvm if d == 0 else lvm2
            nx4 = NX[:, :].rearrange("p (a h t) -> p a h t", a=2, h=2)
            lvl_no = [0]

            def make_L(lvl):
                Lc = Lcs[lvl % 2]
                mk = lvD[:, :, :].rearrange("p (a l) t -> p a l t", a=2)[:, :, lvl, :].unsqueeze(2).broadcast_to([128, 2, 2, 128])
                tt(C, "dve" if lvl % 2 == 0 else "pool", Lc, Lc[:, :, :, :], NX, nx4, lvD, mk, ALU.mult)
                return Lc

            idb = C.identb[:].unsqueeze(1).broadcast_to([128, 2, 128])
            Lc = make_L(0)
            QT = QTs[0]
            tt(C, "pool", QT, QT[:, :, 0, :], Lc, Lc[:, 1, :, :], C.identb, idb, ALU.add)
            tt(C, "pool", QT, QT[:, :, 1, :], Lc, Lc[:, 0, :, :], C.identb, idb, ALU.add)
            Lc = make_L(1)
            yield
            curq = 0
            pz, pq_ = pB, pA
            for lvl in range(1, 6):
                QT, QTn = QTs[curq], QTs[curq ^ 1]
                for hh in range(2):
                    mm(C, pz, pz[:, (hh * 2) * 128:(hh * 2 + 1) * 128], Lc, Lc[:, 0, hh, :], QT, QT[:, hh, 0, :])
                    mm(C, pz, pz[:, (hh * 2 + 1) * 128:(hh * 2 + 2) * 128], Lc, Lc[:, 1, hh, :], QT, QT[:, hh, 1, :])
                cp(C, "act", ZZ, ZZ[:, :, :, :], pz, pz[:, :].rearrange("p (h a t) -> p h a t", h=2, a=2))
                Lc = make_L(lvl + 1)
                yield
                for hh in range(2):
                    o0 = pq_[:, (hh * 2) * 128:(hh * 2 + 1) * 128]
                    o1 = pq_[:, (hh * 2 + 1) * 128:(hh * 2 + 2) * 128]
                    mm(C, pq_, o0, QT, QT[:, hh, 1, :], ZZ, ZZ[:, hh, 0, :], start=True, stop=False)
                    mm(C, pq_, o0, C.identb, C.identb[:], QT, QT[:, hh, 0, :], start=False, stop=True)
                    mm(C, pq_, o1, QT, QT[:, hh, 0, :], ZZ, ZZ[:, hh, 1, :], start=True, stop=False)
                    mm(C, pq_, o1, C.identb, C.identb[:], QT, QT[:, hh, 1, :], start=False, stop=True)
                cp(C, "act", QTn, QTn[:, :, :, :], pq_, pq_[:, :].rearrange("p (h a t) -> p h a t", h=2, a=2))
                curq ^= 1
                yield
            QT = QTs[curq]
            for hh in range(2):
                mm(C, pz, pz[:, hh * 128:(hh + 1) * 128], Lc, Lc[:, 0, hh, :], QT, QT[:, hh, 0, :])
            cp(C, "act", ZZ, ZZ[:, :, 0, :], pz, pz[:, 0:256].rearrange("p (h t) -> p h t", h=2))
            yield
            for hh in range(2):
                o0 = pq_[:, hh * 128:(hh + 1) * 128]
                mm(C, pq_, o0, QT, QT[:, hh, 1, :], ZZ, ZZ[:, hh, 0, :], start=True, stop=False)
                mm(C, pq_, o0, C.identb, C.identb[:], QT, QT[:, hh, 0, :], start=False, stop=True)
            cp(C, "act", P, P[:, :, :], pq_, pq_[:, 0:256].rearrange("p (h t) -> p h t", h=2))
            yield
            pt_ = pB
            for i3, src in enumerate((at, bh, kh)):
                mm(C, pt_, pt_[:, i3 * 128:(i3 + 1) * 128], src, src[:, hp, T_], C.identb, C.identb[:])
            cp(C, "act", tok3, tok3[:, :, :], pt_, pt_[:, 0:384].rearrange("p (i f) -> p i f", i=3))
            yield
            pzz = pA
            for hh in range(2):
                mm(C, pzz, pzz[:, hh * 64:(hh + 1) * 64], AT, AT[:, hh, 256:384], Vt, vsl(hh))
            for hh in range(2):
                cp(C, "pool", rGH, rGH[:, hh, 0:64], tok3, tok3[:, 0, hh * 64:(hh + 1) * 64])
            cp(C, "dve", rGH, rGH[:, :, 64:128], pzz, pzz[:, 0:128].rearrange("p (h v) -> p h v", h=2))
            yield
            pg = pB
            for hh in range(2):
                mm(C, pg, pg[:, hh * 128:(hh + 1) * 128], P, P[:, hh, :], rGH, rGH[:, hh, :])
                mm(C, pg, pg[hh * 64:(hh + 1) * 64, 256:384], tok3, tok3[:, 0, hh * 64:(hh + 1) * 64], P, P[:, hh, :])
            cp(C, "act", GHs, GHs[:, :, :], pg, pg[:, 0:256].rearrange("p (h f) -> p h f", h=2))
            cp(C, "dve", GT, GT[:, :], pg, pg[:, 256:384])
            yield
            while done[hp] < myn:
                yield
            pu = pA
            for hh in range(2):
                R = slice(hh * 64, hh * 64 + 64)
                mm(C, pu, pu[:, hh * 64:(hh + 1) * 64], C.identb, C.identb[:], GHs, GHs[:, hh, 64:128], start=True, stop=False)
                mm(C, pu, pu[:, hh * 64:(hh + 1) * 64], GT, GT[R, :], Sb, Sb[R, :], start=False, stop=True)
            cp(C, "act", Us, Us[:, :, :], pu, pu[:, 0:128].rearrange("p (h v) -> p h v", h=2))
            yield
            if emit_y:
                py = pB
                for hh in range(2):
                    R = slice(hh * 64, hh * 64 + 64)
                    o_ = py[:, hh * 64:(hh + 1) * 64]
                    mm(C, py, o_, rt, rt[R, hp, T_], Sb, Sb[R, :], start=True, stop=False)
                    mm(C, py, o_, AT, AT[:, hh, 128:256], Us, Us[:, hh, :], start=False, stop=False)
                    mm(C, py, o_, AT, AT[:, hh, 384:512], Vt, vsl(hh), start=False, stop=True)
                if first_pass:
                    cp(C, "dve", Yacc, Yacc[:, cidx, hp * 128:(hp + 1) * 128], py, py[:, 0:128])
                else:
                    tt(C, "dve", Yacc, Yacc[:, cidx, hp * 128:(hp + 1) * 128], py, py[:, 0:128], Yacc, Yacc[:, cidx, hp * 128:(hp + 1) * 128], ALU.add)
            ps_ = pA
            for hh in range(2):
                o_ = ps_[hh * 64:(hh + 1) * 64, 256:320]
                mm(C, ps_, o_, tok3, tok3[:, 1, hh * 64:(hh + 1) * 64], Us, Us[:, hh, :], start=True, stop=False)
                mm(C, ps_, o_, tok3, tok3[:, 2, hh * 64:(hh + 1) * 64], Vt, vsl(hh), start=False, stop=True)
            stt(C, S, S[:, :], S, S[:, :], WC, WC[:, hp, jj:jj + 1], ps_, ps_[:, 256:320], ALU.mult, ALU.add)
            cp(C, "act", Sb, Sb[:, :], S, S[:, :])
            done[hp] += 1
            free_sets.append(bidx)
            yield

        PSTEPS = 2

        def run_pass(d):
            tl = tiles_of(TN)
            order = tl if d == 0 else [tl[0]] + tl[:0:-1]
            done[0] = done[1] = 0
            seq = {0: 0, 1: 0}
            active = []
            state = {"prep": None}

            def step_all():
                for g in list(active):
                    try:
                        next(g)
                    except StopIteration:
                        active.remove(g)
                if state["prep"] is not None:
                    try:
                        for _ in range(PSTEPS):
                            next(state["prep"])
                    except StopIteration:
                        state["prep"] = None

            def mk_prep(ti):
                t0, n, s = order[ti]
                return prep(t0, n, s, d, d == 0, opsets[ti % 3])

            for _ in mk_prep(0):
                pass
            for ti, (t0, n, s) in enumerate(order):
                if state["prep"] is not None:
                    for _ in state["prep"]:
                        pass
                state["prep"] = mk_prep(ti + 1) if ti + 1 < len(order) else None
                oset = opsets[ti % 3]
                jjs = list(range(n // 128))
                if d == 1:
                    jjs = jjs[::-1]
                for jj in jjs:
                    for hp in range(2):
                        active.append(unit(t0 // 128 + jj, jj, d, hp, not (last and s), d == 0, oset, None, seq[hp]))
                        seq[hp] += 1
                        while len(active) >= MAXFLY:
                            step_all()
            while active:
                step_all()

        for d in range(2):
            for hp in range(2):
                K.ins("dve", [], [S_[hp]], nc.vector.memset, S_[hp][:], 0.0)
                K.ins("dve", [], [Sb_[hp]], nc.vector.memset, Sb_[hp][:], 0.0)
            run_pass(d)
        for cidx in range(2 if last else 0, NCH if getattr(C, "rdbg", 9) >= 8 else 0):
            K.dma("sp", gl[:, :], C.gtok_d[cidx * 128:(cidx + 1) * 128, :], [C.gtok_d], gl)
            y4 = Yacc[:, cidx, :].rearrange("p (h v) -> p h v", h=4)
            K.ins("dve", [Yacc], [st4], nc.vector.tensor_reduce, out=st4[:, :], in_=y4, axis=AX.X, op=ALU.add)
            K.ins("dve", [st4], [st4], nc.vector.tensor_scalar, out=st4[:, :], in0=st4[:, :], scalar1=1.0 / 64.0, scalar2=None, op0=ALU.mult)
            yt4 = ytmp[:, :].rearrange("p (h v) -> p h v", h=4)
            tt(C, "dve", ytmp, yt4, Yacc, y4, st4, st4[:, :].unsqueeze(2).broadcast_to([128, 4, 64]), ALU.subtract)
            tt(C, "pool", otok, otok[:, :], ytmp, ytmp[:, :], ytmp, ytmp[:, :], ALU.mult)
            K.ins("dve", [otok], [st4b], nc.vector.tensor_reduce, out=st4b[:, :], in_=otok[:, :].rearrange("p (h v) -> p h v", h=4), axis=AX.X, op=ALU.add)
            act(C, st4b, st4b[:, :], st4b, st4b[:, :], AF.Sqrt, extra=[C.gneps], scale=1.0 / 64.0, bias=C.gneps[:, 0:1])
            K.ins("dve", [st4b], [st4b], nc.vector.reciprocal, out=st4b[:, :], in_=st4b[:, :])
            tt(C, "dve", ytmp, yt4, ytmp, yt4, st4b, st4b[:, :].unsqueeze(2).broadcast_to([128, 4, 64]), ALU.mult)
            tt(C, "pool", ytmp, ytmp[:, :], ytmp, ytmp[:, :], lnw, lnw[:, :], ALU.mult)
            tt(C, "dve", ytmp, ytmp[:, :], ytmp, ytmp[:, :], lnb, lnb[:, :], ALU.add)
            tt(C, "dve", otok, otok[:, :].rearrange("p (h v) -> p h v", h=4), Vtok, Vtok[:, cidx, :].rearrange("p (h v) -> p h v", h=4), bon, bon[:, cidx, :].unsqueeze(2).broadcast_to([128, 4, 64]), ALU.mult)
            tt(C, "pool", ytmp, ytmp[:, :], ytmp, ytmp[:, :], otok, otok[:, :], ALU.add)
            tt(C, "dve", otok, otok[:, :], ytmp, ytmp[:, :], gl, gl[:, :], ALU.mult)
            po = psQ[cidx % 2]
            for hp in range(2):
                K.ins("pe", [otok, C.ident], [po], nc.tensor.transpose, out=po[:, hp * 128:(hp + 1) * 128], in_=otok[:, hp * 128:(hp + 1) * 128], identity=C.ident[:])
            cp(C, "act", obst, obst[:, :, :], po, po[:, 0:256].rearrange("p (h t) -> p h t", h=2))
            K.dma("sp", C.obr[768:1024, cidx * 128:(cidx + 1) * 128].rearrange("(h p) t -> p h t", p=128), obst[:, :, :], [obst], C.obr)
```

```python
import numpy as np
import concourse.bass as bass
import concourse.mybir as mybir

F32 = mybir.dt.float32
BF16 = mybir.dt.bfloat16
AF = mybir.ActivationFunctionType
ALU = mybir.AluOpType
AX = mybir.AxisListType


class Tk:
    __slots__ = ("name", "t", "w", "r", "dsem", "dcnt", "psum")

    def __init__(self, name, t=None):
        self.name = name
        self.t = t
        self.w = None
        self.r = {}
        self.dsem = None
        self.dcnt = 0
        self.psum = False

    def __getitem__(self, k):
        return self.t[k]


class Kern:
    EP = 16000

    def __init__(self, nc, es):
        self.nc = nc
        self.es = es
        self.H = {"pe": nc.tensor, "act": nc.scalar, "dve": nc.vector, "pool": nc.gpsimd, "sp": nc.sync}
        self.q = {e: [] for e in self.H}
        self.known = {e: {} for e in self.H}
        self.nsem = 0
        self.dsems = []
        self.dtks = []
        self.lastc = {e: -1 for e in self.H}

    def sb(self, name, shape, dt, stack=None):
        self.uid = getattr(self, "uid", 0) + 1
        name = f"{name}_{self.uid}"
        t = (stack or self.es).enter_context(self.nc.sbuf_tensor(name, list(shape), dt))
        tk = Tk(name, t)
        return tk

    def ps(self, name, shape, dt=F32, stack=None):
        self.uid = getattr(self, "uid", 0) + 1
        name = f"{name}_{self.uid}"
        t = (stack or self.es).enter_context(self.nc.psum_tensor(name, list(shape), dt))
        tk = Tk(name, t)
        tk.psum = True
        return tk

    def dram(self, name, shape, dt, kind="Internal"):
        t = self.nc.dram_tensor(name, list(shape), dt, kind=kind).ap()
        return Tk(name, t)

    def newsem(self, name):
        self.nsem += 1
        return self.es.enter_context(self.nc.semaphore(name))

    def _need(self, e, tok, waits, same_ok):
        if tok is None:
            return
        if tok[0] == "e":
            _, f, idx = tok
            if f == e and (same_ok or e in ("pe", "sp")):
                return
            if self.known[e].get(f, -1) >= idx:
                return
            self.known[e][f] = idx
            waits.append(tok)
            self.q[f][idx][2] = True
        else:
            _, sid, val = tok
            if self.known[e].get(("d", sid), 0) >= val:
                return
            self.known[e][("d", sid)] = val
            waits.append(tok)

    def _deps(self, e, reads, writes, nowaw=False):
        waits = []
        for t in reads:
            self._need(e, t.w, waits, False)
            if t.psum:
                for tok in t.r.values():
                    self._need(e, tok, waits, True)
        for t in writes:
            if not (nowaw and t.w is not None and t.w[0] == "d"):
                self._need(e, t.w, waits, True)
            for tok in t.r.values():
                self._need(e, tok, waits, True)
        return waits

    def ins(self, e, reads, writes, _f, *args, _selfwait=None, **kwargs):
        fn = lambda: _f(*args, **kwargs)
        waits = self._deps(e, reads, writes)
        if _selfwait is not None and self.known[e].get(e, -1) < _selfwait[2]:
            self.known[e][e] = _selfwait[2]
            waits.append(_selfwait)
            self.q[e][_selfwait[2]][2] = True
        idx = len(self.q[e])
        self.q[e].append([fn, waits, False, None])
        self.lastc[e] = idx
        tok = ("e", e, idx)
        for t in reads:
            t.r[e] = tok
        for t in writes:
            t.w = tok
            t.r = {}
        return tok

    def dma(self, e, out_ap, in_ap, reads, write, nowaw=True, **kw):
        waits = self._deps(e, reads, [write], nowaw=nowaw)
        if write.dsem is None:
            free = getattr(self, "dfree", None)
            if free is None:
                free = self.dfree = []
                self.dcount = {}
            if free:
                write.dsem = free.pop()
            else:
                write.dsem = len(self.dsems)
                self.dsems.append(self.newsem(f"d{len(self.dsems)}"))
                self.dcount[write.dsem] = 0
            self.dtks.append(write)
        self.dcount[write.dsem] += 16
        write.dcnt = self.dcount[write.dsem]
        tok = ("d", write.dsem, write.dcnt)
        H = self.H[e]
        self.q[e].append([lambda: H.dma_start(out=out_ap, in_=in_ap, **kw), waits, False, tok])
        for t in reads:
            t.r[("d", write.dsem)] = tok
        write.w = tok
        return tok

    def barrier(self):
        last = dict(self.lastc)
        for e in self.H:
            waits = []
            for f in self.H:
                if f != e and last[f] >= 0:
                    self._need(e, ("e", f, last[f]), waits, False)
            for tk in self.dtks:
                self._need(e, ("d", tk.dsem, tk.dcnt), waits, False)
            self.q[e].append([None, waits, False, None])
        for tk in self.dtks:
            self.dfree.append(tk.dsem)
            tk.dsem = None
        self.dtks = []

    def emit(self):
        nc = self.nc
        sigmap = {}
        for e, lst in self.q.items():
            cnt = 0
            m = {}
            sems = []
            for idx, rec in enumerate(lst):
                if rec[2]:
                    ep, v = divmod(cnt, self.EP)
                    if ep >= len(sems):
                        sems.append(self.newsem(f"e_{e}_{ep}"))
                    m[idx] = (sems[ep], v + 1)
                    cnt += 1
            sigmap[e] = m
        for e, lst in self.q.items():
            H = self.H[e]
            for idx, rec in enumerate(lst):
                fn, waits, sig, dtok = rec
                for tok in waits:
                    if tok[0] == "e":
                        s, v = sigmap[tok[1]][tok[2]]
                        H.wait_ge(s, v)
                    else:
                        H.wait_ge(self.dsems[tok[1]], tok[2])
                if fn is None:
                    assert not sig
                    continue
                ins = fn()
                if dtok is not None:
                    ins.then_inc(self.dsems[dtok[1]], 16)
                    assert not sig
                elif sig:
                    ins.then_inc(sigmap[e][idx][0], 1)
        print("emitted:", {e: len(l) for e, l in self.q.items()}, "sems", self.nsem)


from concourse.bass_utils import run_bass_kernel_spmd
import contextlib
import ml_dtypes

T_ALL = 4352
LC = 256
EPS = 1e-6


EXTRA_INPUTS = [
    ("bands", [128, 20, 128]), ("pw_bd", [2, 2, 128, 128]), ("pscale", [128, 2, 2]),
    ("w_uq", [2, 384, 768]), ("w_ukv_k", [2, 256, 512]), ("w_ukv_v", [2, 256, 512]),
    ("Cq", [128, 128]), ("Prot", [128, 128]), ("gq96", [128, 2]), ("gk96", [128, 2]),
    ("qng", [128, 2, 3]), ("kvng", [128, 2, 2]), ("cs", [32, 2, 4096]),
    ("r_mu", [128, 2, 9, 2]), ("r_kk", [128, 2, 2]), ("r_w0", [128, 2, 2, 2]), ("r_a0", [128, 2, 2, 2]), ("r_ka", [128, 2, 2, 2]),
    ("r_rk", [128, 2, 2]), ("r_w2", [2, 2, 128, 256]), ("r_a2", [2, 2, 128, 256]), ("r_g2", [2, 128, 256]),
    ("r_lnwb", [2, 128, 256]), ("r_lnbb", [2, 128, 256]), ("blk64", [128, 128]), ("ind2", [128, 2]),
    ("masks", [128, 4, 128]), ("rmask", [128, 512]), ("lvmask", [128, 14, 128]),
]


def tiles_of(n, last_layer=False):
    ts = [] if last_layer else [(0, LC, 1)]
    t = LC
    while t < T_ALL:
        ts.append((t, n, 0))
        t += n
    return ts


class Ctx:
    pass


def build(nc, es, dbg=None, stop_after=None):
    K = Kern(nc, es)
    C = Ctx()
    C.rdbg = (dbg or {}).get("rdbg", 9)
    C.pdbg = (dbg or {}).get("pdbg", 9)
    C.sdbg = (dbg or {}).get("sdbg", 9)
    C.bdbg = (dbg or {}).get("bdbg", 9)
    C.maxfly = (dbg or {}).get("maxfly", 4)
    C.ustop = (dbg or {}).get("ustop", 0)
    C.onebank = (dbg or {}).get("onebank", False)
    C.K = K
    C.nc = nc
    dbg = dbg or {}
    ext = lambda name, shape, dt=F32: K.dram(name, shape, dt, kind="ExternalInput")
    I = {}
    I["xin"] = ext("xin", [T_ALL, 1024])
    I["ccT"] = ext("ccT", [128, 16])
    I["w_ada"] = ext("w_ada", [2, 1024, 6144])
    I["badaT"] = ext("badaT", [128, 96])
    I["ng"] = ext("ng", [128, 32])
    I["w_in"] = ext("w_in", [2, 1024, 5152])
    I["w_br"] = ext("w_br", [2, 1024, 1024])
    I["w_o"] = ext("w_o", [2, 1024, 1024])
    I["mlp_w1"] = ext("mlp_w1", [2, 1024, 4096])
    I["mlp_w2"] = ext("mlp_w2", [2, 4096, 1024])
    I["ident"] = ext("ident", [128, 128])
    for nm, shp in EXTRA_INPUTS:
        I[nm] = ext(nm, shp)
    C.I = I
    out = K.dram("out", [4096, 1024], F32, kind="ExternalOutput")
    C.out = out
    douts = dbg.get("outs", ())
    scr = lambda name, shape, dt: K.dram(name, shape, dt, kind=("ExternalOutput" if name in douts else "Internal"))
    C.scr = scr
    C.xT = [scr("xT0", [1024, T_ALL], F32), scr("xT1", [1024, T_ALL], F32)]
    C.zT = scr("zT", [15, 128, T_ALL], F32)
    C.u_d = scr("u_d", [T_ALL, 256], BF16)
    C.obr = scr("obr", [1024, T_ALL], BF16)
    C.moddbg = scr("moddbg", [128, 192], F32)
    C.gtok_d = scr("gtok", [T_ALL, 256], BF16)
    if "obr_in" in dbg:
        C.obr_in = ext("obr_in", [1024, T_ALL])
    C.ident = K.sb("ident_s", [128, 128], F32)
    C.identb = K.sb("identb_s", [128, 128], BF16)
    C.ones = K.sb("ones_s", [128, 128], F32)
    C.epsb = K.sb("epsb_s", [128, 1], F32)
    C.gneps = K.sb("gneps_s", [128, 1], F32)
    C.mod = K.sb("mod_s", [128, 2, 48, 2], F32)
    C.gm = K.sb("gm_s", [128, 2, 2, 8, 2], F32)
    C.ng = K.sb("ng_s", [128, 2, 2, 8], F32)
    K.dma("sp", C.ident[:], I["ident"][:, :], [I["ident"]], C.ident)
    K.dma("sp", C.ng[:].rearrange("p a b c -> p (a b c)"), I["ng"][:, :], [I["ng"]], C.ng)
    K.ins("dve", [], [C.ones], nc.vector.memset, C.ones[:], 1.0)
    K.ins("dve", [], [C.epsb], nc.vector.memset, C.epsb[:], EPS)
    K.ins("dve", [], [C.gneps], nc.vector.memset, C.gneps[:], 64e-5)
    K.ins("dve", [C.ident], [C.identb], nc.vector.tensor_copy, out=C.identb[:], in_=C.ident[:])

    SKIP = dbg.get("skip_pre", False)
    if not SKIP:
        phase_M(C)
    K.barrier()
    if "moddbg" in douts:
        K.dma("sp", C.moddbg[:, :], C.mod[:].rearrange("p l m s -> p (l m s)"), [C.mod], C.moddbg)
    if not SKIP:
        phase_T0(C)
    K.barrier()
    cur = 0
    for li in range(2):
        last = li == 1
        if not SKIP:
            phase_A(C, li, cur)
        K.barrier()
        if stop_after == ("A", li):
            break
        if "obr_in" in dbg:
            phase_dbg_obr(C)
            K.barrier()
        else:
            mix = dbg.get("mix", ("P", "L", "R"))
            if "P" in mix:
                phase_P(C, li)
                K.barrier()
            if "L" in mix:
                phase_MLA(C, li)
                K.barrier()
            if "R" in mix:
                phase_R(C, li)
                K.barrier()
        if stop_after == ("X", li):
            break
        phase_C1(C, li, cur)
        K.barrier()
        cur ^= 1
        phase_C2(C, li, cur)
        K.barrier()
        cur ^= 1
        if stop_after == ("C", li):
            break
    K.barrier()
    K.emit()
    K.C = C
    return K


def phase_dbg_obr(C):
    K, nc = C.K, C.nc
    with contextlib.ExitStack() as st:
        a = K.sb("dbo_a", [128, 8, 512], F32, st)
        b = K.sb("dbo_b", [128, 8, 512], BF16, st)
        for t0 in range(0, T_ALL, 512):
            n = min(512, T_ALL - t0)
            K.dma("sp", a[:, :, :n], C.obr_in[:, t0:t0 + n].rearrange("(c p) t -> p c t", p=128), [C.obr_in], a)
            K.ins("dve", [a], [b], nc.vector.tensor_copy, out=b[:, :, :n], in_=a[:, :, :n])
            K.dma("sp", C.obr[:, t0:t0 + n].rearrange("(c p) t -> p c t", p=128), b[:, :, :n], [b], C.obr)


def phase_M(C):
    K, nc, I = C.K, C.nc, C.I
    with contextlib.ExitStack() as st:
        cc = K.sb("m_cc", [128, 8, 2], F32, st)
        sc = K.sb("m_sc", [128, 8, 2], F32, st)
        bada = K.sb("m_bada", [128, 2, 48], F32, st)
        wb = [K.sb(f"m_w{j}", [128, 8, 768], F32, st) for j in range(2)]
        psMs = [K.ps(f"m_ps{j}", [128, 512], F32, st) for j in range(4)]
        mraw = K.sb("m_raw", [128, 48, 2], F32, st)
        K.dma("sp", cc[:].rearrange("p k s -> p (k s)"), I["ccT"][:, :], [I["ccT"]], cc)
        K.dma("sp", bada[:].rearrange("p l m -> p (l m)"), I["badaT"][:, :], [I["badaT"]], bada)
        K.ins("act", [cc], [sc], nc.scalar.activation, out=sc[:], in_=cc[:], func=AF.Silu)
        n = 0
        for li in range(2):
            for blk in range(8):
                w = wb[n % 2]
                n += 1
                for kc in range(8):
                    K.dma("sp" if kc % 2 == 0 else "act", w[:, kc, :], I["w_ada"][li, kc * 128:(kc + 1) * 128, blk * 768:(blk + 1) * 768], [I["w_ada"]], w)
                for m6 in range(6):
                    m = blk * 6 + m6
                    pm = psMs[m % 4]
                    for kc in range(8):
                        K.ins("pe", [w, sc], [pm], nc.tensor.matmul, pm[:, 0:2], lhsT=w[:, kc, m6 * 128:(m6 + 1) * 128], rhs=sc[:, kc, :], start=(kc == 0), stop=(kc == 7))
                    K.ins("dve", [pm], [mraw], nc.vector.tensor_copy, out=mraw[:, m, :], in_=pm[:, 0:2])
            K.ins("dve", [mraw, bada], [C.mod], nc.vector.tensor_tensor, out=C.mod[:, li, :, :], in0=mraw[:], in1=bada[:, li, :].unsqueeze(2).broadcast_to([128, 48, 2]), op=ALU.add)
            for which in range(2):
                K.ins("dve", [C.mod, C.ng], [C.gm], nc.vector.scalar_tensor_tensor,
                    out=C.gm[:, which, li, :, :], in0=C.mod[:, li, 8 + 24 * which:16 + 24 * which, :], scalar=1.0,
                    in1=C.ng[:, which, li, :].unsqueeze(2).broadcast_to([128, 8, 2]), op0=ALU.add, op1=ALU.mult)


def phase_T0(C):
    K, nc, I = C.K, C.nc, C.I
    with contextlib.ExitStack() as st:
        xin = [K.sb(f"t0_x{j}", [128, 4, 1024], F32, st) for j in range(2)]
        xo = [K.sb(f"t0_o{j}", [128, 8, 512], F32, st) for j in range(2)]
        pss = [K.ps(f"t0_p{j}", [128, 512], F32, st) for j in range(4)]
        npz = 0
        for it, t0 in enumerate(range(0, T_ALL, 512)):
            n = min(512, T_ALL - t0)
            nj = n // 128
            xi = xin[it % 2]
            o = xo[it % 2]
            K.dma("sp", xi[:, :nj, :], I["xin"][t0:t0 + n, :].rearrange("(j p) d -> p j d", p=128), [I["xin"]], xi)
            for c in range(8):
                p = pss[npz % 4]
                npz += 1
                for j in range(nj):
                    K.ins("pe", [xi, C.ident], [p], nc.tensor.transpose, out=p[:, j * 128:(j + 1) * 128], in_=xi[:, j, c * 128:(c + 1) * 128], identity=C.ident[:])
                if c % 2 == 0:
                    K.ins("dve", [p], [o], nc.vector.tensor_copy, out=o[:, c, :n], in_=p[:, :n])
                else:
                    K.ins("act", [p], [o], nc.scalar.copy, out=o[:, c, :n], in_=p[:, :n])
            K.dma("act", C.xT[0][:, t0:t0 + n].rearrange("(c p) t -> p c t", p=128), o[:, :, :n], [o], C.xT[0])


def compute_h_g(C, xt, wk, hT, psS, rstd, which, li, s, n):
    K, nc = C.K, C.nc
    K.ins("act", [xt], [wk], nc.scalar.activation, out=wk[:, :, :n], in_=xt[:, :, :n], func=AF.Square)
    yield
    for c in range(8):
        K.ins("pe", [C.ones, wk], [psS], nc.tensor.matmul, psS[:, :n], lhsT=C.ones[:], rhs=wk[:, c, :n], start=(c == 0), stop=(c == 7))
        if c % 4 == 3:
            yield
    K.ins("act", [psS, C.epsb], [rstd], nc.scalar.activation, out=rstd[:, :n], in_=psS[:, :n], func=AF.Ln, scale=1.0 / 1024.0, bias=C.epsb[:, 0:1])
    yield
    K.ins("act", [rstd], [rstd], nc.scalar.activation, out=rstd[:, :n], in_=rstd[:, :n], func=AF.Exp, scale=-0.5)
    yield
    shb = 0 if which == 0 else 24
    for c in range(8):
        K.ins("dve", [xt, C.gm, rstd], [wk], nc.vector.scalar_tensor_tensor, out=wk[:, c, :n], in0=xt[:, c, :n], scalar=C.gm[:, which, li, c, s:s + 1], in1=rstd[:, :n], op0=ALU.mult, op1=ALU.mult)
        yield
        K.ins("act", [wk, C.mod], [hT], nc.scalar.activation, out=hT[:, c, :n], in_=wk[:, c, :n], func=AF.Identity, bias=C.mod[:, li, shb + c, s:s + 1], scale=1.0)
        yield


def compute_h(C, xt, wk, hT, psS, rstd, which, li, s, n):
    for _ in compute_h_g(C, xt, wk, hT, psS, rstd, which, li, s, n):
        pass


def rr(*gens):
    gens = [g for g in gens if g is not None]
    while gens:
        for g in list(gens):
            try:
                next(g)
            except StopIteration:
                gens.remove(g)


def pipeline_tiles(ntiles, load_fn, h_fn, main_fn):
    load_fn(0)
    for _ in h_fn(0):
        pass
    for i in range(ntiles):
        nxt = None
        if i + 1 < ntiles:
            load_fn(i + 1)
            nxt = h_fn(i + 1)
        rr(main_fn(i), nxt)


def load_w_bf16(C, dst, src_ap_fn, nk, src_tk):
    K = C.K
    for kc in range(nk):
        K.dma("pool", dst[:, kc, :], src_ap_fn(kc), [src_tk], dst)


A_CHUNKS = [(256 + 128 * m, 128) for m in range(3)] + [(640, 128), (768, 128), (896, 32)] + [(928 + 128 * m, 128) for m in range(9)]


def phase_A(C, li, cur):
    K, nc, I = C.K, C.nc, C.I
    with contextlib.ExitStack() as st:
        wA = K.sb("a_w", [128, 8, 2080], BF16, st)
        load_w_bf16(C, wA, lambda kc: I["w_in"][li, kc * 128:(kc + 1) * 128, 0:2080], 8, I["w_in"])
        xts = [K.sb(f"a_x{j}", [128, 8, 512], F32, st) for j in range(2)]
        wk = K.sb("a_wk", [128, 8, 512], F32, st)
        hTs = [K.sb(f"a_h{j}", [128, 8, 512], BF16, st) for j in range(2)]
        rstd = K.sb("a_rstd", [128, 512], F32, st)
        stg = [K.sb(f"a_stg{j}", [128, 15, 512], F32, st) for j in range(1)]
        ustg = K.sb("a_ustg", [128, 4, 256], BF16, st)
        psS = K.ps("a_psS", [128, 512], F32, st)
        psP = [K.ps(f"a_psP{j}", [128, 512], F32, st) for j in range(4)]
        psU = [K.ps(f"a_psU{j}", [128, 512], F32, st) for j in range(2)]
        tiles = tiles_of(512)
        cntp = {"npp": 0}

        def load_fn(i):
            t0, n, s = tiles[i]
            xt = xts[i % 2]
            K.dma("sp", xt[:, :, :n], C.xT[cur][:, t0:t0 + n].rearrange("(c p) t -> p c t", p=128), [C.xT[cur]], xt)

        def h_fn(i):
            t0, n, s = tiles[i]
            return compute_h_g(C, xts[i % 2], wk, hTs[i % 2], psS, rstd, 0, li, s, n)

        def main_fn(i):
            t0, n, s = tiles[i]
            hT = hTs[i % 2]
            sg = stg[0]
            for mi, (c0, M) in enumerate(A_CHUNKS):
                p = psP[cntp["npp"] % 4]
                cntp["npp"] += 1
                for kc in range(8):
                    K.ins("pe", [wA, hT], [p], nc.tensor.matmul, p[:M, :n], lhsT=wA[:, kc, c0:c0 + M], rhs=hT[:, kc, :n], start=(kc == 0), stop=(kc == 7))
                if mi % 2 == 0:
                    K.ins("dve", [p], [sg], nc.vector.tensor_copy, out=sg[:M, mi, :n], in_=p[:M, :n])
                else:
                    K.ins("act", [p], [sg], nc.scalar.copy, out=sg[:M, mi, :n], in_=p[:M, :n])
                yield
            K.dma("sp", C.zT[:, :, t0:t0 + n].rearrange("m p t -> p m t"), sg[:, :, :n], [sg], C.zT)
            for j in range(n // 128):
                p = psU[j % 2]
                for kc in range(8):
                    K.ins("pe", [wA, hT], [p], nc.tensor.matmul, p[:, 0:256], lhsT=hT[:, kc, j * 128:(j + 1) * 128], rhs=wA[:, kc, 0:256], start=(kc == 0), stop=(kc == 7))
                K.ins("act", [p], [ustg], nc.scalar.copy, out=ustg[:, j, :], in_=p[:, 0:256])
                yield
            K.dma("sp", C.u_d[t0:t0 + n, :].rearrange("(j p) c -> p j c", p=128), ustg[:, :n // 128, :], [ustg], C.u_d)

        pipeline_tiles(len(tiles), load_fn, h_fn, main_fn)


BR_K = [(0, 2), (2, 4), (6, 2)]


def phase_C1(C, li, cur):
    K, nc, I = C.K, C.nc, C.I
    last = li == 1
    with contextlib.ExitStack() as st:
        wG = K.sb("c1_wg", [128, 8, 3072], BF16, st)
        wB = K.sb("c1_wb", [128, 8, 1024], BF16, st)
        wO = K.sb("c1_wo", [128, 8, 1024], BF16, st)
        load_w_bf16(C, wG, lambda kc: I["w_in"][li, kc * 128:(kc + 1) * 128, 2080:5152], 8, I["w_in"])
        load_w_bf16(C, wB, lambda kc: I["w_br"][li, kc * 128:(kc + 1) * 128, :], 8, I["w_br"])
        load_w_bf16(C, wO, lambda kc: I["w_o"][li, kc * 128:(kc + 1) * 128, :], 8, I["w_o"])
        xts = [K.sb(f"c1_x{j}", [128, 8, 512], F32, st) for j in range(2)]
        wk = K.sb("c1_wk", [128, 8, 512], F32, st)
        hTs = [K.sb(f"c1_h{j}", [128, 8, 512], BF16, st) for j in range(2)]
        obs = [K.sb(f"c1_ob{j}", [128, 8, 512], BF16, st) for j in range(2)]
        mT = K.sb("c1_m", [128, 8, 512], BF16, st)
        rstd = K.sb("c1_rstd", [128, 512], F32, st)
        sgs = [K.sb(f"c1_sg{j}", [128, 512], BF16, st) for j in range(3)]
        tt_ = [K.sb(f"c1_t{j}", [128, 512], F32, st) for j in range(3)]
        psS = K.ps("c1_psS", [128, 512], F32, st)
        psG = [K.ps(f"c1_psG{j}", [128, 512], F32, st) for j in range(3)]
        psB = [K.ps(f"c1_psB{j}", [128, 512], F32, st) for j in range(3)]
        psO = K.ps("c1_psO", [128, 512], F32, st)
        tiles = tiles_of(512, last)

        def load_fn(i):
            t0, n, s = tiles[i]
            K.dma("sp", xts[i % 2][:, :, :n], C.xT[cur][:, t0:t0 + n].rearrange("(c p) t -> p c t", p=128), [C.xT[cur]], xts[i % 2])
            K.dma("sp", obs[i % 2][:, :, :n], C.obr[:, t0:t0 + n].rearrange("(c p) t -> p c t", p=128), [C.obr], obs[i % 2])

        def h_fn(i):
            t0, n, s = tiles[i]
            return compute_h_g(C, xts[i % 2], wk, hTs[i % 2], psS, rstd, 0, li, s, n)

        def main_fn(i):
            t0, n, s = tiles[i]
            xt, hT, ob = xts[i % 2], hTs[i % 2], obs[i % 2]
            for mo in range(8):
                for b in range(3):
                    pg, pb = psG[b], psB[b]
                    gc = (b * 8 + mo) * 128
                    for kc in range(8):
                        K.ins("pe", [wG, hT], [pg], nc.tensor.matmul, pg[:, :n], lhsT=wG[:, kc, gc:gc + 128], rhs=hT[:, kc, :n], start=(kc == 0), stop=(kc == 7))
                    k0, nk = BR_K[b]
                    for kk in range(nk):
                        K.ins("pe", [wB, ob], [pb], nc.tensor.matmul, pb[:, :n], lhsT=wB[:, k0 + kk, mo * 128:(mo + 1) * 128], rhs=ob[:, k0 + kk, :n], start=(kk == 0), stop=(kk == nk - 1))
                    K.ins("act", [pg], [sgs[b]], nc.scalar.activation, out=sgs[b][:, :n], in_=pg[:, :n], func=AF.Sigmoid)
                    K.ins("dve", [pb, sgs[b]], [tt_[b]], nc.vector.tensor_tensor, out=tt_[b][:, :n], in0=pb[:, :n], in1=sgs[b][:, :n], op=ALU.mult)
                    yield
                K.ins("pool", [tt_[0], tt_[1]], [tt_[0]], nc.gpsimd.tensor_tensor, out=tt_[0][:, :n], in0=tt_[0][:, :n], in1=tt_[1][:, :n], op=ALU.add)
                K.ins("pool", [tt_[0], tt_[2]], [mT], nc.gpsimd.tensor_tensor, out=mT[:, mo, :n], in0=tt_[0][:, :n], in1=tt_[2][:, :n], op=ALU.add)
            for mo in range(8):
                for kc in range(8):
                    K.ins("pe", [wO, mT], [psO], nc.tensor.matmul, psO[:, :n], lhsT=wO[:, kc, mo * 128:(mo + 1) * 128], rhs=mT[:, kc, :n], start=(kc == 0), stop=(kc == 7))
                K.ins("dve", [psO, C.mod, xt], [xt], nc.vector.scalar_tensor_tensor, out=xt[:, mo, :n], in0=psO[:, :n], scalar=C.mod[:, li, 16 + mo, s:s + 1], in1=xt[:, mo, :n], op0=ALU.mult, op1=ALU.add)
                yield
            K.dma("act", C.xT[cur ^ 1][:, t0:t0 + n].rearrange("(c p) t -> p c t", p=128), xt[:, :, :n], [xt], C.xT[cur ^ 1])

        pipeline_tiles(len(tiles), load_fn, h_fn, main_fn)


def phase_C2(C, li, cur):
    K, nc, I = C.K, C.nc, C.I
    last = li == 1
    N = 256
    with contextlib.ExitStack() as st:
        w1 = K.sb("c2_w1", [128, 8, 4096], BF16, st)
        w2 = K.sb("c2_w2", [128, 32, 1024], BF16, st)
        load_w_bf16(C, w1, lambda kc: I["mlp_w1"][li, kc * 128:(kc + 1) * 128, :], 8, I["mlp_w1"])
        load_w_bf16(C, w2, lambda kc: I["mlp_w2"][li, kc * 128:(kc + 1) * 128, :], 32, I["mlp_w2"])
        xts = [K.sb(f"c2_x{j}", [128, 8, N], F32, st) for j in range(2)]
        wk = K.sb("c2_wk", [128, 8, N], F32, st)
        hTs = [K.sb(f"c2_h{j}", [128, 8, N], BF16, st) for j in range(2)]
        uT = K.sb("c2_u", [128, 32, N], BF16, st)
        rstd = K.sb("c2_rstd", [128, N], F32, st)
        rrb = [K.sb(f"c2_r{j}", [128, N], BF16, st) for j in range(3)]
        ostg = [K.sb(f"c2_os{j}", [128, 1024], F32, st) for j in range(2)] if last else None
        psS = K.ps("c2_psS", [128, 512], F32, st)
        ps1 = [K.ps(f"c2_p1{j}", [128, 512], F32, st) for j in range(3)]
        ps2 = [K.ps(f"c2_p2{j}", [128, 512], F32, st) for j in range(2)]
        psT = [K.ps(f"c2_pT{j}", [128, 512], F32, st) for j in range(2)] if last else None
        tiles = tiles_of(N, last)
        cn = {"n1": 0, "n2": 0, "nt": 0}

        def load_fn(i):
            t0, n, s = tiles[i]
            K.dma("sp", xts[i % 2][:, :, :n], C.xT[cur][:, t0:t0 + n].rearrange("(c p) t -> p c t", p=128), [C.xT[cur]], xts[i % 2])

        def h_fn(i):
            t0, n, s = tiles[i]
            return compute_h_g(C, xts[i % 2], wk, hTs[i % 2], psS, rstd, 1, li, s, n)

        def main_fn(i):
            t0, n, s = tiles[i]
            xt, hT = xts[i % 2], hTs[i % 2]
            for f in range(32):
                p = ps1[cn["n1"] % 3]
                r = rrb[cn["n1"] % 3]
                cn["n1"] += 1
                for kc in range(8):
                    K.ins("pe", [w1, hT], [p], nc.tensor.matmul, p[:, :n], lhsT=w1[:, kc, f * 128:(f + 1) * 128], rhs=hT[:, kc, :n], start=(kc == 0), stop=(kc == 7))
                K.ins("act", [p], [r], nc.scalar.activation, out=r[:, :n], in_=p[:, :n], func=AF.Relu)
                K.ins("dve", [r], [uT], nc.vector.tensor_tensor, out=uT[:, f, :n], in0=r[:, :n], in1=r[:, :n], op=ALU.mult)
                if f % 2 == 1:
                    yield
            for mo in range(8):
                p = ps2[cn["n2"] % 2]
                cn["n2"] += 1
                for f in range(32):
                    K.ins("pe", [w2, uT], [p], nc.tensor.matmul, p[:, :n], lhsT=w2[:, f, mo * 128:(mo + 1) * 128], rhs=uT[:, f, :n], start=(f == 0), stop=(f == 31))
                K.ins("dve", [p, C.mod, xt], [xt], nc.vector.scalar_tensor_tensor, out=xt[:, mo, :n], in0=p[:, :n], scalar=C.mod[:, li, 40 + mo, s:s + 1], in1=xt[:, mo, :n], op0=ALU.mult, op1=ALU.add)
                yield
            if not last:
                K.dma("act", C.xT[cur ^ 1][:, t0:t0 + n].rearrange("(c p) t -> p c t", p=128), xt[:, :, :n], [xt], C.xT[cur ^ 1])
            else:
                for j in range(n // 128):
                    og = ostg[cn["nt"] % 2]
                    for half in range(2):
                        p = psT[half]
                        for c4 in range(4):
                            c = half * 4 + c4
                            K.ins("pe", [xt, C.ident], [p], nc.tensor.transpose, out=p[:, c4 * 128:(c4 + 1) * 128], in_=xt[:, c, j * 128:(j + 1) * 128], identity=C.ident[:])
                        if half == 0:
                            K.ins("dve", [p], [og], nc.vector.tensor_copy, out=og[:, 0:512], in_=p[:, :])
                        else:
                            K.ins("act", [p], [og], nc.scalar.copy, out=og[:, 512:1024], in_=p[:, :])
                    cn["nt"] += 1
                    r0 = t0 - LC + j * 128
                    K.dma("act", C.out[r0:r0 + 128, :], og[:, :], [og], C.out)
                    yield

        pipeline_tiles(len(tiles), load_fn, h_fn, main_fn)


def host_inputs(inputs, b):
    f = lambda a: np.ascontiguousarray(a, dtype=np.float32)
    x, c, ctx, c_ctx = inputs["x"], inputs["c"], inputs["ctx"], inputs["c_ctx"]
    m = {}
    m["xin"] = f(np.concatenate([ctx[b], x[b]], axis=0))
    cc = np.stack([c[b], c_ctx], axis=-1)
    m["ccT"] = f(cc.reshape(8, 128, 2).transpose(1, 0, 2).reshape(128, 16))
    m["w_ada"] = f(inputs["w_ada"])
    m["badaT"] = f(inputs["b_ada"].reshape(2, 48, 128).transpose(2, 0, 1).reshape(128, 96))
    ng = np.stack([inputs["norm1_g"], inputs["norm2_g"]], axis=0)
    m["ng"] = f(ng.reshape(2, 2, 8, 128).transpose(3, 0, 1, 2).reshape(128, 32))
    m["w_in"] = f(inputs["w_in"])
    m["w_br"] = f(np.concatenate([inputs["w_br_pool"], inputs["w_br_mla"], inputs["w_br_rwkv"]], axis=1))
    m["w_o"] = f(inputs["w_o"])
    m["mlp_w1"] = f(inputs["mlp_w1"])
    m["mlp_w2"] = f(inputs["mlp_w2"])
    m["ident"] = np.eye(128, dtype=np.float32)
    m.update(host_consts())
    pw = inputs["pool_w"]
    pwbd = np.zeros((2, 2, 128, 128), np.float32)
    for g in range(4):
        a, gi = divmod(g, 2)
        pwbd[:, a, gi * 64:(gi + 1) * 64, gi * 64:(gi + 1) * 64] = pw[:, g]
    m["pw_bd"] = pwbd
    m["pscale"] = f(inputs["pool_scale"].reshape(2, 2, 128).transpose(2, 0, 1))
    m["w_uq"] = f(inputs["mla_w_uq"])
    wkv = inputs["mla_w_ukv"].reshape(2, 256, 8, 128)
    m["w_ukv_k"] = f(wkv[:, :, :, :64].reshape(2, 256, 512))
    m["w_ukv_v"] = f(wkv[:, :, :, 64:].reshape(2, 256, 512))
    g96 = np.zeros((128, 2), np.float32); g96[:96] = inputs["qk_gain_q"].T; m["gq96"] = g96
    g96 = np.zeros((128, 2), np.float32); g96[:96] = inputs["qk_gain_k"].T; m["gk96"] = g96
    m["qng"] = f(inputs["mla_q_norm"].reshape(2, 3, 128).transpose(2, 0, 1))
    m["kvng"] = f(inputs["mla_kv_norm"].reshape(2, 2, 128).transpose(2, 0, 1))
    m["r_mu"] = f(inputs["rwkv_mu"].reshape(2, 2, 9, 128).transpose(3, 0, 2, 1))
    m["r_kk"] = f(inputs["rwkv_kk"].reshape(2, 2, 128).transpose(2, 0, 1))
    for nm in ("w0", "a0", "ka"):
        m["r_" + nm] = f(inputs["rwkv_" + nm].reshape(2, 2, 2, 128).transpose(3, 0, 1, 2))
    m["r_rk"] = f(inputs["rwkv_rk"].reshape(2, 2, 128).transpose(2, 0, 1))
    for nm in ("w2", "a2"):
        z_ = np.zeros((2, 2, 128, 256), np.float32)
        for d_ in range(2):
            z_[:, d_, d_ * 64:(d_ + 1) * 64, :] = inputs["rwkv_" + nm][:, d_]
        m["r_" + nm] = z_
    m["r_g2"] = f(inputs["rwkv_g2"])
    m["r_lnwb"] = f(np.broadcast_to(inputs["rwkv_ln_w"][:, None, :], (2, 128, 256)))
    m["r_lnbb"] = f(np.broadcast_to(inputs["rwkv_ln_b"][:, None, :], (2, 128, 256)))
    return m


_HC = {}


def host_consts():
    if _HC:
        return _HC
    m = _HC
    bands = np.zeros((128, 20, 128), np.float32)
    for g, win in enumerate((2, 4, 8, 16)):
        Ls = 384
        B = np.zeros((Ls, Ls), np.float64)
        for t in range(Ls):
            lo, hi = max(t - win // 2, 0), min(t + win // 2, Ls)
            B[lo:hi, t] = 1.0 / (hi - lo)
            B[t, t] -= 1.0
        bands[:, g * 5 + 0] = B[0:128, 0:128]
        bands[:, g * 5 + 1] = B[128:256, 128:256]
        bands[:, g * 5 + 2] = B[256:384, 256:384]
        bands[:, g * 5 + 3] = B[0:128, 128:256]
        bands[:, g * 5 + 4] = B[256:384, 128:256]
    m["bands"] = bands
    Cq = np.zeros((128, 128), np.float32); Cq[0:64, 0:64] = 1.0 / 64; Cq[64:96, 64:96] = 1.0 / 32
    m["Cq"] = Cq
    Pr = np.zeros((128, 128), np.float32)
    for i in range(16):
        Pr[64 + 16 + i, 64 + i] = 1.0
        Pr[64 + i, 64 + 16 + i] = 1.0
    m["Prot"] = Pr
    tpos = np.arange(4096)
    row = (tpos // 64).astype(np.float32); col = (tpos % 64).astype(np.float32)
    inv = np.power(np.float32(10000.0), -np.arange(8, dtype=np.float32) / np.float32(8)).astype(np.float32)
    ang = np.concatenate([row[:, None] * inv, col[:, None] * inv], axis=-1).astype(np.float32)
    cs = np.zeros((32, 2, 4096), np.float32)
    cs[0:16, 0] = np.cos(ang).T; cs[16:32, 0] = np.cos(ang).T
    cs[0:16, 1] = -np.sin(ang).T; cs[16:32, 1] = np.sin(ang).T
    m["cs"] = cs
    blk = np.zeros((128, 128), np.float32); blk[0:64, 0:64] = 1; blk[64:, 64:] = 1
    m["blk64"] = blk
    ind2 = np.zeros((128, 2), np.float32); ind2[0:64, 0] = 1; ind2[64:, 1] = 1
    m["ind2"] = ind2
    rr, cc_ = np.meshgrid(np.arange(128), np.arange(128), indexing="ij")
    masks = np.stack([rr < cc_, rr <= cc_, rr > cc_, rr >= cc_], axis=1).astype(np.float32)
    m["masks"] = np.ascontiguousarray(masks)
    lv = np.zeros((128, 14, 128), np.float32)
    ii = np.arange(128)
    for li_, mlev in enumerate((1, 2, 4, 8, 16, 32, 64)):
        same = (ii[:, None] // (2 * mlev)) == (ii[None, :] // (2 * mlev))
        MA = same & ((ii[:, None] % (2 * mlev)) >= mlev) & ((ii[None, :] % (2 * mlev)) < mlev)
        lv[:, li_, :] = MA
        lv[:, 7 + li_, :] = MA.T
    m["lvmask"] = lv
    rmask = np.ones((128, 512), np.float32); rmask[:, ::128] = 0
    m["rmask"] = rmask
    return m


def kernel(**inputs):
    nc = bass.Bass("TRN2", target_bir_lowering=False)
    with contextlib.ExitStack() as es:
        build(nc, es)
    in_maps = [host_inputs(inputs, b) for b in range(8)]
    res = run_bass_kernel_spmd(nc, in_maps, core_ids=list(range(8)))
    return np.stack([np.asarray(r["out"], dtype=np.float32) for r in res.results], axis=0)


_PE_SIG = {}


def mm(C, out_tk, out_ap, l_tk, l_ap, r_tk, r_ap, start=True, stop=True):
    sig = (l_ap.base_partition(), l_ap.partition_size())
    prev = _PE_SIG.get(id(out_tk))
    sw = None
    if prev is not None and prev[0] is out_tk and prev[1] != sig and prev[1][1] < 128 and sig[1] < 128:
        sw = prev[2]
    tok = C.K.ins("pe", [l_tk, r_tk], [out_tk], C.nc.tensor.matmul, out_ap, lhsT=l_ap, rhs=r_ap, start=start, stop=stop, _selfwait=sw)
    _PE_SIG[id(out_tk)] = (out_tk, sig, tok)


def cp(C, eng, out_tk, out_ap, in_tk, in_ap):
    if eng == "act":
        C.K.ins("act", [in_tk], [out_tk], C.nc.scalar.copy, out=out_ap, in_=in_ap)
    elif eng == "dve":
        C.K.ins("dve", [in_tk], [out_tk], C.nc.vector.tensor_copy, out=out_ap, in_=in_ap)
    else:
        C.K.ins("pool", [in_tk], [out_tk], C.nc.gpsimd.tensor_copy, out=out_ap, in_=in_ap)


def tt(C, eng, out_tk, out_ap, a_tk, a_ap, b_tk, b_ap, op):
    f = C.nc.vector.tensor_tensor if eng == "dve" else C.nc.gpsimd.tensor_tensor
    C.K.ins(eng, [a_tk, b_tk], [out_tk], f, out=out_ap, in0=a_ap, in1=b_ap, op=op)


def stt(C, out_tk, out_ap, a_tk, a_ap, sc_tk, sc, b_tk, b_ap, op0, op1):
    rd = [a_tk, b_tk] + ([sc_tk] if sc_tk is not None else [])
    C.K.ins("dve", rd, [out_tk], C.nc.vector.scalar_tensor_tensor, out=out_ap, in0=a_ap, scalar=sc, in1=b_ap, op0=op0, op1=op1)


def act(C, out_tk, out_ap, in_tk, in_ap, func, extra=(), **kw):
    C.K.ins("act", [in_tk] + list(extra), [out_tk], C.nc.scalar.activation, out=out_ap, in_=in_ap, func=func, **kw)


def seq_tiles(last_layer, n=512):
    return tiles_of(n, last_layer)


def phase_P(C, li):
    K, nc, I = C.K, C.nc, C.I
    with contextlib.ExitStack() as st:
        uall = K.sb("p_u", [128, 34, 256], BF16, st)
        bands = K.sb("p_bands", [128, 20, 128], BF16, st)
        pw = K.sb("p_pw", [128, 2, 128], BF16, st)
        psc = K.sb("p_sc", [128, 2], F32, st)
        pooled = K.sb("p_pooled", [128, 2, 512], BF16, st)
        ost = K.sb("p_ost", [128, 2, 512], BF16, st)
        psg = [K.ps(f"p_psg{j}", [128, 512], F32, st) for j in range(4)]
        psy = [K.ps(f"p_psy{j}", [128, 512], F32, st) for j in range(2)]
        K.dma("sp", uall[:], C.u_d[:, :].rearrange("(j p) c -> p j c", p=128), [C.u_d], uall)
        K.dma("pool", bands[:], I["bands"][:, :, :], [I["bands"]], bands)
        K.dma("pool", pw[:], I["pw_bd"][li].rearrange("a p d -> p a d"), [I["pw_bd"]], pw)
        K.dma("sp", psc[:], I["pscale"][:, li, :], [I["pscale"]], psc)
        for (t0, n, s) in tiles_of(512):
            j0 = t0 // 128
            first, lastj = (0, 1) if s else (2, 33)
            for a in range(2):
                for gi in range(2):
                    g = 2 * a + gi
                    pg = psg[g]
                    for jj in range(n // 128):
                        j = j0 + jj
                        srcs = []
                        if j > first:
                            srcs.append((j - 1, 3))
                        srcs.append((j, 0 if j == first else (2 if j == lastj else 1)))
                        if j < lastj:
                            srcs.append((j + 1, 4))
                        for si, (sj, kind) in enumerate(srcs):
                            mm(C, pg, pg[:, jj * 128:(jj + 1) * 128], uall, uall[:, sj, a * 128:(a + 1) * 128], bands, bands[:, g * 5 + kind, :], start=(si == 0), stop=(si == len(srcs) - 1))
                    cp(C, "dve" if gi == 0 else "act", pooled, pooled[gi * 64:(gi + 1) * 64, a, :n], pg, pg[gi * 64:(gi + 1) * 64, :n])
                py = psy[a]
                mm(C, py, py[:, :n], pw, pw[:, a, :], pooled, pooled[:, a, :n])
                K.ins("dve", [py, psc], [ost], nc.vector.tensor_scalar, out=ost[:, a, :n], in0=py[:, :n], scalar1=psc[:, a:a + 1], scalar2=None, op0=ALU.mult)
            K.dma("sp", C.obr[0:256, t0:t0 + n].rearrange("(a p) t -> p a t", p=128), ost[:, :, :n], [ost], C.obr)


def rms_feat(C, x, ncn, n, gain, out, psS, rstd, wk, dim):
    K, nc = C.K, C.nc
    act(C, wk, wk[:, :ncn, :n], x, x[:, :ncn, :n], AF.Square)
    for c in range(ncn):
        mm(C, psS, psS[:, :n], C.ones, C.ones[:], wk, wk[:, c, :n], start=(c == 0), stop=(c == ncn - 1))
    act(C, rstd, rstd[:, :n], psS, psS[:, :n], AF.Ln, extra=[C.epsb], scale=1.0 / dim, bias=C.epsb[:, 0:1])
    act(C, rstd, rstd[:, :n], rstd, rstd[:, :n], AF.Exp, scale=-0.5)
    for c in range(ncn):
        stt(C, out, out[:, c, :n], x, x[:, c, :n], gain, gain[:, c:c + 1], rstd, rstd[:, :n], ALU.mult, ALU.mult)


def run_pool(factories, slots, extra=(), extra_steps=1):
    free = list(slots)
    active = []
    extra = list(extra)
    pend = list(factories)
    while pend or active or extra:
        while pend and free:
            sl = free.pop(0)
            active.append((pend.pop(0)(sl), sl))
        for item in list(active):
            try:
                next(item[0])
            except StopIteration:
                active.remove(item)
                free.append(item[1])
        for g in list(extra):
            try:
                for _ in range(extra_steps):
                    next(g)
            except StopIteration:
                extra.remove(g)


def phase_MLA(C, li):
    K, nc, I = C.K, C.nc, C.I
    last = li == 1
    SC = 96.0 ** -0.5
    with contextlib.ExitStack() as st:
        KT = K.sb("l_KT", [128, 8, T_ALL], BF16, st)
        KTh = [Tk(f"l_KT_h{h}", KT.t) for h in range(8)]
        V = K.sb("l_V", [128, 34, 8, 65], BF16, st)
        wuq = K.sb("l_wuq", [128, 3, 768], BF16, st)
        wk_ = K.sb("l_wukvk", [128, 2, 512], BF16, st)
        wv_ = K.sb("l_wukvv", [128, 2, 512], BF16, st)
        Cq = K.sb("l_Cq", [128, 128], F32, st)
        Prot = K.sb("l_Prot", [128, 128], BF16, st)
        gq = K.sb("l_gq", [128, 1], F32, st)
        gk = K.sb("l_gk", [128, 1], F32, st)
        qng = K.sb("l_qng", [128, 3], F32, st)
        kvng = K.sb("l_kvng", [128, 2], F32, st)
        load_w_bf16(C, wuq, lambda kc: I["w_uq"][li, kc * 128:(kc + 1) * 128, :], 3, I["w_uq"])
        load_w_bf16(C, wk_, lambda kc: I["w_ukv_k"][li, kc * 128:(kc + 1) * 128, :], 2, I["w_ukv_k"])
        load_w_bf16(C, wv_, lambda kc: I["w_ukv_v"][li, kc * 128:(kc + 1) * 128, :], 2, I["w_ukv_v"])
        K.dma("sp", Cq[:], I["Cq"][:, :], [I["Cq"]], Cq)
        K.dma("pool", Prot[:], I["Prot"][:, :], [I["Prot"]], Prot)
        gqf = K.sb("l_gqf", [128, 2], F32, st)
        gkf = K.sb("l_gkf", [128, 2], F32, st)
        K.dma("sp", gqf[:], I["gq96"][:, :], [I["gq96"]], gqf)
        K.dma("sp", gkf[:], I["gk96"][:, :], [I["gk96"]], gkf)
        cp(C, "dve", gq, gq[:], gqf, gqf[:, li:li + 1])
        cp(C, "dve", gk, gk[:], gkf, gkf[:, li:li + 1])
        K.dma("sp", qng[:], I["qng"][:, li, :], [I["qng"]], qng)
        K.dma("sp", kvng[:], I["kvng"][:, li, :], [I["kvng"]], kvng)
        xin = K.sb("l_xin", [128, 3, 512], F32, st)
        xkr = K.sb("l_xkr", [128, 512], F32, st)
        xn = K.sb("l_xn", [128, 3, 512], BF16, st)
        wk = K.sb("l_wk", [128, 3, 512], F32, st)
        rstd = K.sb("l_rstd", [128, 512], F32, st)
        cs = K.sb("l_cs", [128, 2, 512], F32, st)
        QTs = [K.sb(f"l_QT{j}", [128, 8, 512], BF16, st) for j in range(2)]
        QTh = [[Tk(f"l_QT{j}_h{h}", QTs[j].t) for h in range(8)] for j in range(2)]
        PT = [K.sb(f"l_PT{j}", [128, 512], BF16, st) for j in range(4)]
        rsb = K.sb("l_rsb", [128, 512], F32, st)
        bcs = K.sb("l_bcs", [128, 512], F32, st)
        ost = K.sb("l_ost", [128, 8, 512], BF16, st)
        psS = K.ps("l_psS", [128, 512], F32, st)
        psSc = [K.ps(f"l_psSc{j}", [128, 512], F32, st) for j in range(3)]
        psO = [K.ps(f"l_psO{j}", [128, 512], F32, st) for j in range(2)]
        psF = psS
        slots = []
        for j in range(2):
            slots.append(dict(
                raw=K.sb(f"l_raw{j}", [128, 512], F32, st), sq=K.sb(f"l_sq{j}", [128, 512], F32, st),
                r2=K.sb(f"l_r2{j}", [128, 512], F32, st), qn=K.sb(f"l_qn{j}", [128, 512], F32, st),
                qnb=K.sb(f"l_qnb{j}", [128, 512], BF16, st), tmp=K.sb(f"l_tmp{j}", [128, 512], F32, st),
                ps=K.ps(f"l_psH{j}", [128, 512], F32, st)))
        K.ins("dve", [], [V], nc.vector.memset, V[:, :, :, 64:65], 1.0)
        R96 = slice(64, 96)

        def head_norm_g(S_, rows, src_tk, src_ap, gain, n, lhs_ap):
            sq, r2, qn, ps = S_["sq"], S_["r2"], S_["qn"], S_["ps"]
            tt(C, "pool", sq, sq[rows, :n], src_tk, src_ap, src_tk, src_ap, ALU.mult)
            yield
            mm(C, ps, ps[0:96, :n], Cq, lhs_ap, sq, sq[rows, :n])
            yield
            act(C, r2, r2[rows, :n], ps, ps[rows, :n], AF.Ln, extra=[C.epsb], scale=1.0, bias=C.epsb[rows, 0:1])
            yield
            act(C, r2, r2[rows, :n], r2, r2[rows, :n], AF.Exp, scale=-0.5)
            yield
            stt(C, qn, qn[rows, :n], src_tk, src_ap, gain, gain[rows, 0:1], r2, r2[rows, :n], ALU.mult, ALU.mult)
            yield

        def rope_g(S_, rows, n, out_tk, out_ap):
            qn, qnb, tmp, ps = S_["qn"], S_["qnb"], S_["tmp"], S_["ps"]
            cp(C, "dve", qnb, qnb[rows, :n], qn, qn[rows, :n])
            yield
            mm(C, ps, ps[0:96, :n], Prot, Prot[rows, 0:96], qnb, qnb[rows, :n])
            yield
            tt(C, "dve", tmp, tmp[rows, :n], ps, ps[rows, :n], cs, cs[rows, 1, :n], ALU.mult)
            tt(C, "pool", qn, qn[rows, :n], qn, qn[rows, :n], cs, cs[rows, 0, :n], ALU.mult)
            yield
            tt(C, "dve", out_tk, out_ap, qn, qn[rows, :n], tmp, tmp[rows, :n], ALU.add)
            yield

        def k_head_f(h, t0, n):
            def g(S_):
                ps, raw, qn = S_["ps"], S_["raw"], S_["qn"]
                for kc in range(2):
                    mm(C, ps, ps[0:64, :n], wk_, wk_[:, kc, h * 64:(h + 1) * 64], xn, xn[:, kc, :n], start=(kc == 0), stop=(kc == 1))
                yield
                cp(C, "dve", raw, raw[0:64, :n], ps, ps[0:64, :n])
                yield
                yield from head_norm_g(S_, slice(0, 64), raw, raw[0:64, :n], gk, n, Cq[0:64, 0:96])
                cp(C, "pool", KTh[h], KT[0:64, h, t0:t0 + n], qn, qn[0:64, :n])
                yield
            return g

        def k_kr_f(t0, n, s):
            def g(S_):
                qn = S_["qn"]
                yield from head_norm_g(S_, R96, xkr, xkr[R96, :n], gk, n, Cq[64:96, 0:96])
                if not s:
                    yield from rope_g(S_, R96, n, qn, qn[R96, :n])
                K.ins("dve", [qn], KTh, nc.vector.tensor_copy, out=KT[64:96, :, t0:t0 + n], in_=qn[64:96, :n].unsqueeze(1).broadcast_to([32, 8, n]))
                yield
            return g

        for (t0, n, s) in tiles_of(512):
            K.dma("sp", xin[:, 0:2, :n], C.zT[3:5, :, t0:t0 + n].rearrange("m p t -> p m t"), [C.zT], xin)
            K.dma("sp", xkr[64:96, :n], C.zT[5, 0:32, t0:t0 + n], [C.zT], xkr)
            if not s:
                K.dma("sp", cs[64:96, :, :n], I["cs"][:, :, t0 - LC:t0 - LC + n], [I["cs"]], cs)
            rms_feat(C, xin, 2, n, kvng, xn, psS, rstd, wk, 256.0)
            for jj in range(n // 128):
                pa = psSc[jj % 2]
                for kc in range(2):
                    mm(C, pa, pa[:, :], xn, xn[:, kc, jj * 128:(jj + 1) * 128], wv_, wv_[:, kc, :], start=(kc == 0), stop=(kc == 1))
                cp(C, "act", V, V[:, t0 // 128 + jj, :, 0:64], pa, pa[:, :].rearrange("p (h d) -> p h d", d=64))
            run_pool([k_kr_f(t0, n, s)] + [k_head_f(h, t0, n) for h in range(8)], slots)

        def q_head_f(h, n, s, qb):
            def g(S_):
                ps, raw, qn = S_["ps"], S_["raw"], S_["qn"]
                QTt, QT = QTh[qb][h], QTs[qb]
                for kc in range(3):
                    mm(C, ps, ps[0:96, :n], wuq, wuq[:, kc, h * 96:(h + 1) * 96], xn, xn[:, kc, :n], start=(kc == 0), stop=(kc == 2))
                yield
                cp(C, "dve", raw, raw[0:96, :n], ps, ps[0:96, :n])
                yield
                yield from head_norm_g(S_, slice(0, 96), raw, raw[0:96, :n], gq, n, Cq[0:96, 0:96])
                cp(C, "pool", QTt, QT[0:64, h, :n], qn, qn[0:64, :n])
                if not s:
                    yield from rope_g(S_, R96, n, QTt, QT[R96, h, :n])
                else:
                    cp(C, "pool", QTt, QT[R96, h, :n], qn, qn[R96, :n])
                yield
            return g

        def q_prep(t0, n, s, qb):
            K.dma("sp", xin[:, 0:3, :n], C.zT[0:3, :, t0:t0 + n].rearrange("m p t -> p m t"), [C.zT], xin)
            if not s:
                K.dma("sp", cs[64:96, :, :n], I["cs"][:, :, t0 - LC:t0 - LC + n], [I["cs"]], cs)
            rms_feat(C, xin, 3, n, qng, xn, psS, rstd, wk, 384.0)
            return [q_head_f(h, n, s, qb) for h in range(8)]

        cnt = {"npt": 0}

        def attn_g(t0, n, s, qb):
            QT = QTs[qb]
            kts = [0, 1] if s else list(range(34))
            steps = [(h, ki, kt) for h in range(8) for ki, kt in enumerate(kts)]
            slot = {}

            def emit_S(i):
                h, ki, kt = steps[i]
                psc_ = psSc[cnt["npt"] % 3]
                pt = PT[cnt["npt"] % 4]
                cnt["npt"] += 1
                slot[i] = pt
                mm(C, psc_, psc_[:, :n], KTh[h], KT[0:96, h, kt * 128:(kt + 1) * 128], QTh[qb][h], QT[0:96, h, :n])
                act(C, pt, pt[:, :n], psc_, psc_[:, :n], AF.Exp, scale=SC)

            emit_S(0)
            if len(steps) > 1:
                emit_S(1)
            for i, (h, ki, kt) in enumerate(steps):
                if i + 2 < len(steps):
                    emit_S(i + 2)
                po = psO[h % 2]
                pt = slot.pop(i)
                mm(C, po, po[0:65, :n], V, V[:, kt, h, :], pt, pt[:, :n], start=(ki == 0), stop=(ki == len(kts) - 1))
                if ki == len(kts) - 1:
                    K.ins("dve", [po], [rsb], nc.vector.reciprocal, out=rsb[64:65, :n], in_=po[64:65, :n])
                    mm(C, psF, psF[0:64, :n], C.ones, C.ones[64:65, 0:64], rsb, rsb[64:65, :n])
                    cp(C, "dve", bcs, bcs[0:64, :n], psF, psF[0:64, :n])
                    tt(C, "dve", ost, ost[0:64, h, :n], po, po[0:64, :n], bcs, bcs[0:64, :n], ALU.mult)
                yield
            K.dma("sp", C.obr[256:768, t0:t0 + n].rearrange("(h p) t -> p h t", p=64), ost[0:64, :, :n], [ost], C.obr)

        qtiles = tiles_of(512, last)
        run_pool(q_prep(*qtiles[0], 0), slots)
        for i, (t0, n, s) in enumerate(qtiles):
            nxt = q_prep(*qtiles[i + 1], (i + 1) % 2) if i + 1 < len(qtiles) else []
            run_pool(nxt, slots, extra=[attn_g(t0, n, s, i % 2)], extra_steps=4)


def phase_R(C, li):
    K, nc, I = C.K, C.nc, C.I
    last = li == 1
    NCH = 34
    TN = 256
    MAXFLY = min(getattr(C, "maxfly", 6), 6)
    with contextlib.ExitStack() as st:
        C.dbgtk = getattr(C, "dbgtk", {})

        def sbt(name, shape, dt=F32):
            tk = K.sb(f"rw{li}_" + name, shape, dt, st)
            C.dbgtk[f"rw{li}_" + name] = tk
            return tk
        mu = sbt("mu", [128, 9, 2]); omm = sbt("omm", [128, 9])
        kkp = sbt("kkp", [128, 2]); w0 = sbt("w0", [128, 2, 2]); a0 = sbt("a0", [128, 2, 2]); ka = sbt("ka", [128, 2, 2]); omka = sbt("omka", [128, 2, 2])
        rk = sbt("rk", [128, 2]); w2 = sbt("w2", [128, 2, 256], BF16); a2 = sbt("a2", [128, 2, 256], BF16); zb7 = sbt("zb7", [128, TN], BF16)
        g2 = sbt("g2", [128, 256], BF16); lnw = sbt("lnw", [128, 256]); lnb = sbt("lnb", [128, 256])
        blk = sbt("blk", [128, 128]); ind2 = sbt("ind2", [128, 2]); masks = sbt("masks", [128, 4, 128]); rmask = sbt("rmask", [128, 512])
        lvm = sbt("lvm", [128, 14, 128], BF16)
        lvm2 = sbt("lvm2", [128, 14, 128], BF16)
        K.dma("sp", mu[:], I["r_mu"][:, li, :, :], [I["r_mu"]], mu)
        K.dma("sp", kkp[:], I["r_kk"][:, li, :], [I["r_kk"]], kkp)
        K.dma("sp", w0[:], I["r_w0"][:, li, :, :], [I["r_w0"]], w0)
        K.dma("sp", a0[:], I["r_a0"][:, li, :, :], [I["r_a0"]], a0)
        K.dma("sp", ka[:], I["r_ka"][:, li, :, :], [I["r_ka"]], ka)
        K.dma("sp", rk[:], I["r_rk"][:, li, :], [I["r_rk"]], rk)
        for d_ in range(2):
            K.dma("pool", w2[:, d_, :], I["r_w2"][li, d_], [I["r_w2"]], w2)
            K.dma("pool", a2[:, d_, :], I["r_a2"][li, d_], [I["r_a2"]], a2)
        K.dma("pool", g2[:], I["r_g2"][li], [I["r_g2"]], g2)
        K.dma("pool", lvm[:], I["lvmask"][:, :, :], [I["lvmask"]], lvm)
        K.dma("pool", lvm2[:, 0:7, :], I["lvmask"][:, 7:14, :], [I["lvmask"]], lvm2)
        K.dma("pool", lvm2[:, 7:14, :], I["lvmask"][:, 0:7, :], [I["lvmask"]], lvm2)
        K.dma("sp", lnw[:], I["r_lnwb"][li], [I["r_lnwb"]], lnw)
        K.dma("sp", lnb[:], I["r_lnbb"][li], [I["r_lnbb"]], lnb)
        K.dma("sp", blk[:], I["blk64"][:, :], [I["blk64"]], blk)
        K.dma("sp", ind2[:], I["ind2"][:, :], [I["ind2"]], ind2)
        K.dma("sp", masks[:], I["masks"][:, :, :], [I["masks"]], masks)
        K.dma("sp", rmask[:], I["rmask"][:, :], [I["rmask"]], rmask)
        tt(C, "dve", omm, omm[:], mu, mu[:, :, 0], mu, mu[:, :, 1], ALU.add)
        K.ins("dve", [omm], [omm], nc.vector.tensor_scalar, out=omm[:], in0=omm[:], scalar1=-1.0, scalar2=1.0, op0=ALU.mult, op1=ALU.add)
        K.ins("dve", [ka], [omka], nc.vector.tensor_scalar, out=omka[:], in0=ka[:], scalar1=-1.0, scalar2=1.0, op0=ALU.mult, op1=ALU.add)
        Yacc = sbt("Yacc", [128, NCH, 256]); Vtok = sbt("Vtok", [128, NCH, 256], BF16)
        VtokT = [Tk(f"Vtok_t{j}", Vtok.t) for j in range(NCH // 2)]
        bon = sbt("bon", [128, NCH, 4]); gst = sbt("gst", [128, 2, 256], BF16)
        S_ = [sbt(f"S{hp}", [128, 64]) for hp in range(2)]
        Sb_ = [sbt(f"Sb{hp}", [128, 64], BF16) for hp in range(2)]
        zr = sbt("zr", [128, 9, TN + 2]); zs = sbt("zs", [128, 9, TN])
        kkn = sbt("kkn", [128, 2, TN]); e1 = sbt("e1", [128, 2, TN]); e2 = sbt("e2", [128, 2, TN])
        xw = sbt("xw", [128, 2, TN]); aa = sbt("aa", [128, 2, TN]); kd = [sbt(f"kd{d}", [128, 2, TN]) for d in range(2)]
        bb = sbt("bb", [128, 2, TN]); cum = sbt("cum", [128, 2, TN]); tot = sbt("tot", [128, 2, 2])
        th = sbt("th", [128, TN], BF16); sgd = sbt("sgd", [128, TN], BF16)
        opsets = []
        for j in range(3):
            o = {nm: sbt(f"o{j}_" + nm, [128, 2, TN], BF16) for nm in ("at", "rt", "bt", "kt", "bh", "kh")}
            o["WC"] = sbt(f"o{j}_WC", [128, 2, 2])
            opsets.append(o)
        banks = [K.ps(f"r_bank{j}", [128, 512], F32, st) for j in range(8)]
        bsets = []
        for j in range(MAXFLY):
            b = dict(
                AT=sbt(f"u{j}_AT", [128, 2, 512], BF16), Lc=[sbt(f"u{j}_Lc{q}", [128, 2, 2, 128], BF16) for q in range(2)],
                QT=[sbt(f"u{j}_QT{q}", [128, 2, 2, 128], BF16) for q in range(2)], ZZ=sbt(f"u{j}_ZZ", [128, 2, 2, 128], BF16),
                P=sbt(f"u{j}_P", [128, 2, 128], BF16), tok3=sbt(f"u{j}_tok3", [128, 3, 128], BF16), rGH=sbt(f"u{j}_rGH", [128, 2, 128], BF16),
                NX=sbt(f"u{j}_NX", [128, 512], BF16), GHs=sbt(f"u{j}_GHs", [128, 2, 128], BF16), GT=sbt(f"u{j}_GT", [128, 128], BF16), Us=sbt(f"u{j}_Us", [128, 2, 64], BF16),
                ps=(banks[2 + j], banks[2 + j]))
            bsets.append(b)
        psW = [banks[0], banks[1]]
        psQ = [banks[0], banks[1]]
        ytmp = sbt("ytmp", [128, 256]); otok = sbt("otok", [128, 256]); st4 = sbt("st4", [128, 4]); st4b = sbt("st4b", [128, 4])
        obst = sbt("obst", [128, 2, 128], BF16); gl = sbt("gl", [128, 256], BF16)

        def prep(t0, n, s, d, first_pass, oset):
            s0, s1 = (0, LC) if s else (LC, T_ALL)
            lo, hi = max(t0 - 1, s0), min(t0 + n + 1, s1)
            if lo == t0 or hi == t0 + n:
                K.ins("pool", [], [zr], nc.gpsimd.memset, zr[:, :, :], 0.0)
            off = lo - (t0 - 1)
            K.dma("sp", zr[:, :, off:off + hi - lo], C.zT[6:15, :, lo:hi].rearrange("m p t -> p m t"), [C.zT], zr)
            yield
            for c in range(9):
                K.ins("dve", [zr, omm], [zs], nc.vector.tensor_scalar, out=zs[:, c, :n], in0=zr[:, c, 1:n + 1], scalar1=omm[:, c:c + 1], scalar2=None, op0=ALU.mult)
                stt(C, zs, zs[:, c, :n], zr, zr[:, c, 0:n], mu, mu[:, c, 0:1], zs, zs[:, c, :n], ALU.mult, ALU.add)
                stt(C, zs, zs[:, c, :n], zr, zr[:, c, 2:n + 2], mu, mu[:, c, 1:2], zs, zs[:, c, :n], ALU.mult, ALU.add)
                if c % 3 == 2:
                    yield
            r_, k_ = zs[:, 0:2, :n], zs[:, 2:4, :n]
            for hp in range(2):
                K.ins("dve", [zs, kkp], [kkn], nc.vector.tensor_scalar, out=kkn[:, hp, :n], in0=zs[:, 2 + hp, :n], scalar1=kkp[:, hp:hp + 1], scalar2=None, op0=ALU.mult)
            tt(C, "pool", e1, e1[:, :, :n], kkn, kkn[:, :, :n], kkn, kkn[:, :, :n], ALU.mult)
            yield
            for hp in range(2):
                pw_ = psW[hp]
                mm(C, pw_, pw_[:, :n], blk, blk[:], e1, e1[:, hp, :n])
                K.ins("dve", [pw_], [e2], nc.vector.tensor_scalar, out=e2[:, hp, :n], in0=pw_[:, :n], scalar1=1e-24, scalar2=None, op0=ALU.max)
            act(C, e2, e2[:, :, :n], e2, e2[:, :, :n], AF.Ln)
            yield
            act(C, e2, e2[:, :, :n], e2, e2[:, :, :n], AF.Exp, scale=-0.5)
            tt(C, "dve", kkn, kkn[:, :, :n], kkn, kkn[:, :, :n], e2, e2[:, :, :n], ALU.mult)
            yield
            dirs = (0, 1) if first_pass else (d,)
            cp(C, "act", zb7, zb7[:, :n], zs, zs[:, 7, :n])
            yield
            for dd in dirs:
                for hp in range(2):
                    pw_ = psW[hp]
                    mm(C, pw_, pw_[:, :n], a2, a2[:, dd, hp * 128:(hp + 1) * 128], zb7, zb7[:, :n])
                    K.ins("dve", [pw_, a0], [aa], nc.vector.tensor_scalar, out=aa[:, hp, :n], in0=pw_[:, :n], scalar1=a0[:, dd, hp:hp + 1], scalar2=None, op0=ALU.add)
                    act(C, aa, aa[:, hp, :n], aa, aa[:, hp, :n], AF.Sigmoid)
                    K.ins("dve", [aa, ka], [e1], nc.vector.tensor_scalar, out=e1[:, hp, :n], in0=aa[:, hp, :n], scalar1=ka[:, dd, hp:hp + 1], scalar2=None, op0=ALU.mult)
                    K.ins("dve", [e1, omka], [e1], nc.vector.tensor_scalar, out=e1[:, hp, :n], in0=e1[:, hp, :n], scalar1=omka[:, dd, hp:hp + 1], scalar2=None, op0=ALU.add)
                    yield
                tt(C, "dve", kd[dd], kd[dd][:, :, :n], e1, e1[:, :, :n], zs, k_, ALU.mult)
                if dd == d:
                    tt(C, "pool", bb, bb[:, :, :n], kkn, kkn[:, :, :n], aa, aa[:, :, :n], ALU.mult)
            if first_pass:
                tidx = t0 // TN
                tt(C, "pool", e1, e1[:, :, :n], kd[0], kd[0][:, :, :n], kd[1], kd[1][:, :, :n], ALU.add)
                tt(C, "dve", e1, e1[:, :, :n], e1, e1[:, :, :n], zs, r_, ALU.mult)
                for hp in range(2):
                    K.ins("dve", [e1, rk], [e1], nc.vector.tensor_scalar, out=e1[:, hp, :n], in0=e1[:, hp, :n], scalar1=rk[:, hp:hp + 1], scalar2=0.5, op0=ALU.mult, op1=ALU.mult)
                act(C, sgd, sgd[:, :n], zs, zs[:, 8, :n], AF.Sigmoid)
                yield
                for jj in range(n // 128):
                    cidx = t0 // 128 + jj
                    pq = psQ[jj % 2]
                    for hp in range(2):
                        mm(C, pq, pq[:, 256 + 2 * hp:258 + 2 * hp], e1, e1[:, hp, jj * 128:(jj + 1) * 128], ind2, ind2[:, :])
                    mm(C, pq, pq[:, 0:256], sgd, sgd[:, jj * 128:(jj + 1) * 128], g2, g2[:, :])
                    cp(C, "act", gst, gst[:, jj, :], pq, pq[:, 0:256])
                    cp(C, "dve", bon, bon[:, cidx, :], pq, pq[:, 256:260])
                    pv = psW[jj % 2]
                    for hp in range(2):
                        K.ins("pe", [zs, C.ident], [pv], nc.tensor.transpose, out=pv[:, hp * 128:(hp + 1) * 128], in_=zs[:, 4 + hp, jj * 128:(jj + 1) * 128], identity=C.ident[:])
                    cp(C, "act", VtokT[tidx], Vtok[:, cidx, :], pv, pv[:, 0:256])
                    yield
                K.dma("sp", C.gtok_d[t0:t0 + n, :].rearrange("(j p) c -> p j c", p=128), gst[:, :n // 128, :], [gst], C.gtok_d)
            act(C, th, th[:, :n], zs, zs[:, 6, :n], AF.Tanh)
            yield
            for hp in range(2):
                pw_ = psW[hp]
                mm(C, pw_, pw_[:, :n], w2, w2[:, d, hp * 128:(hp + 1) * 128], th, th[:, :n])
                K.ins("dve", [pw_, w0], [xw], nc.vector.tensor_scalar, out=xw[:, hp, :n], in0=pw_[:, :n], scalar1=w0[:, d, hp:hp + 1], scalar2=None, op0=ALU.add)
                act(C, xw, xw[:, hp, :n], xw, xw[:, hp, :n], AF.Sigmoid)
                yield
            K.ins("dve", [xw], [xw], nc.vector.tensor_scalar, out=xw[:, :, :n], in0=xw[:, :, :n], scalar1=-0.6065306597126334, scalar2=None, op0=ALU.mult)
            yield
            nchk = n // 128
            for hp in range(2):
                K.ins("dve", [rmask, xw], [cum], nc.vector.tensor_tensor_scan, out=cum[:, hp, :n], data0=rmask[:, :n], data1=xw[:, hp, :n], initial=0.0, op0=ALU.mult, op1=ALU.add)
            cum4 = cum[:, :, :n].rearrange("p h (c t) -> p h c t", t=128)
            cp(C, "dve", tot, tot[:, :, :nchk], cum, cum4[:, :, :, 127])
            yield
            totb = tot[:, :, :nchk].unsqueeze(3).broadcast_to([128, 2, nchk, 128])
            v4 = lambda tk: tk[:, :, :n].rearrange("p h (c t) -> p h c t", t=128)
            if d == 0:
                tt(C, "pool", e2, e2[:, :, :n], cum, cum[:, :, :n], xw, xw[:, :, :n], ALU.subtract)
                tt(C, "dve", e1, v4(e1), tot, totb, cum, cum4, ALU.subtract)
            else:
                tt(C, "dve", e2, v4(e2), tot, totb, cum, cum4, ALU.subtract)
                tt(C, "pool", e1, e1[:, :, :n], cum, cum[:, :, :n], xw, xw[:, :, :n], ALU.subtract)
                tt(C, "dve", cum, cum[:, :, :n], e2, e2[:, :, :n], xw, xw[:, :, :n], ALU.add)
            ci = cum[:, :, :n]
            WC = oset["WC"]
            act(C, WC, WC[:, :, :nchk], tot, tot[:, :, :nchk], AF.Exp)
            yield
            act(C, e2, e2[:, :, :n], e2, e2[:, :, :n], AF.Exp)
            stt(C, oset["at"], oset["at"][:, :, :n], kkn, kkn[:, :, :n], None, -1.0, e2, e2[:, :, :n], ALU.mult, ALU.mult)
            yield
            act(C, e2, e2[:, :, :n], cum, ci, AF.Exp)
            tt(C, "dve", oset["rt"], oset["rt"][:, :, :n], zs, r_, e2, e2[:, :, :n], ALU.mult)
            yield
            act(C, e2, e2[:, :, :n], cum, ci, AF.Exp, scale=-1.0)
            tt(C, "dve", oset["bt"], oset["bt"][:, :, :n], bb, bb[:, :, :n], e2, e2[:, :, :n], ALU.mult)
            tt(C, "pool", oset["kt"], oset["kt"][:, :, :n], kd[d], kd[d][:, :, :n], e2, e2[:, :, :n], ALU.mult)
            yield
            act(C, e1, e1[:, :, :n], e1, e1[:, :, :n], AF.Exp)
            tt(C, "dve", oset["bh"], oset["bh"][:, :, :n], bb, bb[:, :, :n], e1, e1[:, :, :n], ALU.mult)
            tt(C, "pool", oset["kh"], oset["kh"][:, :, :n], kd[d], kd[d][:, :, :n], e1, e1[:, :, :n], ALU.mult)

        done = {0: 0, 1: 0}

        free_sets = list(range(MAXFLY))

        def unit(cidx, jj, d, hp, emit_y, first_pass, oset, B, myn):
            if getattr(C, "rdbg", 9) < 2:
                done[hp] += 1
                return
            bidx = free_sets.pop(0)
            B = bsets[bidx]
            T_ = slice(jj * 128, (jj + 1) * 128)
            mS = 0 if d == 0 else 2
            at, rt, bt, kt, bh, kh, WC = (oset[nm] for nm in ("at", "rt", "bt", "kt", "bh", "kh", "WC"))
            AT, Lcs, QTs, ZZ, P, tok3, rGH, GHs, GT, Us = (B[nm] for nm in ("AT", "Lc", "QT", "ZZ", "P", "tok3", "rGH", "GHs", "GT", "Us"))
            pA, pB = B["ps"]
            if getattr(C, "onebank", False):
                pB = pA
            S, Sb = S_[hp], Sb_[hp]
            Vt = VtokT[cidx // 2]
            vsl = lambda hh: Vtok[:, cidx, hp * 128 + hh * 64: hp * 128 + hh * 64 + 64]
            mview = masks[:, mS:mS + 2, :].unsqueeze(1).broadcast_to([128, 2, 2, 128])
            for hh in range(2):
                R = slice(hh * 64, hh * 64 + 64)
                pa = pA if hh == 0 else pB
                mm(C, pa, pa[:, 0:128], bt, bt[R, hp, T_], at, at[R, hp, T_])
                mm(C, pa, pa[:, 128:256], bt, bt[R, hp, T_], rt, rt[R, hp, T_])
                mm(C, pa, pa[:, 256:384], kt, kt[R, hp, T_], at, at[R, hp, T_])
                mm(C, pa, pa[:, 384:512], kt, kt[R, hp, T_], rt, rt[R, hp, T_])
                tt(C, "dve", AT, AT[:, hh, :].rearrange("p (a m t) -> p a m t", a=2, m=2), pa, pa[:, :].rearrange("p (a m t) -> p a m t", a=2, m=2), masks, mview, ALU.mult)
            yield
            pn = pA
            for hh in range(2):
                R = slice(hh * 64, hh * 64 + 64)
                mm(C, pn, pn[:, hh * 128:(hh + 1) * 128], at, at[R, hp, T_], bt, bt[R, hp, T_])
                mm(C, pn, pn[:, 256 + hh * 128:384 + hh * 128], bt, bt[R, hp, T_], at, at[R, hp, T_])
            NX = B["NX"]
            cp(C, "act", NX, NX[:, :], pn, pn[:, :])
            yield
            lvD = lvm if d == 0 else lvm2
            nx4 = NX[:, :].rearrange("p (a h t) -> p a h t", a=2, h=2)
            lvl_no = [0]

            def make_L(lvl):
                Lc = Lcs[lvl % 2]
                mk = lvD[:, :, :].rearrange("p (a l) t -> p a l t", a=2)[:, :, lvl, :].unsqueeze(2).broadcast_to([128, 2, 2, 128])
                tt(C, "dve" if lvl % 2 == 0 else "pool", Lc, Lc[:, :, :, :], NX, nx4, lvD, mk, ALU.mult)
                return Lc

            idb = C.identb[:].unsqueeze(1).broadcast_to([128, 2, 128])
            Lc = make_L(0)
            QT = QTs[0]
            tt(C, "pool", QT, QT[:, :, 0, :], Lc, Lc[:, 1, :, :], C.identb, idb, ALU.add)
            tt(C, "pool", QT, QT[:, :, 1, :], Lc, Lc[:, 0, :, :], C.identb, idb, ALU.add)
            Lc = make_L(1)
            yield
            curq = 0
            pz, pq_ = pB, pA
            for lvl in range(1, 6):
                QT, QTn = QTs[curq], QTs[curq ^ 1]
                for hh in range(2):
                    mm(C, pz, pz[:, (hh * 2) * 128:(hh * 2 + 1) * 128], Lc, Lc[:, 0, hh, :], QT, QT[:, hh, 0, :])
                    mm(C, pz, pz[:, (hh * 2 + 1) * 128:(hh * 2 + 2) * 128], Lc, Lc[:, 1, hh, :], QT, QT[:, hh, 1, :])
                cp(C, "act", ZZ, ZZ[:, :, :, :], pz, pz[:, :].rearrange("p (h a t) -> p h a t", h=2, a=2))
                Lc = make_L(lvl + 1)
                yield
                for hh in range(2):
                    o0 = pq_[:, (hh * 2) * 128:(hh * 2 + 1) * 128]
                    o1 = pq_[:, (hh * 2 + 1) * 128:(hh * 2 + 2) * 128]
                    mm(C, pq_, o0, QT, QT[:, hh, 1, :], ZZ, ZZ[:, hh, 0, :], start=True, stop=False)
                    mm(C, pq_, o0, C.identb, C.identb[:], QT, QT[:, hh, 0, :], start=False, stop=True)
                    mm(C, pq_, o1, QT, QT[:, hh, 0, :], ZZ, ZZ[:, hh, 1, :], start=True, stop=False)
                    mm(C, pq_, o1, C.identb, C.identb[:], QT, QT[:, hh, 1, :], start=False, stop=True)
                cp(C, "act", QTn, QTn[:, :, :, :], pq_, pq_[:, :].rearrange("p (h a t) -> p h a t", h=2, a=2))
                curq ^= 1
                yield
            QT = QTs[curq]
            for hh in range(2):
                mm(C, pz, pz[:, hh * 128:(hh + 1) * 128], Lc, Lc[:, 0, hh, :], QT, QT[:, hh, 0, :])
            cp(C, "act", ZZ, ZZ[:, :, 0, :], pz, pz[:, 0:256].rearrange("p (h t) -> p h t", h=2))
            yield
            for hh in range(2):
                o0 = pq_[:, hh * 128:(hh + 1) * 128]
                mm(C, pq_, o0, QT, QT[:, hh, 1, :], ZZ, ZZ[:, hh, 0, :], start=True, stop=False)
                mm(C, pq_, o0, C.identb, C.identb[:], QT, QT[:, hh, 0, :], start=False, stop=True)
            cp(C, "act", P, P[:, :, :], pq_, pq_[:, 0:256].rearrange("p (h t) -> p h t", h=2))
            yield
            pt_ = pB
            for i3, src in enumerate((at, bh, kh)):
                mm(C, pt_, pt_[:, i3 * 128:(i3 + 1) * 128], src, src[:, hp, T_], C.identb, C.identb[:])
            cp(C, "act", tok3, tok3[:, :, :], pt_, pt_[:, 0:384].rearrange("p (i f) -> p i f", i=3))
            yield
            pzz = pA
            for hh in range(2):
                mm(C, pzz, pzz[:, hh * 64:(hh + 1) * 64], AT, AT[:, hh, 256:384], Vt, vsl(hh))
            for hh in range(2):
                cp(C, "pool", rGH, rGH[:, hh, 0:64], tok3, tok3[:, 0, hh * 64:(hh + 1) * 64])
            cp(C, "dve", rGH, rGH[:, :, 64:128], pzz, pzz[:, 0:128].rearrange("p (h v) -> p h v", h=2))
            yield
            pg = pB
            for hh in range(2):
                mm(C, pg, pg[:, hh * 128:(hh + 1) * 128], P, P[:, hh, :], rGH, rGH[:, hh, :])
                mm(C, pg, pg[hh * 64:(hh + 1) * 64, 256:384], tok3, tok3[:, 0, hh * 64:(hh + 1) * 64], P, P[:, hh, :])
            cp(C, "act", GHs, GHs[:, :, :], pg, pg[:, 0:256].rearrange("p (h f) -> p h f", h=2))
            cp(C, "dve", GT, GT[:, :], pg, pg[:, 256:384])
            yield
            while done[hp] < myn:
                yield
            pu = pA
            for hh in range(2):
                R = slice(hh * 64, hh * 64 + 64)
                mm(C, pu, pu[:, hh * 64:(hh + 1) * 64], C.identb, C.identb[:], GHs, GHs[:, hh, 64:128], start=True, stop=False)
                mm(C, pu, pu[:, hh * 64:(hh + 1) * 64], GT, GT[R, :], Sb, Sb[R, :], start=False, stop=True)
            cp(C, "act", Us, Us[:, :, :], pu, pu[:, 0:128].rearrange("p (h v) -> p h v", h=2))
            yield
            if emit_y:
                py = pB
                for hh in range(2):
                    R = slice(hh * 64, hh * 64 + 64)
                    o_ = py[:, hh * 64:(hh + 1) * 64]
                    mm(C, py, o_, rt, rt[R, hp, T_], Sb, Sb[R, :], start=True, stop=False)
                    mm(C, py, o_, AT, AT[:, hh, 128:256], Us, Us[:, hh, :], start=False, stop=False)
                    mm(C, py, o_, AT, AT[:, hh, 384:512], Vt, vsl(hh), start=False, stop=True)
                if first_pass:
                    cp(C, "dve", Yacc, Yacc[:, cidx, hp * 128:(hp + 1) * 128], py, py[:, 0:128])
                else:
                    tt(C, "dve", Yacc, Yacc[:, cidx, hp * 128:(hp + 1) * 128], py, py[:, 0:128], Yacc, Yacc[:, cidx, hp * 128:(hp + 1) * 128], ALU.add)
            ps_ = pA
            for hh in range(2):
                o_ = ps_[hh * 64:(hh + 1) * 64, 256:320]
                mm(C, ps_, o_, tok3, tok3[:, 1, hh * 64:(hh + 1) * 64], Us, Us[:, hh, :], start=True, stop=False)
                mm(C, ps_, o_, tok3, tok3[:, 2, hh * 64:(hh + 1) * 64], Vt, vsl(hh), start=False, stop=True)
            stt(C, S, S[:, :], S, S[:, :], WC, WC[:, hp, jj:jj + 1], ps_, ps_[:, 256:320], ALU.mult, ALU.add)
            cp(C, "act", Sb, Sb[:, :], S, S[:, :])
            done[hp] += 1
            free_sets.append(bidx)
            yield

        PSTEPS = 2

        def run_pass(d):
            tl = tiles_of(TN)
            order = tl if d == 0 else [tl[0]] + tl[:0:-1]
            done[0] = done[1] = 0
            seq = {0: 0, 1: 0}
            active = []
            state = {"prep": None}

            def step_all():
                for g in list(active):
                    try:
                        next(g)
                    except StopIteration:
                        active.remove(g)
                if state["prep"] is not None:
                    try:
                        for _ in range(PSTEPS):
                            next(state["prep"])
                    except StopIteration:
                        state["prep"] = None

            def mk_prep(ti):
                t0, n, s = order[ti]
                return prep(t0, n, s, d, d == 0, opsets[ti % 3])

            for _ in mk_prep(0):
                pass
            for ti, (t0, n, s) in enumerate(order):
                if state["prep"] is not None:
                    for _ in state["prep"]:
                        pass
                state["prep"] = mk_prep(ti + 1) if ti + 1 < len(order) else None
                oset = opsets[ti % 3]
                jjs = list(range(n // 128))
                if d == 1:
                    jjs = jjs[::-1]
                for jj in jjs:
                    for hp in range(2):
                        active.append(unit(t0 // 128 + jj, jj, d, hp, not (last and s), d == 0, oset, None, seq[hp]))
                        seq[hp] += 1
                        while len(active) >= MAXFLY:
                            step_all()
            while active:
                step_all()

        for d in range(2):
            for hp in range(2):
                K.ins("dve", [], [S_[hp]], nc.vector.memset, S_[hp][:], 0.0)
                K.ins("dve", [], [Sb_[hp]], nc.vector.memset, Sb_[hp][:], 0.0)
            run_pass(d)
        for cidx in range(2 if last else 0, NCH if getattr(C, "rdbg", 9) >= 8 else 0):
            K.dma("sp", gl[:, :], C.gtok_d[cidx * 128:(cidx + 1) * 128, :], [C.gtok_d], gl)
            y4 = Yacc[:, cidx, :].rearrange("p (h v) -> p h v", h=4)
            K.ins("dve", [Yacc], [st4], nc.vector.tensor_reduce, out=st4[:, :], in_=y4, axis=AX.X, op=ALU.add)
            K.ins("dve", [st4], [st4], nc.vector.tensor_scalar, out=st4[:, :], in0=st4[:, :], scalar1=1.0 / 64.0, scalar2=None, op0=ALU.mult)
            yt4 = ytmp[:, :].rearrange("p (h v) -> p h v", h=4)
            tt(C, "dve", ytmp, yt4, Yacc, y4, st4, st4[:, :].unsqueeze(2).broadcast_to([128, 4, 64]), ALU.subtract)
            tt(C, "pool", otok, otok[:, :], ytmp, ytmp[:, :], ytmp, ytmp[:, :], ALU.mult)
            K.ins("dve", [otok], [st4b], nc.vector.tensor_reduce, out=st4b[:, :], in_=otok[:, :].rearrange("p (h v) -> p h v", h=4), axis=AX.X, op=ALU.add)
            act(C, st4b, st4b[:, :], st4b, st4b[:, :], AF.Sqrt, extra=[C.gneps], scale=1.0 / 64.0, bias=C.gneps[:, 0:1])
            K.ins("dve", [st4b], [st4b], nc.vector.reciprocal, out=st4b[:, :], in_=st4b[:, :])
            tt(C, "dve", ytmp, yt4, ytmp, yt4, st4b, st4b[:, :].unsqueeze(2).broadcast_to([128, 4, 64]), ALU.mult)
            tt(C, "pool", ytmp, ytmp[:, :], ytmp, ytmp[:, :], lnw, lnw[:, :], ALU.mult)
            tt(C, "dve", ytmp, ytmp[:, :], ytmp, ytmp[:, :], lnb, lnb[:, :], ALU.add)
            tt(C, "dve", otok, otok[:, :].rearrange("p (h v) -> p h v", h=4), Vtok, Vtok[:, cidx, :].rearrange("p (h v) -> p h v", h=4), bon, bon[:, cidx, :].unsqueeze(2).broadcast_to([128, 4, 64]), ALU.mult)
            tt(C, "pool", ytmp, ytmp[:, :], ytmp, ytmp[:, :], otok, otok[:, :], ALU.add)
            tt(C, "dve", otok, otok[:, :], ytmp, ytmp[:, :], gl, gl[:, :], ALU.mult)
            po = psQ[cidx % 2]
            for hp in range(2):
                K.ins("pe", [otok, C.ident], [po], nc.tensor.transpose, out=po[:, hp * 128:(hp + 1) * 128], in_=otok[:, hp * 128:(hp + 1) * 128], identity=C.ident[:])
            cp(C, "act", obst, obst[:, :, :], po, po[:, 0:256].rearrange("p (h t) -> p h t", h=2))
            K.dma("sp", C.obr[768:1024, cidx * 128:(cidx + 1) * 128].rearrange("(h p) t -> p h t", p=128), obst[:, :, :], [obst], C.obr)
```

```python
import numpy as np
import concourse.bass as bass
import concourse.mybir as mybir

F32 = mybir.dt.float32
BF16 = mybir.dt.bfloat16
AF = mybir.ActivationFunctionType
ALU = mybir.AluOpType
AX = mybir.AxisListType


class Tk:
    __slots__ = ("name", "t", "w", "r", "dsem", "dcnt", "psum")

    def __init__(self, name, t=None):
        self.name = name
        self.t = t
        self.w = None
        self.r = {}
        self.dsem = None
        self.dcnt = 0
        self.psum = False

    def __getitem__(self, k):
        return self.t[k]


class Kern:
    EP = 16000

    def __init__(self, nc, es):
        self.nc = nc
        self.es = es
        self.H = {"pe": nc.tensor, "act": nc.scalar, "dve": nc.vector, "pool": nc.gpsimd, "sp": nc.sync}
        self.q = {e: [] for e in self.H}
        self.known = {e: {} for e in self.H}
        self.nsem = 0
        self.dsems = []
        self.dtks = []
        self.lastc = {e: -1 for e in self.H}

    def sb(self, name, shape, dt, stack=None):
        self.uid = getattr(self, "uid", 0) + 1
        name = f"{name}_{self.uid}"
        t = (stack or self.es).enter_context(self.nc.sbuf_tensor(name, list(shape), dt))
        tk = Tk(name, t)
        return tk

    def ps(self, name, shape, dt=F32, stack=None):
        self.uid = getattr(self, "uid", 0) + 1
        name = f"{name}_{self.uid}"
        t = (stack or self.es).enter_context(self.nc.psum_tensor(name, list(shape), dt))
        tk = Tk(name, t)
        tk.psum = True
        return tk

    def dram(self, name, shape, dt, kind="Internal"):
        t = self.nc.dram_tensor(name, list(shape), dt, kind=kind).ap()
        return Tk(name, t)

    def newsem(self, name):
        self.nsem += 1
        return self.es.enter_context(self.nc.semaphore(name))

    def _need(self, e, tok, waits, same_ok):
        if tok is None:
            return
        if tok[0] == "e":
            _, f, idx = tok
            if f == e and (same_ok or e in ("pe", "sp")):
                return
            if self.known[e].get(f, -1) >= idx:
                return
            self.known[e][f] = idx
            waits.append(tok)
            self.q[f][idx][2] = True
        else:
            _, sid, val = tok
            if self.known[e].get(("d", sid), 0) >= val:
                return
            self.known[e][("d", sid)] = val
            waits.append(tok)

    def _deps(self, e, reads, writes, nowaw=False):
        waits = []
        for t in reads:
            self._need(e, t.w, waits, False)
            if t.psum:
                for tok in t.r.values():
                    self._need(e, tok, waits, True)
        for t in writes:
            if not (nowaw and t.w is not None and t.w[0] == "d"):
                self._need(e, t.w, waits, True)
            for tok in t.r.values():
                self._need(e, tok, waits, True)
        return waits

    def ins(self, e, reads, writes, _f, *args, _selfwait=None, **kwargs):
        fn = lambda: _f(*args, **kwargs)
        waits = self._deps(e, reads, writes)
        if _selfwait is not None and self.known[e].get(e, -1) < _selfwait[2]:
            self.known[e][e] = _selfwait[2]
            waits.append(_selfwait)
            self.q[e][_selfwait[2]][2] = True
        idx = len(self.q[e])
        self.q[e].append([fn, waits, False, None])
        self.lastc[e] = idx
        tok = ("e", e, idx)
        for t in reads:
            t.r[e] = tok
        for t in writes:
            t.w = tok
            t.r = {}
        return tok

    def dma(self, e, out_ap, in_ap, reads, write, nowaw=True, **kw):
        waits = self._deps(e, reads, [write], nowaw=nowaw)
        if write.dsem is None:
            free = getattr(self, "dfree", None)
            if free is None:
                free = self.dfree = []
                self.dcount = {}
            if free:
                write.dsem = free.pop()
            else:
                write.dsem = len(self.dsems)
                self.dsems.append(self.newsem(f"d{len(self.dsems)}"))
                self.dcount[write.dsem] = 0
            self.dtks.append(write)
        self.dcount[write.dsem] += 16
        write.dcnt = self.dcount[write.dsem]
        tok = ("d", write.dsem, write.dcnt)
        H = self.H[e]
        self.q[e].append([lambda: H.dma_start(out=out_ap, in_=in_ap, **kw), waits, False, tok])
        for t in reads:
            t.r[("d", write.dsem)] = tok
        write.w = tok
        return tok

    def barrier(self):
        last = dict(self.lastc)
        for e in self.H:
            waits = []
            for f in self.H:
                if f != e and last[f] >= 0:
                    self._need(e, ("e", f, last[f]), waits, False)
            for tk in self.dtks:
                self._need(e, ("d", tk.dsem, tk.dcnt), waits, False)
            self.q[e].append([None, waits, False, None])
        for tk in self.dtks:
            self.dfree.append(tk.dsem)
            tk.dsem = None
        self.dtks = []

    def emit(self):
        nc = self.nc
        sigmap = {}
        for e, lst in self.q.items():
            cnt = 0
            m = {}
            sems = []
            for idx, rec in enumerate(lst):
                if rec[2]:
                    ep, v = divmod(cnt, self.EP)
                    if ep >= len(sems):
                        sems.append(self.newsem(f"e_{e}_{ep}"))
                    m[idx] = (sems[ep], v + 1)
                    cnt += 1
            sigmap[e] = m
        for e, lst in self.q.items():
            H = self.H[e]
            for idx, rec in enumerate(lst):
                fn, waits, sig, dtok = rec
                for tok in waits:
                    if tok[0] == "e":
                        s, v = sigmap[tok[1]][tok[2]]
                        H.wait_ge(s, v)
                    else:
                        H.wait_ge(self.dsems[tok[1]], tok[2])
                if fn is None:
                    assert not sig
                    continue
                ins = fn()
                if dtok is not None:
                    ins.then_inc(self.dsems[dtok[1]], 16)
                    assert not sig
                elif sig:
                    ins.then_inc(sigmap[e][idx][0], 1)
        print("emitted:", {e: len(l) for e, l in self.q.items()}, "sems", self.nsem)


from concourse.bass_utils import run_bass_kernel_spmd
import contextlib
import ml_dtypes

T_ALL = 4352
LC = 256
EPS = 1e-6


EXTRA_INPUTS = [
    ("bands", [128, 20, 128]), ("pw_bd", [2, 2, 128, 128]), ("pscale", [128, 2, 2]),
    ("w_uq", [2, 384, 768]), ("w_ukv_k", [2, 256, 512]), ("w_ukv_v", [2, 256, 512]),
    ("Cq", [128, 128]), ("Prot", [128, 128]), ("gq96", [128, 2]), ("gk96", [128, 2]),
    ("qng", [128, 2, 3]), ("kvng", [128, 2, 2]), ("cs", [32, 2, 4096]),
    ("r_mu", [128, 2, 9, 2]), ("r_kk", [128, 2, 2]), ("r_w0", [128, 2, 2, 2]), ("r_a0", [128, 2, 2, 2]), ("r_ka", [128, 2, 2, 2]),
    ("r_rk", [128, 2, 2]), ("r_w2", [2, 2, 128, 256]), ("r_a2", [2, 2, 128, 256]), ("r_g2", [2, 128, 256]),
    ("r_lnwb", [2, 128, 256]), ("r_lnbb", [2, 128, 256]), ("blk64", [128, 128]), ("ind2", [128, 2]),
    ("masks", [128, 4, 128]), ("rmask", [128, 512]), ("lvmask", [128, 14, 128]),
]


def tiles_of(n, last_layer=False):
    ts = [] if last_layer else [(0, LC, 1)]
    t = LC
    while t < T_ALL:
        ts.append((t, n, 0))
        t += n
    return ts


class Ctx:
    pass


def build(nc, es, dbg=None, stop_after=None):
    K = Kern(nc, es)
    C = Ctx()
    C.rdbg = (dbg or {}).get("rdbg", 9)
    C.pdbg = (dbg or {}).get("pdbg", 9)
    C.sdbg = (dbg or {}).get("sdbg", 9)
    C.bdbg = (dbg or {}).get("bdbg", 9)
    C.maxfly = (dbg or {}).get("maxfly", 4)
    C.ustop = (dbg or {}).get("ustop", 0)
    C.onebank = (dbg or {}).get("onebank", False)
    C.psteps = (dbg or {}).get("psteps", 1)
    C.pevery = (dbg or {}).get("pevery", 1)
    C.K = K
    C.nc = nc
    dbg = dbg or {}
    ext = lambda name, shape, dt=F32: K.dram(name, shape, dt, kind="ExternalInput")
    I = {}
    I["xin"] = ext("xin", [T_ALL, 1024])
    I["ccT"] = ext("ccT", [128, 16])
    I["w_ada"] = ext("w_ada", [2, 1024, 6144])
    I["badaT"] = ext("badaT", [128, 96])
    I["ng"] = ext("ng", [128, 32])
    I["w_in"] = ext("w_in", [2, 1024, 5152])
    I["w_br"] = ext("w_br", [2, 1024, 1024])
    I["w_o"] = ext("w_o", [2, 1024, 1024])
    I["mlp_w1"] = ext("mlp_w1", [2, 1024, 4096])
    I["mlp_w2"] = ext("mlp_w2", [2, 4096, 1024])
    I["ident"] = ext("ident", [128, 128])
    for nm, shp in EXTRA_INPUTS:
        I[nm] = ext(nm, shp)
    C.I = I
    out = K.dram("out", [4096, 1024], F32, kind="ExternalOutput")
    C.out = out
    douts = dbg.get("outs", ())
    scr = lambda name, shape, dt: K.dram(name, shape, dt, kind=("ExternalOutput" if name in douts else "Internal"))
    C.scr = scr
    C.xT = [scr("xT0", [1024, T_ALL], F32), scr("xT1", [1024, T_ALL], F32)]
    C.zT = scr("zT", [15, 128, T_ALL], F32)
    C.u_d = scr("u_d", [T_ALL, 256], BF16)
    C.obr = scr("obr", [1024, T_ALL], BF16)
    C.moddbg = scr("moddbg", [128, 192], F32)
    C.gtok_d = scr("gtok", [T_ALL, 256], BF16)
    if "obr_in" in dbg:
        C.obr_in = ext("obr_in", [1024, T_ALL])
    C.ident = K.sb("ident_s", [128, 128], F32)
    C.identb = K.sb("identb_s", [128, 128], BF16)
    C.ones = K.sb("ones_s", [128, 128], F32)
    C.epsb = K.sb("epsb_s", [128, 1], F32)
    C.gneps = K.sb("gneps_s", [128, 1], F32)
    C.mod = K.sb("mod_s", [128, 2, 48, 2], F32)
    C.gm = K.sb("gm_s", [128, 2, 2, 8, 2], F32)
    C.ng = K.sb("ng_s", [128, 2, 2, 8], F32)
    K.dma("sp", C.ident[:], I["ident"][:, :], [I["ident"]], C.ident)
    K.dma("sp", C.ng[:].rearrange("p a b c -> p (a b c)"), I["ng"][:, :], [I["ng"]], C.ng)
    K.ins("dve", [], [C.ones], nc.vector.memset, C.ones[:], 1.0)
    K.ins("dve", [], [C.epsb], nc.vector.memset, C.epsb[:], EPS)
    K.ins("dve", [], [C.gneps], nc.vector.memset, C.gneps[:], 64e-5)
    K.ins("dve", [C.ident], [C.identb], nc.vector.tensor_copy, out=C.identb[:], in_=C.ident[:])

    SKIP = dbg.get("skip_pre", False)
    if not SKIP:
        phase_M(C)
    K.barrier()
    if "moddbg" in douts:
        K.dma("sp", C.moddbg[:, :], C.mod[:].rearrange("p l m s -> p (l m s)"), [C.mod], C.moddbg)
    if not SKIP:
        phase_T0(C)
    K.barrier()
    cur = 0
    for li in range(2):
        last = li == 1
        if not SKIP:
            phase_A(C, li, cur)
        K.barrier()
        if stop_after == ("A", li):
            break
        if "obr_in" in dbg:
            phase_dbg_obr(C)
            K.barrier()
        else:
            mix = dbg.get("mix", ("P", "L", "R"))
            if "P" in mix:
                phase_P(C, li)
                K.barrier()
            if "L" in mix:
                phase_MLA(C, li)
                K.barrier()
            if "R" in mix:
                phase_R(C, li)
                K.barrier()
        if stop_after == ("X", li):
            break
        phase_C1(C, li, cur)
        K.barrier()
        cur ^= 1
        phase_C2(C, li, cur)
        K.barrier()
        cur ^= 1
        if stop_after == ("C", li):
            break
    K.barrier()
    K.emit()
    K.C = C
    return K


def phase_dbg_obr(C):
    K, nc = C.K, C.nc
    with contextlib.ExitStack() as st:
        a = K.sb("dbo_a", [128, 8, 512], F32, st)
        b = K.sb("dbo_b", [128, 8, 512], BF16, st)
        for t0 in range(0, T_ALL, 512):
            n = min(512, T_ALL - t0)
            K.dma("sp", a[:, :, :n], C.obr_in[:, t0:t0 + n].rearrange("(c p) t -> p c t", p=128), [C.obr_in], a)
            K.ins("dve", [a], [b], nc.vector.tensor_copy, out=b[:, :, :n], in_=a[:, :, :n])
            K.dma("sp", C.obr[:, t0:t0 + n].rearrange("(c p) t -> p c t", p=128), b[:, :, :n], [b], C.obr)


def phase_M(C):
    K, nc, I = C.K, C.nc, C.I
    with contextlib.ExitStack() as st:
        cc = K.sb("m_cc", [128, 8, 2], F32, st)
        sc = K.sb("m_sc", [128, 8, 2], F32, st)
        bada = K.sb("m_bada", [128, 2, 48], F32, st)
        wb = [K.sb(f"m_w{j}", [128, 8, 768], F32, st) for j in range(2)]
        psMs = [K.ps(f"m_ps{j}", [128, 512], F32, st) for j in range(4)]
        mraw = K.sb("m_raw", [128, 48, 2], F32, st)
        K.dma("sp", cc[:].rearrange("p k s -> p (k s)"), I["ccT"][:, :], [I["ccT"]], cc)
        K.dma("sp", bada[:].rearrange("p l m -> p (l m)"), I["badaT"][:, :], [I["badaT"]], bada)
        K.ins("act", [cc], [sc], nc.scalar.activation, out=sc[:], in_=cc[:], func=AF.Silu)
        n = 0
        for li in range(2):
            for blk in range(8):
                w = wb[n % 2]
                n += 1
                for kc in range(8):
                    K.dma("sp" if kc % 2 == 0 else "act", w[:, kc, :], I["w_ada"][li, kc * 128:(kc + 1) * 128, blk * 768:(blk + 1) * 768], [I["w_ada"]], w)
                for m6 in range(6):
                    m = blk * 6 + m6
                    pm = psMs[m % 4]
                    for kc in range(8):
                        K.ins("pe", [w, sc], [pm], nc.tensor.matmul, pm[:, 0:2], lhsT=w[:, kc, m6 * 128:(m6 + 1) * 128], rhs=sc[:, kc, :], start=(kc == 0), stop=(kc == 7))
                    K.ins("dve", [pm], [mraw], nc.vector.tensor_copy, out=mraw[:, m, :], in_=pm[:, 0:2])
            K.ins("dve", [mraw, bada], [C.mod], nc.vector.tensor_tensor, out=C.mod[:, li, :, :], in0=mraw[:], in1=bada[:, li, :].unsqueeze(2).broadcast_to([128, 48, 2]), op=ALU.add)
            for which in range(2):
                K.ins("dve", [C.mod, C.ng], [C.gm], nc.vector.scalar_tensor_tensor,
                    out=C.gm[:, which, li, :, :], in0=C.mod[:, li, 8 + 24 * which:16 + 24 * which, :], scalar=1.0,
                    in1=C.ng[:, which, li, :].unsqueeze(2).broadcast_to([128, 8, 2]), op0=ALU.add, op1=ALU.mult)


def phase_T0(C):
    K, nc, I = C.K, C.nc, C.I
    with contextlib.ExitStack() as st:
        xin = [K.sb(f"t0_x{j}", [128, 4, 1024], F32, st) for j in range(2)]
        xo = [K.sb(f"t0_o{j}", [128, 8, 512], F32, st) for j in range(2)]
        pss = [K.ps(f"t0_p{j}", [128, 512], F32, st) for j in range(4)]
        npz = 0
        for it, t0 in enumerate(range(0, T_ALL, 512)):
            n = min(512, T_ALL - t0)
            nj = n // 128
            xi = xin[it % 2]
            o = xo[it % 2]
            K.dma("sp", xi[:, :nj, :], I["xin"][t0:t0 + n, :].rearrange("(j p) d -> p j d", p=128), [I["xin"]], xi)
            for c in range(8):
                p = pss[npz % 4]
                npz += 1
                for j in range(nj):
                    K.ins("pe", [xi, C.ident], [p], nc.tensor.transpose, out=p[:, j * 128:(j + 1) * 128], in_=xi[:, j, c * 128:(c + 1) * 128], identity=C.ident[:])
                if c % 2 == 0:
                    K.ins("dve", [p], [o], nc.vector.tensor_copy, out=o[:, c, :n], in_=p[:, :n])
                else:
                    K.ins("act", [p], [o], nc.scalar.copy, out=o[:, c, :n], in_=p[:, :n])
            K.dma("act", C.xT[0][:, t0:t0 + n].rearrange("(c p) t -> p c t", p=128), o[:, :, :n], [o], C.xT[0])


def compute_h_g(C, xt, wk, hT, psS, rstd, which, li, s, n):
    K, nc = C.K, C.nc
    K.ins("act", [xt], [wk], nc.scalar.activation, out=wk[:, :, :n], in_=xt[:, :, :n], func=AF.Square)
    yield
    for c in range(8):
        K.ins("pe", [C.ones, wk], [psS], nc.tensor.matmul, psS[:, :n], lhsT=C.ones[:], rhs=wk[:, c, :n], start=(c == 0), stop=(c == 7))
        if c % 4 == 3:
            yield
    K.ins("act", [psS, C.epsb], [rstd], nc.scalar.activation, out=rstd[:, :n], in_=psS[:, :n], func=AF.Ln, scale=1.0 / 1024.0, bias=C.epsb[:, 0:1])
    yield
    K.ins("act", [rstd], [rstd], nc.scalar.activation, out=rstd[:, :n], in_=rstd[:, :n], func=AF.Exp, scale=-0.5)
    yield
    shb = 0 if which == 0 else 24
    for c in range(8):
        K.ins("dve", [xt, C.gm, rstd], [wk], nc.vector.scalar_tensor_tensor, out=wk[:, c, :n], in0=xt[:, c, :n], scalar=C.gm[:, which, li, c, s:s + 1], in1=rstd[:, :n], op0=ALU.mult, op1=ALU.mult)
        yield
        K.ins("act", [wk, C.mod], [hT], nc.scalar.activation, out=hT[:, c, :n], in_=wk[:, c, :n], func=AF.Identity, bias=C.mod[:, li, shb + c, s:s + 1], scale=1.0)
        yield


def compute_h(C, xt, wk, hT, psS, rstd, which, li, s, n):
    for _ in compute_h_g(C, xt, wk, hT, psS, rstd, which, li, s, n):
        pass


def rr(*gens):
    gens = [g for g in gens if g is not None]
    while gens:
        for g in list(gens):
            try:
                next(g)
            except StopIteration:
                gens.remove(g)


def pipeline_tiles(ntiles, load_fn, h_fn, main_fn):
    load_fn(0)
    for _ in h_fn(0):
        pass
    for i in range(ntiles):
        nxt = None
        if i + 1 < ntiles:
            load_fn(i + 1)
            nxt = h_fn(i + 1)
        rr(main_fn(i), nxt)


def load_w_bf16(C, dst, src_ap_fn, nk, src_tk):
    K = C.K
    for kc in range(nk):
        K.dma("pool", dst[:, kc, :], src_ap_fn(kc), [src_tk], dst)


A_CHUNKS = [(256 + 128 * m, 128) for m in range(3)] + [(640, 128), (768, 128), (896, 32)] + [(928 + 128 * m, 128) for m in range(9)]


def phase_A(C, li, cur):
    K, nc, I = C.K, C.nc, C.I
    with contextlib.ExitStack() as st:
        wA = K.sb("a_w", [128, 8, 2080], BF16, st)
        load_w_bf16(C, wA, lambda kc: I["w_in"][li, kc * 128:(kc + 1) * 128, 0:2080], 8, I["w_in"])
        xts = [K.sb(f"a_x{j}", [128, 8, 512], F32, st) for j in range(2)]
        wk = K.sb("a_wk", [128, 8, 512], F32, st)
        hTs = [K.sb(f"a_h{j}", [128, 8, 512], BF16, st) for j in range(2)]
        rstd = K.sb("a_rstd", [128, 512], F32, st)
        stg = [K.sb(f"a_stg{j}", [128, 15, 512], F32, st) for j in range(1)]
        ustg = K.sb("a_ustg", [128, 4, 256], BF16, st)
        psS = K.ps("a_psS", [128, 512], F32, st)
        psP = [K.ps(f"a_psP{j}", [128, 512], F32, st) for j in range(4)]
        psU = [K.ps(f"a_psU{j}", [128, 512], F32, st) for j in range(2)]
        tiles = tiles_of(512)
        cntp = {"npp": 0}

        def load_fn(i):
            t0, n, s = tiles[i]
            xt = xts[i % 2]
            K.dma("sp", xt[:, :, :n], C.xT[cur][:, t0:t0 + n].rearrange("(c p) t -> p c t", p=128), [C.xT[cur]], xt)

        def h_fn(i):
            t0, n, s = tiles[i]
            return compute_h_g(C, xts[i % 2], wk, hTs[i % 2], psS, rstd, 0, li, s, n)

        def main_fn(i):
            t0, n, s = tiles[i]
            hT = hTs[i % 2]
            sg = stg[0]
            for mi, (c0, M) in enumerate(A_CHUNKS):
                p = psP[cntp["npp"] % 4]
                cntp["npp"] += 1
                for kc in range(8):
                    K.ins("pe", [wA, hT], [p], nc.tensor.matmul, p[:M, :n], lhsT=wA[:, kc, c0:c0 + M], rhs=hT[:, kc, :n], start=(kc == 0), stop=(kc == 7))
                if mi % 2 == 0:
                    K.ins("dve", [p], [sg], nc.vector.tensor_copy, out=sg[:M, mi, :n], in_=p[:M, :n])
                else:
                    K.ins("act", [p], [sg], nc.scalar.copy, out=sg[:M, mi, :n], in_=p[:M, :n])
                yield
            K.dma("sp", C.zT[:, :, t0:t0 + n].rearrange("m p t -> p m t"), sg[:, :, :n], [sg], C.zT)
            for j in range(n // 128):
                p = psU[j % 2]
                for kc in range(8):
                    K.ins("pe", [wA, hT], [p], nc.tensor.matmul, p[:, 0:256], lhsT=hT[:, kc, j * 128:(j + 1) * 128], rhs=wA[:, kc, 0:256], start=(kc == 0), stop=(kc == 7))
                K.ins("act", [p], [ustg], nc.scalar.copy, out=ustg[:, j, :], in_=p[:, 0:256])
                yield
            K.dma("sp", C.u_d[t0:t0 + n, :].rearrange("(j p) c -> p j c", p=128), ustg[:, :n // 128, :], [ustg], C.u_d)

        pipeline_tiles(len(tiles), load_fn, h_fn, main_fn)


BR_K = [(0, 2), (2, 4), (6, 2)]


def phase_C1(C, li, cur):
    K, nc, I = C.K, C.nc, C.I
    last = li == 1
    with contextlib.ExitStack() as st:
        wG = K.sb("c1_wg", [128, 8, 3072], BF16, st)
        wB = K.sb("c1_wb", [128, 8, 1024], BF16, st)
        wO = K.sb("c1_wo", [128, 8, 1024], BF16, st)
        load_w_bf16(C, wG, lambda kc: I["w_in"][li, kc * 128:(kc + 1) * 128, 2080:5152], 8, I["w_in"])
        load_w_bf16(C, wB, lambda kc: I["w_br"][li, kc * 128:(kc + 1) * 128, :], 8, I["w_br"])
        load_w_bf16(C, wO, lambda kc: I["w_o"][li, kc * 128:(kc + 1) * 128, :], 8, I["w_o"])
        xts = [K.sb(f"c1_x{j}", [128, 8, 512], F32, st) for j in range(2)]
        wk = K.sb("c1_wk", [128, 8, 512], F32, st)
        hTs = [K.sb(f"c1_h{j}", [128, 8, 512], BF16, st) for j in range(2)]
        obs = [K.sb(f"c1_ob{j}", [128, 8, 512], BF16, st) for j in range(2)]
        mT = K.sb("c1_m", [128, 8, 512], BF16, st)
        rstd = K.sb("c1_rstd", [128, 512], F32, st)
        sgs = [K.sb(f"c1_sg{j}", [128, 512], BF16, st) for j in range(3)]
        tt_ = [K.sb(f"c1_t{j}", [128, 512], F32, st) for j in range(3)]
        psS = K.ps("c1_psS", [128, 512], F32, st)
        psG = [K.ps(f"c1_psG{j}", [128, 512], F32, st) for j in range(3)]
        psB = [K.ps(f"c1_psB{j}", [128, 512], F32, st) for j in range(3)]
        psO = K.ps("c1_psO", [128, 512], F32, st)
        tiles = tiles_of(512, last)

        def load_fn(i):
            t0, n, s = tiles[i]
            K.dma("sp", xts[i % 2][:, :, :n], C.xT[cur][:, t0:t0 + n].rearrange("(c p) t -> p c t", p=128), [C.xT[cur]], xts[i % 2])
            K.dma("sp", obs[i % 2][:, :, :n], C.obr[:, t0:t0 + n].rearrange("(c p) t -> p c t", p=128), [C.obr], obs[i % 2])

        def h_fn(i):
            t0, n, s = tiles[i]
            return compute_h_g(C, xts[i % 2], wk, hTs[i % 2], psS, rstd, 0, li, s, n)

        def main_fn(i):
            t0, n, s = tiles[i]
            xt, hT, ob = xts[i % 2], hTs[i % 2], obs[i % 2]
            for mo in range(8):
                for b in range(3):
                    pg, pb = psG[b], psB[b]
                    gc = (b * 8 + mo) * 128
                    for kc in range(8):
                        K.ins("pe", [wG, hT], [pg], nc.tensor.matmul, pg[:, :n], lhsT=wG[:, kc, gc:gc + 128], rhs=hT[:, kc, :n], start=(kc == 0), stop=(kc == 7))
                    k0, nk = BR_K[b]
                    for kk in range(nk):
                        K.ins("pe", [wB, ob], [pb], nc.tensor.matmul, pb[:, :n], lhsT=wB[:, k0 + kk, mo * 128:(mo + 1) * 128], rhs=ob[:, k0 + kk, :n], start=(kk == 0), stop=(kk == nk - 1))
                    K.ins("act", [pg], [sgs[b]], nc.scalar.activation, out=sgs[b][:, :n], in_=pg[:, :n], func=AF.Sigmoid)
                    K.ins("dve", [pb, sgs[b]], [tt_[b]], nc.vector.tensor_tensor, out=tt_[b][:, :n], in0=pb[:, :n], in1=sgs[b][:, :n], op=ALU.mult)
                    yield
                K.ins("pool", [tt_[0], tt_[1]], [tt_[0]], nc.gpsimd.tensor_tensor, out=tt_[0][:, :n], in0=tt_[0][:, :n], in1=tt_[1][:, :n], op=ALU.add)
                K.ins("pool", [tt_[0], tt_[2]], [mT], nc.gpsimd.tensor_tensor, out=mT[:, mo, :n], in0=tt_[0][:, :n], in1=tt_[2][:, :n], op=ALU.add)
            for mo in range(8):
                for kc in range(8):
                    K.ins("pe", [wO, mT], [psO], nc.tensor.matmul, psO[:, :n], lhsT=wO[:, kc, mo * 128:(mo + 1) * 128], rhs=mT[:, kc, :n], start=(kc == 0), stop=(kc == 7))
                K.ins("dve", [psO, C.mod, xt], [xt], nc.vector.scalar_tensor_tensor, out=xt[:, mo, :n], in0=psO[:, :n], scalar=C.mod[:, li, 16 + mo, s:s + 1], in1=xt[:, mo, :n], op0=ALU.mult, op1=ALU.add)
                yield
            K.dma("act", C.xT[cur ^ 1][:, t0:t0 + n].rearrange("(c p) t -> p c t", p=128), xt[:, :, :n], [xt], C.xT[cur ^ 1])

        pipeline_tiles(len(tiles), load_fn, h_fn, main_fn)


def phase_C2(C, li, cur):
    K, nc, I = C.K, C.nc, C.I
    last = li == 1
    N = 256
    with contextlib.ExitStack() as st:
        w1 = K.sb("c2_w1", [128, 8, 4096], BF16, st)
        w2 = K.sb("c2_w2", [128, 32, 1024], BF16, st)
        load_w_bf16(C, w1, lambda kc: I["mlp_w1"][li, kc * 128:(kc + 1) * 128, :], 8, I["mlp_w1"])
        load_w_bf16(C, w2, lambda kc: I["mlp_w2"][li, kc * 128:(kc + 1) * 128, :], 32, I["mlp_w2"])
        xts = [K.sb(f"c2_x{j}", [128, 8, N], F32, st) for j in range(2)]
        wk = K.sb("c2_wk", [128, 8, N], F32, st)
        hTs = [K.sb(f"c2_h{j}", [128, 8, N], BF16, st) for j in range(2)]
        uT = K.sb("c2_u", [128, 32, N], BF16, st)
        rstd = K.sb("c2_rstd", [128, N], F32, st)
        rrb = [K.sb(f"c2_r{j}", [128, N], BF16, st) for j in range(3)]
        ostg = [K.sb(f"c2_os{j}", [128, 1024], F32, st) for j in range(2)] if last else None
        psS = K.ps("c2_psS", [128, 512], F32, st)
        ps1 = [K.ps(f"c2_p1{j}", [128, 512], F32, st) for j in range(3)]
        ps2 = [K.ps(f"c2_p2{j}", [128, 512], F32, st) for j in range(2)]
        psT = [K.ps(f"c2_pT{j}", [128, 512], F32, st) for j in range(2)] if last else None
        tiles = tiles_of(N, last)
        cn = {"n1": 0, "n2": 0, "nt": 0}

        def load_fn(i):
            t0, n, s = tiles[i]
            K.dma("sp", xts[i % 2][:, :, :n], C.xT[cur][:, t0:t0 + n].rearrange("(c p) t -> p c t", p=128), [C.xT[cur]], xts[i % 2])

        def h_fn(i):
            t0, n, s = tiles[i]
            return compute_h_g(C, xts[i % 2], wk, hTs[i % 2], psS, rstd, 1, li, s, n)

        def main_fn(i):
            t0, n, s = tiles[i]
            xt, hT = xts[i % 2], hTs[i % 2]
            for f in range(32):
                p = ps1[cn["n1"] % 3]
                r = rrb[cn["n1"] % 3]
                cn["n1"] += 1
                for kc in range(8):
                    K.ins("pe", [w1, hT], [p], nc.tensor.matmul, p[:, :n], lhsT=w1[:, kc, f * 128:(f + 1) * 128], rhs=hT[:, kc, :n], start=(kc == 0), stop=(kc == 7))
                K.ins("act", [p], [r], nc.scalar.activation, out=r[:, :n], in_=p[:, :n], func=AF.Relu)
                K.ins("dve", [r], [uT], nc.vector.tensor_tensor, out=uT[:, f, :n], in0=r[:, :n], in1=r[:, :n], op=ALU.mult)
                if f % 2 == 1:
                    yield
            for mo in range(8):
                p = ps2[cn["n2"] % 2]
                cn["n2"] += 1
                for f in range(32):
                    K.ins("pe", [w2, uT], [p], nc.tensor.matmul, p[:, :n], lhsT=w2[:, f, mo * 128:(mo + 1) * 128], rhs=uT[:, f, :n], start=(f == 0), stop=(f == 31))
                K.ins("dve", [p, C.mod, xt], [xt], nc.vector.scalar_tensor_tensor, out=xt[:, mo, :n], in0=p[:, :n], scalar=C.mod[:, li, 40 + mo, s:s + 1], in1=xt[:, mo, :n], op0=ALU.mult, op1=ALU.add)
                yield
            if not last:
                K.dma("act", C.xT[cur ^ 1][:, t0:t0 + n].rearrange("(c p) t -> p c t", p=128), xt[:, :, :n], [xt], C.xT[cur ^ 1])
            else:
                for j in range(n // 128):
                    og = ostg[cn["nt"] % 2]
                    for half in range(2):
                        p = psT[half]
                        for c4 in range(4):
                            c = half * 4 + c4
                            K.ins("pe", [xt, C.ident], [p], nc.tensor.transpose, out=p[:, c4 * 128:(c4 + 1) * 128], in_=xt[:, c, j * 128:(j + 1) * 128], identity=C.ident[:])
                        if half == 0:
                            K.ins("dve", [p], [og], nc.vector.tensor_copy, out=og[:, 0:512], in_=p[:, :])
                        else:
                            K.ins("act", [p], [og], nc.scalar.copy, out=og[:, 512:1024], in_=p[:, :])
                    cn["nt"] += 1
                    r0 = t0 - LC + j * 128
                    K.dma("act", C.out[r0:r0 + 128, :], og[:, :], [og], C.out)
                    yield

        pipeline_tiles(len(tiles), load_fn, h_fn, main_fn)


def host_inputs(inputs, b):
    f = lambda a: np.ascontiguousarray(a, dtype=np.float32)
    x, c, ctx, c_ctx = inputs["x"], inputs["c"], inputs["ctx"], inputs["c_ctx"]
    m = {}
    m["xin"] = f(np.concatenate([ctx[b], x[b]], axis=0))
    cc = np.stack([c[b], c_ctx], axis=-1)
    m["ccT"] = f(cc.reshape(8, 128, 2).transpose(1, 0, 2).reshape(128, 16))
    m["w_ada"] = f(inputs["w_ada"])
    m["badaT"] = f(inputs["b_ada"].reshape(2, 48, 128).transpose(2, 0, 1).reshape(128, 96))
    ng = np.stack([inputs["norm1_g"], inputs["norm2_g"]], axis=0)
    m["ng"] = f(ng.reshape(2, 2, 8, 128).transpose(3, 0, 1, 2).reshape(128, 32))
    m["w_in"] = f(inputs["w_in"])
    m["w_br"] = f(np.concatenate([inputs["w_br_pool"], inputs["w_br_mla"], inputs["w_br_rwkv"]], axis=1))
    m["w_o"] = f(inputs["w_o"])
    m["mlp_w1"] = f(inputs["mlp_w1"])
    m["mlp_w2"] = f(inputs["mlp_w2"])
    m["ident"] = np.eye(128, dtype=np.float32)
    m.update(host_consts())
    pw = inputs["pool_w"]
    pwbd = np.zeros((2, 2, 128, 128), np.float32)
    for g in range(4):
        a, gi = divmod(g, 2)
        pwbd[:, a, gi * 64:(gi + 1) * 64, gi * 64:(gi + 1) * 64] = pw[:, g]
    m["pw_bd"] = pwbd
    m["pscale"] = f(inputs["pool_scale"].reshape(2, 2, 128).transpose(2, 0, 1))
    m["w_uq"] = f(inputs["mla_w_uq"])
    wkv = inputs["mla_w_ukv"].reshape(2, 256, 8, 128)
    m["w_ukv_k"] = f(wkv[:, :, :, :64].reshape(2, 256, 512))
    m["w_ukv_v"] = f(wkv[:, :, :, 64:].reshape(2, 256, 512))
    g96 = np.zeros((128, 2), np.float32); g96[:96] = inputs["qk_gain_q"].T; m["gq96"] = g96
    g96 = np.zeros((128, 2), np.float32); g96[:96] = inputs["qk_gain_k"].T; m["gk96"] = g96
    m["qng"] = f(inputs["mla_q_norm"].reshape(2, 3, 128).transpose(2, 0, 1))
    m["kvng"] = f(inputs["mla_kv_norm"].reshape(2, 2, 128).transpose(2, 0, 1))
    m["r_mu"] = f(inputs["rwkv_mu"].reshape(2, 2, 9, 128).transpose(3, 0, 2, 1))
    m["r_kk"] = f(inputs["rwkv_kk"].reshape(2, 2, 128).transpose(2, 0, 1))
    for nm in ("w0", "a0", "ka"):
        m["r_" + nm] = f(inputs["rwkv_" + nm].reshape(2, 2, 2, 128).transpose(3, 0, 1, 2))
    m["r_rk"] = f(inputs["rwkv_rk"].reshape(2, 2, 128).transpose(2, 0, 1))
    for nm in ("w2", "a2"):
        z_ = np.zeros((2, 2, 128, 256), np.float32)
        for d_ in range(2):
            z_[:, d_, d_ * 64:(d_ + 1) * 64, :] = inputs["rwkv_" + nm][:, d_]
        m["r_" + nm] = z_
    m["r_g2"] = f(inputs["rwkv_g2"])
    m["r_lnwb"] = f(np.broadcast_to(inputs["rwkv_ln_w"][:, None, :], (2, 128, 256)))
    m["r_lnbb"] = f(np.broadcast_to(inputs["rwkv_ln_b"][:, None, :], (2, 128, 256)))
    return m


_HC = {}


def host_consts():
    if _HC:
        return _HC
    m = _HC
    bands = np.zeros((128, 20, 128), np.float32)
    for g, win in enumerate((2, 4, 8, 16)):
        Ls = 384
        B = np.zeros((Ls, Ls), np.float64)
        for t in range(Ls):
            lo, hi = max(t - win // 2, 0), min(t + win // 2, Ls)
            B[lo:hi, t] = 1.0 / (hi - lo)
            B[t, t] -= 1.0
        bands[:, g * 5 + 0] = B[0:128, 0:128]
        bands[:, g * 5 + 1] = B[128:256, 128:256]
        bands[:, g * 5 + 2] = B[256:384, 256:384]
        bands[:, g * 5 + 3] = B[0:128, 128:256]
        bands[:, g * 5 + 4] = B[256:384, 128:256]
    m["bands"] = bands
    Cq = np.zeros((128, 128), np.float32); Cq[0:64, 0:64] = 1.0 / 64; Cq[64:96, 64:96] = 1.0 / 32
    m["Cq"] = Cq
    Pr = np.zeros((128, 128), np.float32)
    for i in range(16):
        Pr[64 + 16 + i, 64 + i] = 1.0
        Pr[64 + i, 64 + 16 + i] = 1.0
    m["Prot"] = Pr
    tpos = np.arange(4096)
    row = (tpos // 64).astype(np.float32); col = (tpos % 64).astype(np.float32)
    inv = np.power(np.float32(10000.0), -np.arange(8, dtype=np.float32) / np.float32(8)).astype(np.float32)
    ang = np.concatenate([row[:, None] * inv, col[:, None] * inv], axis=-1).astype(np.float32)
    cs = np.zeros((32, 2, 4096), np.float32)
    cs[0:16, 0] = np.cos(ang).T; cs[16:32, 0] = np.cos(ang).T
    cs[0:16, 1] = -np.sin(ang).T; cs[16:32, 1] = np.sin(ang).T
    m["cs"] = cs
    blk = np.zeros((128, 128), np.float32); blk[0:64, 0:64] = 1; blk[64:, 64:] = 1
    m["blk64"] = blk
    ind2 = np.zeros((128, 2), np.float32); ind2[0:64, 0] = 1; ind2[64:, 1] = 1
    m["ind2"] = ind2
    rr, cc_ = np.meshgrid(np.arange(128), np.arange(128), indexing="ij")
    masks = np.stack([rr < cc_, rr <= cc_, rr > cc_, rr >= cc_], axis=1).astype(np.float32)
    m["masks"] = np.ascontiguousarray(masks)
    lv = np.zeros((128, 14, 128), np.float32)
    ii = np.arange(128)
    for li_, mlev in enumerate((1, 2, 4, 8, 16, 32, 64)):
        same = (ii[:, None] // (2 * mlev)) == (ii[None, :] // (2 * mlev))
        MA = same & ((ii[:, None] % (2 * mlev)) >= mlev) & ((ii[None, :] % (2 * mlev)) < mlev)
        lv[:, li_, :] = MA
        lv[:, 7 + li_, :] = MA.T
    m["lvmask"] = lv
    rmask = np.ones((128, 512), np.float32); rmask[:, ::128] = 0
    m["rmask"] = rmask
    return m


def kernel(**inputs):
    nc = bass.Bass("TRN2", target_bir_lowering=False)
    with contextlib.ExitStack() as es:
        build(nc, es)
    in_maps = [host_inputs(inputs, b) for b in range(8)]
    res = run_bass_kernel_spmd(nc, in_maps, core_ids=list(range(8)))
    return np.stack([np.asarray(r["out"], dtype=np.float32) for r in res.results], axis=0)


_PE_SIG = {}


def mm(C, out_tk, out_ap, l_tk, l_ap, r_tk, r_ap, start=True, stop=True):
    sig = (l_ap.base_partition(), l_ap.partition_size())
    prev = _PE_SIG.get(id(out_tk))
    sw = None
    if prev is not None and prev[0] is out_tk and prev[1] != sig and prev[1][1] < 128 and sig[1] < 128:
        sw = prev[2]
    tok = C.K.ins("pe", [l_tk, r_tk], [out_tk], C.nc.tensor.matmul, out_ap, lhsT=l_ap, rhs=r_ap, start=start, stop=stop, _selfwait=sw)
    _PE_SIG[id(out_tk)] = (out_tk, sig, tok)


def cp(C, eng, out_tk, out_ap, in_tk, in_ap):
    if eng == "act":
        C.K.ins("act", [in_tk], [out_tk], C.nc.scalar.copy, out=out_ap, in_=in_ap)
    elif eng == "dve":
        C.K.ins("dve", [in_tk], [out_tk], C.nc.vector.tensor_copy, out=out_ap, in_=in_ap)
    else:
        C.K.ins("pool", [in_tk], [out_tk], C.nc.gpsimd.tensor_copy, out=out_ap, in_=in_ap)


def tt(C, eng, out_tk, out_ap, a_tk, a_ap, b_tk, b_ap, op):
    f = C.nc.vector.tensor_tensor if eng == "dve" else C.nc.gpsimd.tensor_tensor
    C.K.ins(eng, [a_tk, b_tk], [out_tk], f, out=out_ap, in0=a_ap, in1=b_ap, op=op)


def stt(C, out_tk, out_ap, a_tk, a_ap, sc_tk, sc, b_tk, b_ap, op0, op1):
    rd = [a_tk, b_tk] + ([sc_tk] if sc_tk is not None else [])
    C.K.ins("dve", rd, [out_tk], C.nc.vector.scalar_tensor_tensor, out=out_ap, in0=a_ap, scalar=sc, in1=b_ap, op0=op0, op1=op1)


def act(C, out_tk, out_ap, in_tk, in_ap, func, extra=(), **kw):
    C.K.ins("act", [in_tk] + list(extra), [out_tk], C.nc.scalar.activation, out=out_ap, in_=in_ap, func=func, **kw)


def seq_tiles(last_layer, n=512):
    return tiles_of(n, last_layer)


def phase_P(C, li):
    K, nc, I = C.K, C.nc, C.I
    with contextlib.ExitStack() as st:
        uall = K.sb("p_u", [128, 34, 256], BF16, st)
        bands = K.sb("p_bands", [128, 20, 128], BF16, st)
        pw = K.sb("p_pw", [128, 2, 128], BF16, st)
        psc = K.sb("p_sc", [128, 2], F32, st)
        pooled = K.sb("p_pooled", [128, 2, 512], BF16, st)
        ost = K.sb("p_ost", [128, 2, 512], BF16, st)
        psg = [K.ps(f"p_psg{j}", [128, 512], F32, st) for j in range(4)]
        psy = [K.ps(f"p_psy{j}", [128, 512], F32, st) for j in range(2)]
        K.dma("sp", uall[:], C.u_d[:, :].rearrange("(j p) c -> p j c", p=128), [C.u_d], uall)
        K.dma("pool", bands[:], I["bands"][:, :, :], [I["bands"]], bands)
        K.dma("pool", pw[:], I["pw_bd"][li].rearrange("a p d -> p a d"), [I["pw_bd"]], pw)
        K.dma("sp", psc[:], I["pscale"][:, li, :], [I["pscale"]], psc)
        for (t0, n, s) in tiles_of(512):
            j0 = t0 // 128
            first, lastj = (0, 1) if s else (2, 33)
            for a in range(2):
                for gi in range(2):
                    g = 2 * a + gi
                    pg = psg[g]
                    for jj in range(n // 128):
                        j = j0 + jj
                        srcs = []
                        if j > first:
                            srcs.append((j - 1, 3))
                        srcs.append((j, 0 if j == first else (2 if j == lastj else 1)))
                        if j < lastj:
                            srcs.append((j + 1, 4))
                        for si, (sj, kind) in enumerate(srcs):
                            mm(C, pg, pg[:, jj * 128:(jj + 1) * 128], uall, uall[:, sj, a * 128:(a + 1) * 128], bands, bands[:, g * 5 + kind, :], start=(si == 0), stop=(si == len(srcs) - 1))
                    cp(C, "dve" if gi == 0 else "act", pooled, pooled[gi * 64:(gi + 1) * 64, a, :n], pg, pg[gi * 64:(gi + 1) * 64, :n])
                py = psy[a]
                mm(C, py, py[:, :n], pw, pw[:, a, :], pooled, pooled[:, a, :n])
                K.ins("dve", [py, psc], [ost], nc.vector.tensor_scalar, out=ost[:, a, :n], in0=py[:, :n], scalar1=psc[:, a:a + 1], scalar2=None, op0=ALU.mult)
            K.dma("sp", C.obr[0:256, t0:t0 + n].rearrange("(a p) t -> p a t", p=128), ost[:, :, :n], [ost], C.obr)


def rms_feat(C, x, ncn, n, gain, out, psS, rstd, wk, dim):
    K, nc = C.K, C.nc
    act(C, wk, wk[:, :ncn, :n], x, x[:, :ncn, :n], AF.Square)
    for c in range(ncn):
        mm(C, psS, psS[:, :n], C.ones, C.ones[:], wk, wk[:, c, :n], start=(c == 0), stop=(c == ncn - 1))
    act(C, rstd, rstd[:, :n], psS, psS[:, :n], AF.Ln, extra=[C.epsb], scale=1.0 / dim, bias=C.epsb[:, 0:1])
    act(C, rstd, rstd[:, :n], rstd, rstd[:, :n], AF.Exp, scale=-0.5)
    for c in range(ncn):
        stt(C, out, out[:, c, :n], x, x[:, c, :n], gain, gain[:, c:c + 1], rstd, rstd[:, :n], ALU.mult, ALU.mult)


def run_pool(factories, slots, extra=(), extra_steps=1):
    free = list(slots)
    active = []
    extra = list(extra)
    pend = list(factories)
    while pend or active or extra:
        while pend and free:
            sl = free.pop(0)
            active.append((pend.pop(0)(sl), sl))
        for item in list(active):
            try:
                next(item[0])
            except StopIteration:
                active.remove(item)
                free.append(item[1])
        for g in list(extra):
            try:
                for _ in range(extra_steps):
                    next(g)
            except StopIteration:
                extra.remove(g)


def phase_MLA(C, li):
    K, nc, I = C.K, C.nc, C.I
    last = li == 1
    SC = 96.0 ** -0.5
    with contextlib.ExitStack() as st:
        KT = K.sb("l_KT", [128, 8, T_ALL], BF16, st)
        KTh = [Tk(f"l_KT_h{h}", KT.t) for h in range(8)]
        V = K.sb("l_V", [128, 34, 8, 65], BF16, st)
        wuq = K.sb("l_wuq", [128, 3, 768], BF16, st)
        wk_ = K.sb("l_wukvk", [128, 2, 512], BF16, st)
        wv_ = K.sb("l_wukvv", [128, 2, 512], BF16, st)
        Cq = K.sb("l_Cq", [128, 128], F32, st)
        Prot = K.sb("l_Prot", [128, 128], BF16, st)
        gq = K.sb("l_gq", [128, 1], F32, st)
        gk = K.sb("l_gk", [128, 1], F32, st)
        qng = K.sb("l_qng", [128, 3], F32, st)
        kvng = K.sb("l_kvng", [128, 2], F32, st)
        load_w_bf16(C, wuq, lambda kc: I["w_uq"][li, kc * 128:(kc + 1) * 128, :], 3, I["w_uq"])
        load_w_bf16(C, wk_, lambda kc: I["w_ukv_k"][li, kc * 128:(kc + 1) * 128, :], 2, I["w_ukv_k"])
        load_w_bf16(C, wv_, lambda kc: I["w_ukv_v"][li, kc * 128:(kc + 1) * 128, :], 2, I["w_ukv_v"])
        K.dma("sp", Cq[:], I["Cq"][:, :], [I["Cq"]], Cq)
        K.dma("pool", Prot[:], I["Prot"][:, :], [I["Prot"]], Prot)
        gqf = K.sb("l_gqf", [128, 2], F32, st)
        gkf = K.sb("l_gkf", [128, 2], F32, st)
        K.dma("sp", gqf[:], I["gq96"][:, :], [I["gq96"]], gqf)
        K.dma("sp", gkf[:], I["gk96"][:, :], [I["gk96"]], gkf)
        cp(C, "dve", gq, gq[:], gqf, gqf[:, li:li + 1])
        cp(C, "dve", gk, gk[:], gkf, gkf[:, li:li + 1])
        K.dma("sp", qng[:], I["qng"][:, li, :], [I["qng"]], qng)
        K.dma("sp", kvng[:], I["kvng"][:, li, :], [I["kvng"]], kvng)
        xin = K.sb("l_xin", [128, 3, 512], F32, st)
        xkr = K.sb("l_xkr", [128, 512], F32, st)
        xn = K.sb("l_xn", [128, 3, 512], BF16, st)
        wk = K.sb("l_wk", [128, 3, 512], F32, st)
        rstd = K.sb("l_rstd", [128, 512], F32, st)
        cs = K.sb("l_cs", [128, 2, 512], F32, st)
        QTs = [K.sb(f"l_QT{j}", [128, 8, 512], BF16, st) for j in range(2)]
        QTh = [[Tk(f"l_QT{j}_h{h}", QTs[j].t) for h in range(8)] for j in range(2)]
        PT = [K.sb(f"l_PT{j}", [128, 512], BF16, st) for j in range(4)]
        rsb = K.sb("l_rsb", [128, 512], F32, st)
        bcs = K.sb("l_bcs", [128, 512], F32, st)
        ost = K.sb("l_ost", [128, 8, 512], BF16, st)
        psS = K.ps("l_psS", [128, 512], F32, st)
        psSc = [K.ps(f"l_psSc{j}", [128, 512], F32, st) for j in range(3)]
        psO = [K.ps(f"l_psO{j}", [128, 512], F32, st) for j in range(2)]
        psF = psS
        slots = []
        for j in range(2):
            slots.append(dict(
                raw=K.sb(f"l_raw{j}", [128, 512], F32, st), sq=K.sb(f"l_sq{j}", [128, 512], F32, st),
                r2=K.sb(f"l_r2{j}", [128, 512], F32, st), qn=K.sb(f"l_qn{j}", [128, 512], F32, st),
                qnb=K.sb(f"l_qnb{j}", [128, 512], BF16, st), tmp=K.sb(f"l_tmp{j}", [128, 512], F32, st),
                ps=K.ps(f"l_psH{j}", [128, 512], F32, st)))
        K.ins("dve", [], [V], nc.vector.memset, V[:, :, :, 64:65], 1.0)
        R96 = slice(64, 96)

        def head_norm_g(S_, rows, src_tk, src_ap, gain, n, lhs_ap):
            sq, r2, qn, ps = S_["sq"], S_["r2"], S_["qn"], S_["ps"]
            tt(C, "pool", sq, sq[rows, :n], src_tk, src_ap, src_tk, src_ap, ALU.mult)
            yield
            mm(C, ps, ps[0:96, :n], Cq, lhs_ap, sq, sq[rows, :n])
            yield
            act(C, r2, r2[rows, :n], ps, ps[rows, :n], AF.Ln, extra=[C.epsb], scale=1.0, bias=C.epsb[rows, 0:1])
            yield
            act(C, r2, r2[rows, :n], r2, r2[rows, :n], AF.Exp, scale=-0.5)
            yield
            stt(C, qn, qn[rows, :n], src_tk, src_ap, gain, gain[rows, 0:1], r2, r2[rows, :n], ALU.mult, ALU.mult)
            yield

        def rope_g(S_, rows, n, out_tk, out_ap):
            qn, qnb, tmp, ps = S_["qn"], S_["qnb"], S_["tmp"], S_["ps"]
            cp(C, "dve", qnb, qnb[rows, :n], qn, qn[rows, :n])
            yield
            mm(C, ps, ps[0:96, :n], Prot, Prot[rows, 0:96], qnb, qnb[rows, :n])
            yield
            tt(C, "dve", tmp, tmp[rows, :n], ps, ps[rows, :n], cs, cs[rows, 1, :n], ALU.mult)
            tt(C, "pool", qn, qn[rows, :n], qn, qn[rows, :n], cs, cs[rows, 0, :n], ALU.mult)
            yield
            tt(C, "dve", out_tk, out_ap, qn, qn[rows, :n], tmp, tmp[rows, :n], ALU.add)
            yield

        def k_head_f(h, t0, n):
            def g(S_):
                ps, raw, qn = S_["ps"], S_["raw"], S_["qn"]
                for kc in range(2):
                    mm(C, ps, ps[0:64, :n], wk_, wk_[:, kc, h * 64:(h + 1) * 64], xn, xn[:, kc, :n], start=(kc == 0), stop=(kc == 1))
                yield
                cp(C, "dve", raw, raw[0:64, :n], ps, ps[0:64, :n])
                yield
                yield from head_norm_g(S_, slice(0, 64), raw, raw[0:64, :n], gk, n, Cq[0:64, 0:96])
                cp(C, "pool", KTh[h], KT[0:64, h, t0:t0 + n], qn, qn[0:64, :n])
                yield
            return g

        def k_kr_f(t0, n, s):
            def g(S_):
                qn = S_["qn"]
                yield from head_norm_g(S_, R96, xkr, xkr[R96, :n], gk, n, Cq[64:96, 0:96])
                if not s:
                    yield from rope_g(S_, R96, n, qn, qn[R96, :n])
                K.ins("dve", [qn], KTh, nc.vector.tensor_copy, out=KT[64:96, :, t0:t0 + n], in_=qn[64:96, :n].unsqueeze(1).broadcast_to([32, 8, n]))
                yield
            return g

        for (t0, n, s) in tiles_of(512):
            K.dma("sp", xin[:, 0:2, :n], C.zT[3:5, :, t0:t0 + n].rearrange("m p t -> p m t"), [C.zT], xin)
            K.dma("sp", xkr[64:96, :n], C.zT[5, 0:32, t0:t0 + n], [C.zT], xkr)
            if not s:
                K.dma("sp", cs[64:96, :, :n], I["cs"][:, :, t0 - LC:t0 - LC + n], [I["cs"]], cs)
            rms_feat(C, xin, 2, n, kvng, xn, psS, rstd, wk, 256.0)
            for jj in range(n // 128):
                pa = psSc[jj % 2]
                for kc in range(2):
                    mm(C, pa, pa[:, :], xn, xn[:, kc, jj * 128:(jj + 1) * 128], wv_, wv_[:, kc, :], start=(kc == 0), stop=(kc == 1))
                cp(C, "act", V, V[:, t0 // 128 + jj, :, 0:64], pa, pa[:, :].rearrange("p (h d) -> p h d", d=64))
            run_pool([k_kr_f(t0, n, s)] + [k_head_f(h, t0, n) for h in range(8)], slots)

        def q_head_f(h, n, s, qb):
            def g(S_):
                ps, raw, qn = S_["ps"], S_["raw"], S_["qn"]
                QTt, QT = QTh[qb][h], QTs[qb]
                for kc in range(3):
                    mm(C, ps, ps[0:96, :n], wuq, wuq[:, kc, h * 96:(h + 1) * 96], xn, xn[:, kc, :n], start=(kc == 0), stop=(kc == 2))
                yield
                cp(C, "dve", raw, raw[0:96, :n], ps, ps[0:96, :n])
                yield
                yield from head_norm_g(S_, slice(0, 96), raw, raw[0:96, :n], gq, n, Cq[0:96, 0:96])
                cp(C, "pool", QTt, QT[0:64, h, :n], qn, qn[0:64, :n])
                if not s:
                    yield from rope_g(S_, R96, n, QTt, QT[R96, h, :n])
                else:
                    cp(C, "pool", QTt, QT[R96, h, :n], qn, qn[R96, :n])
                yield
            return g

        def q_prep(t0, n, s, qb):
            K.dma("sp", xin[:, 0:3, :n], C.zT[0:3, :, t0:t0 + n].rearrange("m p t -> p m t"), [C.zT], xin)
            if not s:
                K.dma("sp", cs[64:96, :, :n], I["cs"][:, :, t0 - LC:t0 - LC + n], [I["cs"]], cs)
            rms_feat(C, xin, 3, n, qng, xn, psS, rstd, wk, 384.0)
            return [q_head_f(h, n, s, qb) for h in range(8)]

        cnt = {"npt": 0}

        def attn_g(t0, n, s, qb):
            QT = QTs[qb]
            kts = [0, 1] if s else list(range(34))
            steps = [(h, ki, kt) for h in range(8) for ki, kt in enumerate(kts)]
            slot = {}

            def emit_S(i):
                h, ki, kt = steps[i]
                psc_ = psSc[cnt["npt"] % 3]
                pt = PT[cnt["npt"] % 4]
                cnt["npt"] += 1
                slot[i] = pt
                mm(C, psc_, psc_[:, :n], KTh[h], KT[0:96, h, kt * 128:(kt + 1) * 128], QTh[qb][h], QT[0:96, h, :n])
                act(C, pt, pt[:, :n], psc_, psc_[:, :n], AF.Exp, scale=SC)

            emit_S(0)
            if len(steps) > 1:
                emit_S(1)
            for i, (h, ki, kt) in enumerate(steps):
                if i + 2 < len(steps):
                    emit_S(i + 2)
                po = psO[h % 2]
                pt = slot.pop(i)
                mm(C, po, po[0:65, :n], V, V[:, kt, h, :], pt, pt[:, :n], start=(ki == 0), stop=(ki == len(kts) - 1))
                if ki == len(kts) - 1:
                    K.ins("dve", [po], [rsb], nc.vector.reciprocal, out=rsb[64:65, :n], in_=po[64:65, :n])
                    mm(C, psF, psF[0:64, :n], C.ones, C.ones[64:65, 0:64], rsb, rsb[64:65, :n])
                    cp(C, "dve", bcs, bcs[0:64, :n], psF, psF[0:64, :n])
                    tt(C, "dve", ost, ost[0:64, h, :n], po, po[0:64, :n], bcs, bcs[0:64, :n], ALU.mult)
                yield
            K.dma("sp", C.obr[256:768, t0:t0 + n].rearrange("(h p) t -> p h t", p=64), ost[0:64, :, :n], [ost], C.obr)

        qtiles = tiles_of(512, last)
        run_pool(q_prep(*qtiles[0], 0), slots)
        for i, (t0, n, s) in enumerate(qtiles):
            nxt = q_prep(*qtiles[i + 1], (i + 1) % 2) if i + 1 < len(qtiles) else []
            run_pool(nxt, slots, extra=[attn_g(t0, n, s, i % 2)], extra_steps=4)


def phase_R(C, li):
    K, nc, I = C.K, C.nc, C.I
    last = li == 1
    NCH = 34
    TN = 256
    MAXFLY = min(getattr(C, "maxfly", 6), 6)
    with contextlib.ExitStack() as st:
        C.dbgtk = getattr(C, "dbgtk", {})

        def sbt(name, shape, dt=F32):
            tk = K.sb(f"rw{li}_" + name, shape, dt, st)
            C.dbgtk[f"rw{li}_" + name] = tk
            return tk
        mu = sbt("mu", [128, 9, 2]); omm = sbt("omm", [128, 9])
        kkp = sbt("kkp", [128, 2]); w0 = sbt("w0", [128, 2, 2]); a0 = sbt("a0", [128, 2, 2]); ka = sbt("ka", [128, 2, 2]); omka = sbt("omka", [128, 2, 2])
        rk = sbt("rk", [128, 2]); w2 = sbt("w2", [128, 2, 256], BF16); a2 = sbt("a2", [128, 2, 256], BF16); zb7 = sbt("zb7", [128, TN], BF16)
        g2 = sbt("g2", [128, 256], BF16); lnw = sbt("lnw", [128, 256]); lnb = sbt("lnb", [128, 256])
        blk = sbt("blk", [128, 128]); ind2 = sbt("ind2", [128, 2]); masks = sbt("masks", [128, 4, 128]); rmask = sbt("rmask", [128, 512])
        lvm = sbt("lvm", [128, 14, 128], BF16)
        lvm2 = sbt("lvm2", [128, 14, 128], BF16)
        K.dma("sp", mu[:], I["r_mu"][:, li, :, :], [I["r_mu"]], mu)
        K.dma("sp", kkp[:], I["r_kk"][:, li, :], [I["r_kk"]], kkp)
        K.dma("sp", w0[:], I["r_w0"][:, li, :, :], [I["r_w0"]], w0)
        K.dma("sp", a0[:], I["r_a0"][:, li, :, :], [I["r_a0"]], a0)
        K.dma("sp", ka[:], I["r_ka"][:, li, :, :], [I["r_ka"]], ka)
        K.dma("sp", rk[:], I["r_rk"][:, li, :], [I["r_rk"]], rk)
        for d_ in range(2):
            K.dma("pool", w2[:, d_, :], I["r_w2"][li, d_], [I["r_w2"]], w2)
            K.dma("pool", a2[:, d_, :], I["r_a2"][li, d_], [I["r_a2"]], a2)
        K.dma("pool", g2[:], I["r_g2"][li], [I["r_g2"]], g2)
        K.dma("pool", lvm[:], I["lvmask"][:, :, :], [I["lvmask"]], lvm)
        K.dma("pool", lvm2[:, 0:7, :], I["lvmask"][:, 7:14, :], [I["lvmask"]], lvm2)
        K.dma("pool", lvm2[:, 7:14, :], I["lvmask"][:, 0:7, :], [I["lvmask"]], lvm2)
        K.dma("sp", lnw[:], I["r_lnwb"][li], [I["r_lnwb"]], lnw)
        K.dma("sp", lnb[:], I["r_lnbb"][li], [I["r_lnbb"]], lnb)
        K.dma("sp", blk[:], I["blk64"][:, :], [I["blk64"]], blk)
        K.dma("sp", ind2[:], I["ind2"][:, :], [I["ind2"]], ind2)
        K.dma("sp", masks[:], I["masks"][:, :, :], [I["masks"]], masks)
        K.dma("sp", rmask[:], I["rmask"][:, :], [I["rmask"]], rmask)
        tt(C, "dve", omm, omm[:], mu, mu[:, :, 0], mu, mu[:, :, 1], ALU.add)
        K.ins("dve", [omm], [omm], nc.vector.tensor_scalar, out=omm[:], in0=omm[:], scalar1=-1.0, scalar2=1.0, op0=ALU.mult, op1=ALU.add)
        K.ins("dve", [ka], [omka], nc.vector.tensor_scalar, out=omka[:], in0=ka[:], scalar1=-1.0, scalar2=1.0, op0=ALU.mult, op1=ALU.add)
        Yacc = sbt("Yacc", [128, NCH, 256]); Vtok = sbt("Vtok", [128, NCH, 256], BF16)
        VtokT = [Tk(f"Vtok_t{j}", Vtok.t) for j in range(NCH // 2)]
        bon = sbt("bon", [128, NCH, 4]); gst = sbt("gst", [128, 2, 256], BF16)
        S_ = [sbt(f"S{hp}", [128, 64]) for hp in range(2)]
        Sb_ = [sbt(f"Sb{hp}", [128, 64], BF16) for hp in range(2)]
        zr = sbt("zr", [128, 9, TN + 2]); zs = sbt("zs", [128, 9, TN])
        kkn = sbt("kkn", [128, 2, TN]); e1 = sbt("e1", [128, 2, TN]); e2 = sbt("e2", [128, 2, TN])
        xw = sbt("xw", [128, 2, TN]); aa = sbt("aa", [128, 2, TN]); kd = [sbt(f"kd{d}", [128, 2, TN]) for d in range(2)]
        bb = sbt("bb", [128, 2, TN]); cum = sbt("cum", [128, 2, TN]); tot = sbt("tot", [128, 2, 2])
        th = sbt("th", [128, TN], BF16); sgd = sbt("sgd", [128, TN], BF16)
        opsets = []
        for j in range(3):
            o = {nm: sbt(f"o{j}_" + nm, [128, 2, TN], BF16) for nm in ("at", "rt", "bt", "kt", "bh", "kh")}
            o["WC"] = sbt(f"o{j}_WC", [128, 2, 2])
            opsets.append(o)
        banks = [K.ps(f"r_bank{j}", [128, 512], F32, st) for j in range(8)]
        bsets = []
        for j in range(MAXFLY):
            b = dict(
                AT=sbt(f"u{j}_AT", [128, 2, 512], BF16), Lc=[sbt(f"u{j}_Lc{q}", [128, 2, 2, 128], BF16) for q in range(2)],
                QT=[sbt(f"u{j}_QT{q}", [128, 2, 2, 128], BF16) for q in range(2)], ZZ=sbt(f"u{j}_ZZ", [128, 2, 2, 128], BF16),
                P=sbt(f"u{j}_P", [128, 2, 128], BF16), tok3=sbt(f"u{j}_tok3", [128, 3, 128], BF16), rGH=sbt(f"u{j}_rGH", [128, 2, 128], BF16),
                NX=sbt(f"u{j}_NX", [128, 512], BF16), GHs=sbt(f"u{j}_GHs", [128, 2, 128], BF16), GT=sbt(f"u{j}_GT", [128, 128], BF16), Us=sbt(f"u{j}_Us", [128, 2, 64], BF16),
                ps=(banks[2 + j], banks[2 + j]))
            bsets.append(b)
        psW = [banks[0], banks[1]]
        psQ = [banks[0], banks[1]]
        ytmp = sbt("ytmp", [128, 256]); otok = sbt("otok", [128, 256]); st4 = sbt("st4", [128, 4]); st4b = sbt("st4b", [128, 4])
        obst = sbt("obst", [128, 2, 128], BF16); gl = sbt("gl", [128, 256], BF16)

        def prep(t0, n, s, d, first_pass, oset):
            s0, s1 = (0, LC) if s else (LC, T_ALL)
            lo, hi = max(t0 - 1, s0), min(t0 + n + 1, s1)
            if lo == t0 or hi == t0 + n:
                K.ins("pool", [], [zr], nc.gpsimd.memset, zr[:, :, :], 0.0)
            off = lo - (t0 - 1)
            K.dma("sp", zr[:, :, off:off + hi - lo], C.zT[6:15, :, lo:hi].rearrange("m p t -> p m t"), [C.zT], zr)
            yield
            for c in range(9):
                K.ins("dve", [zr, omm], [zs], nc.vector.tensor_scalar, out=zs[:, c, :n], in0=zr[:, c, 1:n + 1], scalar1=omm[:, c:c + 1], scalar2=None, op0=ALU.mult)
                stt(C, zs, zs[:, c, :n], zr, zr[:, c, 0:n], mu, mu[:, c, 0:1], zs, zs[:, c, :n], ALU.mult, ALU.add)
                stt(C, zs, zs[:, c, :n], zr, zr[:, c, 2:n + 2], mu, mu[:, c, 1:2], zs, zs[:, c, :n], ALU.mult, ALU.add)
                if c % 3 == 2:
                    yield
            r_, k_ = zs[:, 0:2, :n], zs[:, 2:4, :n]
            for hp in range(2):
                K.ins("dve", [zs, kkp], [kkn], nc.vector.tensor_scalar, out=kkn[:, hp, :n], in0=zs[:, 2 + hp, :n], scalar1=kkp[:, hp:hp + 1], scalar2=None, op0=ALU.mult)
            tt(C, "pool", e1, e1[:, :, :n], kkn, kkn[:, :, :n], kkn, kkn[:, :, :n], ALU.mult)
            yield
            for hp in range(2):
                pw_ = psW[hp]
                mm(C, pw_, pw_[:, :n], blk, blk[:], e1, e1[:, hp, :n])
                K.ins("dve", [pw_], [e2], nc.vector.tensor_scalar, out=e2[:, hp, :n], in0=pw_[:, :n], scalar1=1e-24, scalar2=None, op0=ALU.max)
            act(C, e2, e2[:, :, :n], e2, e2[:, :, :n], AF.Ln)
            yield
            act(C, e2, e2[:, :, :n], e2, e2[:, :, :n], AF.Exp, scale=-0.5)
            tt(C, "dve", kkn, kkn[:, :, :n], kkn, kkn[:, :, :n], e2, e2[:, :, :n], ALU.mult)
            yield
            dirs = (0, 1) if first_pass else (d,)
            cp(C, "act", zb7, zb7[:, :n], zs, zs[:, 7, :n])
            yield
            for dd in dirs:
                for hp in range(2):
                    pw_ = psW[hp]
                    mm(C, pw_, pw_[:, :n], a2, a2[:, dd, hp * 128:(hp + 1) * 128], zb7, zb7[:, :n])
                    K.ins("dve", [pw_, a0], [aa], nc.vector.tensor_scalar, out=aa[:, hp, :n], in0=pw_[:, :n], scalar1=a0[:, dd, hp:hp + 1], scalar2=None, op0=ALU.add)
                    act(C, aa, aa[:, hp, :n], aa, aa[:, hp, :n], AF.Sigmoid)
                    K.ins("dve", [aa, ka], [e1], nc.vector.tensor_scalar, out=e1[:, hp, :n], in0=aa[:, hp, :n], scalar1=ka[:, dd, hp:hp + 1], scalar2=None, op0=ALU.mult)
                    K.ins("dve", [e1, omka], [e1], nc.vector.tensor_scalar, out=e1[:, hp, :n], in0=e1[:, hp, :n], scalar1=omka[:, dd, hp:hp + 1], scalar2=None, op0=ALU.add)
                    yield
                tt(C, "dve", kd[dd], kd[dd][:, :, :n], e1, e1[:, :, :n], zs, k_, ALU.mult)
                if dd == d:
                    tt(C, "pool", bb, bb[:, :, :n], kkn, kkn[:, :, :n], aa, aa[:, :, :n], ALU.mult)
            if first_pass:
                tidx = t0 // TN
                tt(C, "pool", e1, e1[:, :, :n], kd[0], kd[0][:, :, :n], kd[1], kd[1][:, :, :n], ALU.add)
                tt(C, "dve", e1, e1[:, :, :n], e1, e1[:, :, :n], zs, r_, ALU.mult)
                for hp in range(2):
                    K.ins("dve", [e1, rk], [e1], nc.vector.tensor_scalar, out=e1[:, hp, :n], in0=e1[:, hp, :n], scalar1=rk[:, hp:hp + 1], scalar2=0.5, op0=ALU.mult, op1=ALU.mult)
                act(C, sgd, sgd[:, :n], zs, zs[:, 8, :n], AF.Sigmoid)
                yield
                for jj in range(n // 128):
                    cidx = t0 // 128 + jj
                    pq = psQ[jj % 2]
                    for hp in range(2):
                        mm(C, pq, pq[:, 256 + 2 * hp:258 + 2 * hp], e1, e1[:, hp, jj * 128:(jj + 1) * 128], ind2, ind2[:, :])
                    mm(C, pq, pq[:, 0:256], sgd, sgd[:, jj * 128:(jj + 1) * 128], g2, g2[:, :])
                    cp(C, "act", gst, gst[:, jj, :], pq, pq[:, 0:256])
                    cp(C, "dve", bon, bon[:, cidx, :], pq, pq[:, 256:260])
                    pv = psW[jj % 2]
                    for hp in range(2):
                        K.ins("pe", [zs, C.ident], [pv], nc.tensor.transpose, out=pv[:, hp * 128:(hp + 1) * 128], in_=zs[:, 4 + hp, jj * 128:(jj + 1) * 128], identity=C.ident[:])
                    cp(C, "act", VtokT[tidx], Vtok[:, cidx, :], pv, pv[:, 0:256])
                    yield
                K.dma("sp", C.gtok_d[t0:t0 + n, :].rearrange("(j p) c -> p j c", p=128), gst[:, :n // 128, :], [gst], C.gtok_d)
            act(C, th, th[:, :n], zs, zs[:, 6, :n], AF.Tanh)
            yield
            for hp in range(2):
                pw_ = psW[hp]
                mm(C, pw_, pw_[:, :n], w2, w2[:, d, hp * 128:(hp + 1) * 128], th, th[:, :n])
                K.ins("dve", [pw_, w0], [xw], nc.vector.tensor_scalar, out=xw[:, hp, :n], in0=pw_[:, :n], scalar1=w0[:, d, hp:hp + 1], scalar2=None, op0=ALU.add)
                act(C, xw, xw[:, hp, :n], xw, xw[:, hp, :n], AF.Sigmoid)
                yield
            K.ins("dve", [xw], [xw], nc.vector.tensor_scalar, out=xw[:, :, :n], in0=xw[:, :, :n], scalar1=-0.6065306597126334, scalar2=None, op0=ALU.mult)
            yield
            nchk = n // 128
            for hp in range(2):
                K.ins("dve", [rmask, xw], [cum], nc.vector.tensor_tensor_scan, out=cum[:, hp, :n], data0=rmask[:, :n], data1=xw[:, hp, :n], initial=0.0, op0=ALU.mult, op1=ALU.add)
            cum4 = cum[:, :, :n].rearrange("p h (c t) -> p h c t", t=128)
            cp(C, "dve", tot, tot[:, :, :nchk], cum, cum4[:, :, :, 127])
            yield
            totb = tot[:, :, :nchk].unsqueeze(3).broadcast_to([128, 2, nchk, 128])
            v4 = lambda tk: tk[:, :, :n].rearrange("p h (c t) -> p h c t", t=128)
            if d == 0:
                tt(C, "pool", e2, e2[:, :, :n], cum, cum[:, :, :n], xw, xw[:, :, :n], ALU.subtract)
                tt(C, "dve", e1, v4(e1), tot, totb, cum, cum4, ALU.subtract)
            else:
                tt(C, "dve", e2, v4(e2), tot, totb, cum, cum4, ALU.subtract)
                tt(C, "pool", e1, e1[:, :, :n], cum, cum[:, :, :n], xw, xw[:, :, :n], ALU.subtract)
                tt(C, "dve", cum, cum[:, :, :n], e2, e2[:, :, :n], xw, xw[:, :, :n], ALU.add)
            ci = cum[:, :, :n]
            WC = oset["WC"]
            act(C, WC, WC[:, :, :nchk], tot, tot[:, :, :nchk], AF.Exp)
            yield
            act(C, e2, e2[:, :, :n], e2, e2[:, :, :n], AF.Exp)
            stt(C, oset["at"], oset["at"][:, :, :n], kkn, kkn[:, :, :n], None, -1.0, e2, e2[:, :, :n], ALU.mult, ALU.mult)
            yield
            act(C, e2, e2[:, :, :n], cum, ci, AF.Exp)
            tt(C, "dve", oset["rt"], oset["rt"][:, :, :n], zs, r_, e2, e2[:, :, :n], ALU.mult)
            yield
            act(C, e2, e2[:, :, :n], cum, ci, AF.Exp, scale=-1.0)
            tt(C, "dve", oset["bt"], oset["bt"][:, :, :n], bb, bb[:, :, :n], e2, e2[:, :, :n], ALU.mult)
            tt(C, "pool", oset["kt"], oset["kt"][:, :, :n], kd[d], kd[d][:, :, :n], e2, e2[:, :, :n], ALU.mult)
            yield
            act(C, e1, e1[:, :, :n], e1, e1[:, :, :n], AF.Exp)
            tt(C, "dve", oset["bh"], oset["bh"][:, :, :n], bb, bb[:, :, :n], e1, e1[:, :, :n], ALU.mult)
            tt(C, "pool", oset["kh"], oset["kh"][:, :, :n], kd[d], kd[d][:, :, :n], e1, e1[:, :, :n], ALU.mult)

        done = {0: 0, 1: 0}

        free_sets = list(range(MAXFLY))

        def unit(cidx, jj, d, hp, emit_y, first_pass, oset, B, myn):
            if getattr(C, "rdbg", 9) < 2:
                done[hp] += 1
                return
            bidx = free_sets.pop(0)
            B = bsets[bidx]
            T_ = slice(jj * 128, (jj + 1) * 128)
            mS = 0 if d == 0 else 2
            at, rt, bt, kt, bh, kh, WC = (oset[nm] for nm in ("at", "rt", "bt", "kt", "bh", "kh", "WC"))
            AT, Lcs, QTs, ZZ, P, tok3, rGH, GHs, GT, Us = (B[nm] for nm in ("AT", "Lc", "QT", "ZZ", "P", "tok3", "rGH", "GHs", "GT", "Us"))
            pA, pB = B["ps"]
            if getattr(C, "onebank", False):
                pB = pA
            S, Sb = S_[hp], Sb_[hp]
            Vt = VtokT[cidx // 2]
            vsl = lambda hh: Vtok[:, cidx, hp * 128 + hh * 64: hp * 128 + hh * 64 + 64]
            mview = masks[:, mS:mS + 2, :].unsqueeze(1).broadcast_to([128, 2, 2, 128])
            for hh in range(2):
                R = slice(hh * 64, hh * 64 + 64)
                pa = pA if hh == 0 else pB
                mm(C, pa, pa[:, 0:128], bt, bt[R, hp, T_], at, at[R, hp, T_])
                mm(C, pa, pa[:, 128:256], bt, bt[R, hp, T_], rt, rt[R, hp, T_])
                mm(C, pa, pa[:, 256:384], kt, kt[R, hp, T_], at, at[R, hp, T_])
                mm(C, pa, pa[:, 384:512], kt, kt[R, hp, T_], rt, rt[R, hp, T_])
                tt(C, "dve", AT, AT[:, hh, :].rearrange("p (a m t) -> p a m t", a=2, m=2), pa, pa[:, :].rearrange("p (a m t) -> p a m t", a=2, m=2), masks, mview, ALU.mult)
            yield
            pn = pA
            for hh in range(2):
                R = slice(hh * 64, hh * 64 + 64)
                mm(C, pn, pn[:, hh * 128:(hh + 1) * 128], at, at[R, hp, T_], bt, bt[R, hp, T_])
                mm(C, pn, pn[:, 256 + hh * 128:384 + hh * 128], bt, bt[R, hp, T_], at, at[R, hp, T_])
            NX = B["NX"]
            cp(C, "act", NX, NX[:, :], pn, pn[:, :])
            yield
            lvD = lvm if d == 0 else lvm2
            nx4 = NX[:, :].rearrange("p (a h t) -> p a h t", a=2, h=2)
            lvl_no = [0]

            def make_L(lvl):
                Lc = Lcs[lvl % 2]
                mk = lvD[:, :, :].rearrange("p (a l) t -> p a l t", a=2)[:, :, lvl, :].unsqueeze(2).broadcast_to([128, 2, 2, 128])
                tt(C, "dve" if lvl % 2 == 0 else "pool", Lc, Lc[:, :, :, :], NX, nx4, lvD, mk, ALU.mult)
                return Lc

            idb = C.identb[:].unsqueeze(1).broadcast_to([128, 2, 128])
            Lc = make_L(0)
            QT = QTs[0]
            tt(C, "pool", QT, QT[:, :, 0, :], Lc, Lc[:, 1, :, :], C.identb, idb, ALU.add)
            tt(C, "pool", QT, QT[:, :, 1, :], Lc, Lc[:, 0, :, :], C.identb, idb, ALU.add)
            Lc = make_L(1)
            yield
            curq = 0
            pz, pq_ = pB, pA
            for lvl in range(1, 6):
                QT, QTn = QTs[curq], QTs[curq ^ 1]
                for hh in range(2):
                    mm(C, pz, pz[:, (hh * 2) * 128:(hh * 2 + 1) * 128], Lc, Lc[:, 0, hh, :], QT, QT[:, hh, 0, :])
                    mm(C, pz, pz[:, (hh * 2 + 1) * 128:(hh * 2 + 2) * 128], Lc, Lc[:, 1, hh, :], QT, QT[:, hh, 1, :])
                cp(C, "act", ZZ, ZZ[:, :, :, :], pz, pz[:, :].rearrange("p (h a t) -> p h a t", h=2, a=2))
                Lc = make_L(lvl + 1)
                yield
                for hh in range(2):
                    o0 = pq_[:, (hh * 2) * 128:(hh * 2 + 1) * 128]
                    o1 = pq_[:, (hh * 2 + 1) * 128:(hh * 2 + 2) * 128]
                    mm(C, pq_, o0, QT, QT[:, hh, 1, :], ZZ, ZZ[:, hh, 0, :], start=True, stop=False)
                    mm(C, pq_, o0, C.identb, C.identb[:], QT, QT[:, hh, 0, :], start=False, stop=True)
                    mm(C, pq_, o1, QT, QT[:, hh, 0, :], ZZ, ZZ[:, hh, 1, :], start=True, stop=False)
                    mm(C, pq_, o1, C.identb, C.identb[:], QT, QT[:, hh, 1, :], start=False, stop=True)
                cp(C, "act", QTn, QTn[:, :, :, :], pq_, pq_[:, :].rearrange("p (h a t) -> p h a t", h=2, a=2))
                curq ^= 1
                yield
            QT = QTs[curq]
            for hh in range(2):
                mm(C, pz, pz[:, hh * 128:(hh + 1) * 128], Lc, Lc[:, 0, hh, :], QT, QT[:, hh, 0, :])
            cp(C, "act", ZZ, ZZ[:, :, 0, :], pz, pz[:, 0:256].rearrange("p (h t) -> p h t", h=2))
            yield
            for hh in range(2):
                o0 = pq_[:, hh * 128:(hh + 1) * 128]
                mm(C, pq_, o0, QT, QT[:, hh, 1, :], ZZ, ZZ[:, hh, 0, :], start=True, stop=False)
                mm(C, pq_, o0, C.identb, C.identb[:], QT, QT[:, hh, 0, :], start=False, stop=True)
            cp(C, "act", P, P[:, :, :], pq_, pq_[:, 0:256].rearrange("p (h t) -> p h t", h=2))
            yield
            pt_ = pB
            for i3, src in enumerate((at, bh, kh)):
                mm(C, pt_, pt_[:, i3 * 128:(i3 + 1) * 128], src, src[:, hp, T_], C.identb, C.identb[:])
            cp(C, "act", tok3, tok3[:, :, :], pt_, pt_[:, 0:384].rearrange("p (i f) -> p i f", i=3))
            yield
            pzz = pA
            for hh in range(2):
                mm(C, pzz, pzz[:, hh * 64:(hh + 1) * 64], AT, AT[:, hh, 256:384], Vt, vsl(hh))
            for hh in range(2):
                cp(C, "pool", rGH, rGH[:, hh, 0:64], tok3, tok3[:, 0, hh * 64:(hh + 1) * 64])
            cp(C, "dve", rGH, rGH[:, :, 64:128], pzz, pzz[:, 0:128].rearrange("p (h v) -> p h v", h=2))
            yield
            pg = pB
            for hh in range(2):
                mm(C, pg, pg[:, hh * 128:(hh + 1) * 128], P, P[:, hh, :], rGH, rGH[:, hh, :])
                mm(C, pg, pg[hh * 64:(hh + 1) * 64, 256:384], tok3, tok3[:, 0, hh * 64:(hh + 1) * 64], P, P[:, hh, :])
            cp(C, "act", GHs, GHs[:, :, :], pg, pg[:, 0:256].rearrange("p (h f) -> p h f", h=2))
            cp(C, "dve", GT, GT[:, :], pg, pg[:, 256:384])
            yield
            while done[hp] < myn:
                yield
            pu = pA
            for hh in range(2):
                R = slice(hh * 64, hh * 64 + 64)
                mm(C, pu, pu[:, hh * 64:(hh + 1) * 64], C.identb, C.identb[:], GHs, GHs[:, hh, 64:128], start=True, stop=False)
                mm(C, pu, pu[:, hh * 64:(hh + 1) * 64], GT, GT[R, :], Sb, Sb[R, :], start=False, stop=True)
            cp(C, "act", Us, Us[:, :, :], pu, pu[:, 0:128].rearrange("p (h v) -> p h v", h=2))
            yield
            if emit_y:
                py = pB
                for hh in range(2):
                    R = slice(hh * 64, hh * 64 + 64)
                    o_ = py[:, hh * 64:(hh + 1) * 64]
                    mm(C, py, o_, rt, rt[R, hp, T_], Sb, Sb[R, :], start=True, stop=False)
                    mm(C, py, o_, AT, AT[:, hh, 128:256], Us, Us[:, hh, :], start=False, stop=False)
                    mm(C, py, o_, AT, AT[:, hh, 384:512], Vt, vsl(hh), start=False, stop=True)
                if first_pass:
                    cp(C, "dve", Yacc, Yacc[:, cidx, hp * 128:(hp + 1) * 128], py, py[:, 0:128])
                else:
                    tt(C, "dve", Yacc, Yacc[:, cidx, hp * 128:(hp + 1) * 128], py, py[:, 0:128], Yacc, Yacc[:, cidx, hp * 128:(hp + 1) * 128], ALU.add)
            ps_ = pA
            for hh in range(2):
                o_ = ps_[hh * 64:(hh + 1) * 64, 256:320]
                mm(C, ps_, o_, tok3, tok3[:, 1, hh * 64:(hh + 1) * 64], Us, Us[:, hh, :], start=True, stop=False)
                mm(C, ps_, o_, tok3, tok3[:, 2, hh * 64:(hh + 1) * 64], Vt, vsl(hh), start=False, stop=True)
            stt(C, S, S[:, :], S, S[:, :], WC, WC[:, hp, jj:jj + 1], ps_, ps_[:, 256:320], ALU.mult, ALU.add)
            cp(C, "act", Sb, Sb[:, :], S, S[:, :])
            done[hp] += 1
            free_sets.append(bidx)
            yield

        PSTEPS = getattr(C, "psteps", 1)
        PEVERY = getattr(C, "pevery", 1)

        def run_pass(d):
            tl = tiles_of(TN)
            order = tl if d == 0 else [tl[0]] + tl[:0:-1]
            done[0] = done[1] = 0
            seq = {0: 0, 1: 0}
            active = []
            state = {"prep": None}

            def step_all():
                for g in list(active):
                    try:
                        next(g)
                    except StopIteration:
                        active.remove(g)
                state["round"] = state.get("round", 0) + 1
                if state["prep"] is not None and state["round"] % PEVERY == 0:
                    try:
                        for _ in range(PSTEPS):
                            next(state["prep"])
                    except StopIteration:
                        state["prep"] = None

            def mk_prep(ti):
                t0, n, s = order[ti]
                return prep(t0, n, s, d, d == 0, opsets[ti % 3])

            for _ in mk_prep(0):
                pass
            for ti, (t0, n, s) in enumerate(order):
                if state["prep"] is not None:
                    for _ in state["prep"]:
                        pass
                state["prep"] = mk_prep(ti + 1) if ti + 1 < len(order) else None
                oset = opsets[ti % 3]
                jjs = list(range(n // 128))
                if d == 1:
                    jjs = jjs[::-1]
                for jj in jjs:
                    for hp in range(2):
                        active.append(unit(t0 // 128 + jj, jj, d, hp, not (last and s), d == 0, oset, None, seq[hp]))
                        seq[hp] += 1
                        while len(active) >= MAXFLY:
                            step_all()
            while active:
                step_all()

        for d in range(2):
            for hp in range(2):
                K.ins("dve", [], [S_[hp]], nc.vector.memset, S_[hp][:], 0.0)
                K.ins("dve", [], [Sb_[hp]], nc.vector.memset, Sb_[hp][:], 0.0)
            run_pass(d)
        for cidx in range(2 if last else 0, NCH if getattr(C, "rdbg", 9) >= 8 else 0):
            K.dma("sp", gl[:, :], C.gtok_d[cidx * 128:(cidx + 1) * 128, :], [C.gtok_d], gl)
            y4 = Yacc[:, cidx, :].rearrange("p (h v) -> p h v", h=4)
            K.ins("dve", [Yacc], [st4], nc.vector.tensor_reduce, out=st4[:, :], in_=y4, axis=AX.X, op=ALU.add)
            K.ins("dve", [st4], [st4], nc.vector.tensor_scalar, out=st4[:, :], in0=st4[:, :], scalar1=1.0 / 64.0, scalar2=None, op0=ALU.mult)
            yt4 = ytmp[:, :].rearrange("p (h v) -> p h v", h=4)
            tt(C, "dve", ytmp, yt4, Yacc, y4, st4, st4[:, :].unsqueeze(2).broadcast_to([128, 4, 64]), ALU.subtract)
            tt(C, "pool", otok, otok[:, :], ytmp, ytmp[:, :], ytmp, ytmp[:, :], ALU.mult)
            K.ins("dve", [otok], [st4b], nc.vector.tensor_reduce, out=st4b[:, :], in_=otok[:, :].rearrange("p (h v) -> p h v", h=4), axis=AX.X, op=ALU.add)
            act(C, st4b, st4b[:, :], st4b, st4b[:, :], AF.Sqrt, extra=[C.gneps], scale=1.0 / 64.0, bias=C.gneps[:, 0:1])
            K.ins("dve", [st4b], [st4b], nc.vector.reciprocal, out=st4b[:, :], in_=st4b[:, :])
            tt(C, "dve", ytmp, yt4, ytmp, yt4, st4b, st4b[:, :].unsqueeze(2).broadcast_to([128, 4, 64]), ALU.mult)
            tt(C, "pool", ytmp, ytmp[:, :], ytmp, ytmp[:, :], lnw, lnw[:, :], ALU.mult)
            tt(C, "dve", ytmp, ytmp[:, :], ytmp, ytmp[:, :], lnb, lnb[:, :], ALU.add)
            tt(C, "dve", otok, otok[:, :].rearrange("p (h v) -> p h v", h=4), Vtok, Vtok[:, cidx, :].rearrange("p (h v) -> p h v", h=4), bon, bon[:, cidx, :].unsqueeze(2).broadcast_to([128, 4, 64]), ALU.mult)
            tt(C, "pool", ytmp, ytmp[:, :], ytmp, ytmp[:, :], otok, otok[:, :], ALU.add)
            tt(C, "dve", otok, otok[:, :], ytmp, ytmp[:, :], gl, gl[:, :], ALU.mult)
            po = psQ[cidx % 2]
            for hp in range(2):
                K.ins("pe", [otok, C.ident], [po], nc.tensor.transpose, out=po[:, hp * 128:(hp + 1) * 128], in_=otok[:, hp * 128:(hp + 1) * 128], identity=C.ident[:])
            cp(C, "act", obst, obst[:, :, :], po, po[:, 0:256].rearrange("p (h t) -> p h t", h=2))
            K.dma("sp", C.obr[768:1024, cidx * 128:(cidx + 1) * 128].rearrange("(h p) t -> p h t", p=128), obst[:, :, :], [obst], C.obr)
```

```python
import numpy as np
import concourse.bass as bass
import concourse.mybir as mybir

F32 = mybir.dt.float32
BF16 = mybir.dt.bfloat16
AF = mybir.ActivationFunctionType
ALU = mybir.AluOpType
AX = mybir.AxisListType


class Tk:
    __slots__ = ("name", "t", "w", "r", "dsem", "dcnt", "psum", "isdram")

    def __init__(self, name, t=None):
        self.name = name
        self.t = t
        self.w = None
        self.r = {}
        self.dsem = None
        self.dcnt = 0
        self.psum = False
        self.isdram = False

    def __getitem__(self, k):
        return self.t[k]


class Kern:
    EP = 16000

    def __init__(self, nc, es):
        self.nc = nc
        self.es = es
        self.H = {"pe": nc.tensor, "act": nc.scalar, "dve": nc.vector, "pool": nc.gpsimd, "sp": nc.sync}
        self.q = {e: [] for e in self.H}
        self.known = {e: {} for e in self.H}
        self.nsem = 0
        self.dsems = []
        self.dtks = []
        self.lastc = {e: -1 for e in self.H}

    def sb(self, name, shape, dt, stack=None):
        self.uid = getattr(self, "uid", 0) + 1
        name = f"{name}_{self.uid}"
        t = (stack or self.es).enter_context(self.nc.sbuf_tensor(name, list(shape), dt))
        tk = Tk(name, t)
        return tk

    def ps(self, name, shape, dt=F32, stack=None):
        self.uid = getattr(self, "uid", 0) + 1
        name = f"{name}_{self.uid}"
        t = (stack or self.es).enter_context(self.nc.psum_tensor(name, list(shape), dt))
        tk = Tk(name, t)
        tk.psum = True
        return tk

    def dram(self, name, shape, dt, kind="Internal"):
        t = self.nc.dram_tensor(name, list(shape), dt, kind=kind).ap()
        tk = Tk(name, t)
        tk.isdram = True
        return tk

    def newsem(self, name):
        self.nsem += 1
        return self.es.enter_context(self.nc.semaphore(name))

    def _need(self, e, tok, waits, same_ok):
        if tok is None:
            return
        if tok[0] == "e":
            _, f, idx = tok
            if f == e and (same_ok or e in ("pe", "sp")):
                return
            if self.known[e].get(f, -1) >= idx:
                return
            self.known[e][f] = idx
            waits.append(tok)
            self.q[f][idx][2] = True
        else:
            _, sid, val = tok
            if self.known[e].get(("d", sid), 0) >= val:
                return
            self.known[e][("d", sid)] = val
            waits.append(tok)

    def _deps(self, e, reads, writes, nowaw=False):
        waits = []
        for t in reads:
            self._need(e, t.w, waits, False)
            if t.psum:
                for tok in t.r.values():
                    self._need(e, tok, waits, True)
        for t in writes:
            if not (nowaw and t.w is not None and t.w[0] == "d"):
                self._need(e, t.w, waits, True)
            for tok in t.r.values():
                self._need(e, tok, waits, True)
        return waits

    def ins(self, e, reads, writes, _f, *args, _selfwait=None, **kwargs):
        fn = lambda: _f(*args, **kwargs)
        waits = self._deps(e, reads, writes)
        if _selfwait is not None and self.known[e].get(e, -1) < _selfwait[2]:
            self.known[e][e] = _selfwait[2]
            waits.append(_selfwait)
            self.q[e][_selfwait[2]][2] = True
        idx = len(self.q[e])
        self.q[e].append([fn, waits, False, None])
        self.lastc[e] = idx
        tok = ("e", e, idx)
        for t in reads:
            t.r[e] = tok
        for t in writes:
            t.w = tok
            t.r = {}
        return tok

    def dma(self, e, out_ap, in_ap, reads, write, nowaw=True, **kw):
        waits = self._deps(e, reads, [write], nowaw=nowaw)
        dst = write
        srcs = [t for t in reads if not t.isdram]
        if write.isdram and len(srcs) == 1:
            write = srcs[0]
        if write.dsem is None:
            free = getattr(self, "dfree", None)
            if free is None:
                free = self.dfree = []
                self.dcount = {}
            if free:
                write.dsem = free.pop()
            else:
                write.dsem = len(self.dsems)
                self.dsems.append(self.newsem(f"d{len(self.dsems)}"))
                self.dcount[write.dsem] = 0
            self.dtks.append(write)
        self.dcount[write.dsem] += 16
        write.dcnt = self.dcount[write.dsem]
        tok = ("d", write.dsem, write.dcnt)
        H = self.H[e]
        self.q[e].append([lambda: H.dma_start(out=out_ap, in_=in_ap, **kw), waits, False, tok])
        for t in reads:
            t.r[("d", write.dsem)] = tok
        dst.w = tok
        return tok

    def barrier(self):
        last = dict(self.lastc)
        for e in self.H:
            waits = []
            for f in self.H:
                if f != e and last[f] >= 0:
                    self._need(e, ("e", f, last[f]), waits, False)
            for tk in self.dtks:
                self._need(e, ("d", tk.dsem, tk.dcnt), waits, False)
            self.q[e].append([None, waits, False, None])
        for tk in self.dtks:
            self.dfree.append(tk.dsem)
            tk.dsem = None
        self.dtks = []

    def emit(self):
        nc = self.nc
        sigmap = {}
        for e, lst in self.q.items():
            cnt = 0
            m = {}
            sems = []
            for idx, rec in enumerate(lst):
                if rec[2]:
                    ep, v = divmod(cnt, self.EP)
                    if ep >= len(sems):
                        sems.append(self.newsem(f"e_{e}_{ep}"))
                    m[idx] = (sems[ep], v + 1)
                    cnt += 1
            sigmap[e] = m
        for e, lst in self.q.items():
            H = self.H[e]
            for idx, rec in enumerate(lst):
                fn, waits, sig, dtok = rec
                for tok in waits:
                    if tok[0] == "e":
                        s, v = sigmap[tok[1]][tok[2]]
                        H.wait_ge(s, v)
                    else:
                        H.wait_ge(self.dsems[tok[1]], tok[2])
                if fn is None:
                    assert not sig
                    continue
                ins = fn()
                if dtok is not None:
                    ins.then_inc(self.dsems[dtok[1]], 16)
                    assert not sig
                elif sig:
                    ins.then_inc(sigmap[e][idx][0], 1)
        print("emitted:", {e: len(l) for e, l in self.q.items()}, "sems", self.nsem)


from concourse.bass_utils import run_bass_kernel_spmd
import contextlib
import ml_dtypes

T_ALL = 4352
LC = 256
EPS = 1e-6


EXTRA_INPUTS = [
    ("bands", [128, 20, 128]), ("pw_bd", [2, 2, 128, 128]), ("pscale", [128, 2, 2]),
    ("w_uq", [2, 384, 768]), ("w_ukv_k", [2, 256, 512]), ("w_ukv_v", [2, 256, 512]),
    ("Cq", [128, 128]), ("Prot", [128, 128]), ("gq96", [128, 2]), ("gk96", [128, 2]),
    ("qng", [128, 2, 3]), ("kvng", [128, 2, 2]), ("cs", [32, 2, 4096]),
    ("r_mu", [128, 2, 9, 2]), ("r_kk", [128, 2, 2]), ("r_w0", [128, 2, 2, 2]), ("r_a0", [128, 2, 2, 2]), ("r_ka", [128, 2, 2, 2]),
    ("r_rk", [128, 2, 2]), ("r_w2", [2, 2, 128, 256]), ("r_a2", [2, 2, 128, 256]), ("r_g2", [2, 128, 256]),
    ("r_lnwb", [2, 128, 256]), ("r_lnbb", [2, 128, 256]), ("blk64", [128, 128]), ("ind2", [128, 2]),
    ("masks", [128, 4, 128]), ("rmask", [128, 512]), ("lvmask", [128, 14, 128]),
]


def tiles_of(n, last_layer=False):
    ts = [] if last_layer else [(0, LC, 1)]
    t = LC
    while t < T_ALL:
        ts.append((t, n, 0))
        t += n
    return ts


class Ctx:
    pass


def build(nc, es, dbg=None, stop_after=None):
    K = Kern(nc, es)
    C = Ctx()
    C.rdbg = (dbg or {}).get("rdbg", 9)
    C.pdbg = (dbg or {}).get("pdbg", 9)
    C.sdbg = (dbg or {}).get("sdbg", 9)
    C.bdbg = (dbg or {}).get("bdbg", 9)
    C.maxfly = (dbg or {}).get("maxfly", 4)
    C.ustop = (dbg or {}).get("ustop", 0)
    C.onebank = (dbg or {}).get("onebank", False)
    C.psteps = (dbg or {}).get("psteps", 1)
    C.pevery = (dbg or {}).get("pevery", 1)
    C.K = K
    C.nc = nc
    dbg = dbg or {}
    ext = lambda name, shape, dt=F32: K.dram(name, shape, dt, kind="ExternalInput")
    I = {}
    I["xin"] = ext("xin", [T_ALL, 1024])
    I["ccT"] = ext("ccT", [128, 16])
    I["w_ada"] = ext("w_ada", [2, 1024, 6144])
    I["badaT"] = ext("badaT", [128, 96])
    I["ng"] = ext("ng", [128, 32])
    I["w_in"] = ext("w_in", [2, 1024, 5152])
    I["w_br"] = ext("w_br", [2, 1024, 1024])
    I["w_o"] = ext("w_o", [2, 1024, 1024])
    I["mlp_w1"] = ext("mlp_w1", [2, 1024, 4096])
    I["mlp_w2"] = ext("mlp_w2", [2, 4096, 1024])
    I["ident"] = ext("ident", [128, 128])
    for nm, shp in EXTRA_INPUTS:
        I[nm] = ext(nm, shp)
    C.I = I
    out = K.dram("out", [4096, 1024], F32, kind="ExternalOutput")
    C.out = out
    douts = dbg.get("outs", ())
    scr = lambda name, shape, dt: K.dram(name, shape, dt, kind=("ExternalOutput" if name in douts else "Internal"))
    C.scr = scr
    C.xT = [scr("xT0", [1024, T_ALL], F32), scr("xT1", [1024, T_ALL], F32)]
    C.zT = scr("zT", [15, 128, T_ALL], F32)
    C.u_d = scr("u_d", [T_ALL, 256], BF16)
    C.obr = scr("obr", [1024, T_ALL], BF16)
    C.moddbg = scr("moddbg", [128, 192], F32)
    C.gtok_d = scr("gtok", [T_ALL, 256], BF16)
    if "obr_in" in dbg:
        C.obr_in = ext("obr_in", [1024, T_ALL])
    C.ident = K.sb("ident_s", [128, 128], F32)
    C.identb = K.sb("identb_s", [128, 128], BF16)
    C.ones = K.sb("ones_s", [128, 128], F32)
    C.epsb = K.sb("epsb_s", [128, 1], F32)
    C.gneps = K.sb("gneps_s", [128, 1], F32)
    C.mod = K.sb("mod_s", [128, 2, 48, 2], F32)
    C.gm = K.sb("gm_s", [128, 2, 2, 8, 2], F32)
    C.ng = K.sb("ng_s", [128, 2, 2, 8], F32)
    K.dma("sp", C.ident[:], I["ident"][:, :], [I["ident"]], C.ident)
    K.dma("sp", C.ng[:].rearrange("p a b c -> p (a b c)"), I["ng"][:, :], [I["ng"]], C.ng)
    K.ins("dve", [], [C.ones], nc.vector.memset, C.ones[:], 1.0)
    K.ins("dve", [], [C.epsb], nc.vector.memset, C.epsb[:], EPS)
    K.ins("dve", [], [C.gneps], nc.vector.memset, C.gneps[:], 64e-5)
    K.ins("dve", [C.ident], [C.identb], nc.vector.tensor_copy, out=C.identb[:], in_=C.ident[:])

    SKIP = dbg.get("skip_pre", False)
    if not SKIP:
        phase_M(C)
    K.barrier()
    if "moddbg" in douts:
        K.dma("sp", C.moddbg[:, :], C.mod[:].rearrange("p l m s -> p (l m s)"), [C.mod], C.moddbg)
    if not SKIP:
        phase_T0(C)
    K.barrier()
    cur = 0
    for li in range(2):
        last = li == 1
        if not SKIP:
            phase_A(C, li, cur)
        K.barrier()
        if stop_after == ("A", li):
            break
        if "obr_in" in dbg:
            phase_dbg_obr(C)
            K.barrier()
        else:
            mix = dbg.get("mix", ("P", "L", "R"))
            if "P" in mix:
                phase_P(C, li)
                K.barrier()
            if "L" in mix:
                phase_MLA(C, li)
                K.barrier()
            if "R" in mix:
                phase_R(C, li)
                K.barrier()
        if stop_after == ("X", li):
            break
        phase_C1(C, li, cur)
        K.barrier()
        cur ^= 1
        phase_C2(C, li, cur)
        K.barrier()
        cur ^= 1
        if stop_after == ("C", li):
            break
    K.barrier()
    K.emit()
    K.C = C
    return K


def phase_dbg_obr(C):
    K, nc = C.K, C.nc
    with contextlib.ExitStack() as st:
        a = K.sb("dbo_a", [128, 8, 512], F32, st)
        b = K.sb("dbo_b", [128, 8, 512], BF16, st)
        for t0 in range(0, T_ALL, 512):
            n = min(512, T_ALL - t0)
            K.dma("sp", a[:, :, :n], C.obr_in[:, t0:t0 + n].rearrange("(c p) t -> p c t", p=128), [C.obr_in], a)
            K.ins("dve", [a], [b], nc.vector.tensor_copy, out=b[:, :, :n], in_=a[:, :, :n])
            K.dma("sp", C.obr[:, t0:t0 + n].rearrange("(c p) t -> p c t", p=128), b[:, :, :n], [b], C.obr)


def phase_M(C):
    K, nc, I = C.K, C.nc, C.I
    with contextlib.ExitStack() as st:
        cc = K.sb("m_cc", [128, 8, 2], F32, st)
        sc = K.sb("m_sc", [128, 8, 2], F32, st)
        bada = K.sb("m_bada", [128, 2, 48], F32, st)
        wb = [K.sb(f"m_w{j}", [128, 8, 768], F32, st) for j in range(2)]
        psMs = [K.ps(f"m_ps{j}", [128, 512], F32, st) for j in range(4)]
        mraw = K.sb("m_raw", [128, 48, 2], F32, st)
        K.dma("sp", cc[:].rearrange("p k s -> p (k s)"), I["ccT"][:, :], [I["ccT"]], cc)
        K.dma("sp", bada[:].rearrange("p l m -> p (l m)"), I["badaT"][:, :], [I["badaT"]], bada)
        K.ins("act", [cc], [sc], nc.scalar.activation, out=sc[:], in_=cc[:], func=AF.Silu)
        n = 0
        for li in range(2):
            for blk in range(8):
                w = wb[n % 2]
                n += 1
                for kc in range(8):
                    K.dma("sp" if kc % 2 == 0 else "act", w[:, kc, :], I["w_ada"][li, kc * 128:(kc + 1) * 128, blk * 768:(blk + 1) * 768], [I["w_ada"]], w)
                for m6 in range(6):
                    m = blk * 6 + m6
                    pm = psMs[m % 4]
                    for kc in range(8):
                        K.ins("pe", [w, sc], [pm], nc.tensor.matmul, pm[:, 0:2], lhsT=w[:, kc, m6 * 128:(m6 + 1) * 128], rhs=sc[:, kc, :], start=(kc == 0), stop=(kc == 7))
                    K.ins("dve", [pm], [mraw], nc.vector.tensor_copy, out=mraw[:, m, :], in_=pm[:, 0:2])
            K.ins("dve", [mraw, bada], [C.mod], nc.vector.tensor_tensor, out=C.mod[:, li, :, :], in0=mraw[:], in1=bada[:, li, :].unsqueeze(2).broadcast_to([128, 48, 2]), op=ALU.add)
            for which in range(2):
                K.ins("dve", [C.mod, C.ng], [C.gm], nc.vector.scalar_tensor_tensor,
                    out=C.gm[:, which, li, :, :], in0=C.mod[:, li, 8 + 24 * which:16 + 24 * which, :], scalar=1.0,
                    in1=C.ng[:, which, li, :].unsqueeze(2).broadcast_to([128, 8, 2]), op0=ALU.add, op1=ALU.mult)


def phase_T0(C):
    K, nc, I = C.K, C.nc, C.I
    with contextlib.ExitStack() as st:
        xin = [K.sb(f"t0_x{j}", [128, 4, 1024], F32, st) for j in range(2)]
        xo = [K.sb(f"t0_o{j}", [128, 8, 512], F32, st) for j in range(2)]
        pss = [K.ps(f"t0_p{j}", [128, 512], F32, st) for j in range(4)]
        npz = 0
        for it, t0 in enumerate(range(0, T_ALL, 512)):
            n = min(512, T_ALL - t0)
            nj = n // 128
            xi = xin[it % 2]
            o = xo[it % 2]
            K.dma("sp", xi[:, :nj, :], I["xin"][t0:t0 + n, :].rearrange("(j p) d -> p j d", p=128), [I["xin"]], xi)
            for c in range(8):
                p = pss[npz % 4]
                npz += 1
                for j in range(nj):
                    K.ins("pe", [xi, C.ident], [p], nc.tensor.transpose, out=p[:, j * 128:(j + 1) * 128], in_=xi[:, j, c * 128:(c + 1) * 128], identity=C.ident[:])
                if c % 2 == 0:
                    K.ins("dve", [p], [o], nc.vector.tensor_copy, out=o[:, c, :n], in_=p[:, :n])
                else:
                    K.ins("act", [p], [o], nc.scalar.copy, out=o[:, c, :n], in_=p[:, :n])
            K.dma("act", C.xT[0][:, t0:t0 + n].rearrange("(c p) t -> p c t", p=128), o[:, :, :n], [o], C.xT[0])


def compute_h_g(C, xt, wk, hT, psS, rstd, which, li, s, n):
    K, nc = C.K, C.nc
    K.ins("act", [xt], [wk], nc.scalar.activation, out=wk[:, :, :n], in_=xt[:, :, :n], func=AF.Square)
    yield
    for c in range(8):
        K.ins("pe", [C.ones, wk], [psS], nc.tensor.matmul, psS[:, :n], lhsT=C.ones[:], rhs=wk[:, c, :n], start=(c == 0), stop=(c == 7))
        if c % 4 == 3:
            yield
    K.ins("act", [psS, C.epsb], [rstd], nc.scalar.activation, out=rstd[:, :n], in_=psS[:, :n], func=AF.Ln, scale=1.0 / 1024.0, bias=C.epsb[:, 0:1])
    yield
    K.ins("act", [rstd], [rstd], nc.scalar.activation, out=rstd[:, :n], in_=rstd[:, :n], func=AF.Exp, scale=-0.5)
    yield
    shb = 0 if which == 0 else 24
    for c in range(8):
        K.ins("dve", [xt, C.gm, rstd], [wk], nc.vector.scalar_tensor_tensor, out=wk[:, c, :n], in0=xt[:, c, :n], scalar=C.gm[:, which, li, c, s:s + 1], in1=rstd[:, :n], op0=ALU.mult, op1=ALU.mult)
        yield
        K.ins("act", [wk, C.mod], [hT], nc.scalar.activation, out=hT[:, c, :n], in_=wk[:, c, :n], func=AF.Identity, bias=C.mod[:, li, shb + c, s:s + 1], scale=1.0)
        yield


def compute_h(C, xt, wk, hT, psS, rstd, which, li, s, n):
    for _ in compute_h_g(C, xt, wk, hT, psS, rstd, which, li, s, n):
        pass


def rr(*gens):
    gens = [g for g in gens if g is not None]
    while gens:
        for g in list(gens):
            try:
                next(g)
            except StopIteration:
                gens.remove(g)


def pipeline_tiles(ntiles, load_fn, h_fn, main_fn):
    load_fn(0)
    for _ in h_fn(0):
        pass
    for i in range(ntiles):
        nxt = None
        if i + 1 < ntiles:
            load_fn(i + 1)
            nxt = h_fn(i + 1)
        rr(main_fn(i), nxt)


def load_w_bf16(C, dst, src_ap_fn, nk, src_tk):
    K = C.K
    for kc in range(nk):
        K.dma("pool", dst[:, kc, :], src_ap_fn(kc), [src_tk], dst)


A_CHUNKS = [(256 + 128 * m, 128) for m in range(3)] + [(640, 128), (768, 128), (896, 32)] + [(928 + 128 * m, 128) for m in range(9)]


def phase_A(C, li, cur):
    K, nc, I = C.K, C.nc, C.I
    with contextlib.ExitStack() as st:
        wA = K.sb("a_w", [128, 8, 2080], BF16, st)
        load_w_bf16(C, wA, lambda kc: I["w_in"][li, kc * 128:(kc + 1) * 128, 0:2080], 8, I["w_in"])
        xts = [K.sb(f"a_x{j}", [128, 8, 512], F32, st) for j in range(2)]
        wk = K.sb("a_wk", [128, 8, 512], F32, st)
        hTs = [K.sb(f"a_h{j}", [128, 8, 512], BF16, st) for j in range(2)]
        rstd = K.sb("a_rstd", [128, 512], F32, st)
        stg = [K.sb(f"a_stg{j}", [128, 15, 512], F32, st) for j in range(1)]
        ustg = K.sb("a_ustg", [128, 4, 256], BF16, st)
        psS = K.ps("a_psS", [128, 512], F32, st)
        psP = [K.ps(f"a_psP{j}", [128, 512], F32, st) for j in range(4)]
        psU = [K.ps(f"a_psU{j}", [128, 512], F32, st) for j in range(2)]
        tiles = tiles_of(512)
        cntp = {"npp": 0}

        def load_fn(i):
            t0, n, s = tiles[i]
            xt = xts[i % 2]
            K.dma("sp", xt[:, :, :n], C.xT[cur][:, t0:t0 + n].rearrange("(c p) t -> p c t", p=128), [C.xT[cur]], xt)

        def h_fn(i):
            t0, n, s = tiles[i]
            return compute_h_g(C, xts[i % 2], wk, hTs[i % 2], psS, rstd, 0, li, s, n)

        def main_fn(i):
            t0, n, s = tiles[i]
            hT = hTs[i % 2]
            sg = stg[0]
            for mi, (c0, M) in enumerate(A_CHUNKS):
                p = psP[cntp["npp"] % 4]
                cntp["npp"] += 1
                for kc in range(8):
                    K.ins("pe", [wA, hT], [p], nc.tensor.matmul, p[:M, :n], lhsT=wA[:, kc, c0:c0 + M], rhs=hT[:, kc, :n], start=(kc == 0), stop=(kc == 7))
                if mi % 2 == 0:
                    K.ins("dve", [p], [sg], nc.vector.tensor_copy, out=sg[:M, mi, :n], in_=p[:M, :n])
                else:
                    K.ins("act", [p], [sg], nc.scalar.copy, out=sg[:M, mi, :n], in_=p[:M, :n])
                yield
            K.dma("sp", C.zT[:, :, t0:t0 + n].rearrange("m p t -> p m t"), sg[:, :, :n], [sg], C.zT)
            for j in range(n // 128):
                p = psU[j % 2]
                for kc in range(8):
                    K.ins("pe", [wA, hT], [p], nc.tensor.matmul, p[:, 0:256], lhsT=hT[:, kc, j * 128:(j + 1) * 128], rhs=wA[:, kc, 0:256], start=(kc == 0), stop=(kc == 7))
                K.ins("act", [p], [ustg], nc.scalar.copy, out=ustg[:, j, :], in_=p[:, 0:256])
                yield
            K.dma("sp", C.u_d[t0:t0 + n, :].rearrange("(j p) c -> p j c", p=128), ustg[:, :n // 128, :], [ustg], C.u_d)

        pipeline_tiles(len(tiles), load_fn, h_fn, main_fn)


BR_K = [(0, 2), (2, 4), (6, 2)]


def phase_C1(C, li, cur):
    K, nc, I = C.K, C.nc, C.I
    last = li == 1
    with contextlib.ExitStack() as st:
        wG = K.sb("c1_wg", [128, 8, 3072], BF16, st)
        wB = K.sb("c1_wb", [128, 8, 1024], BF16, st)
        wO = K.sb("c1_wo", [128, 8, 1024], BF16, st)
        load_w_bf16(C, wG, lambda kc: I["w_in"][li, kc * 128:(kc + 1) * 128, 2080:5152], 8, I["w_in"])
        load_w_bf16(C, wB, lambda kc: I["w_br"][li, kc * 128:(kc + 1) * 128, :], 8, I["w_br"])
        load_w_bf16(C, wO, lambda kc: I["w_o"][li, kc * 128:(kc + 1) * 128, :], 8, I["w_o"])
        xts = [K.sb(f"c1_x{j}", [128, 8, 512], F32, st) for j in range(2)]
        wk = K.sb("c1_wk", [128, 8, 512], F32, st)
        hTs = [K.sb(f"c1_h{j}", [128, 8, 512], BF16, st) for j in range(2)]
        obs = [K.sb(f"c1_ob{j}", [128, 8, 512], BF16, st) for j in range(2)]
        mT = K.sb("c1_m", [128, 8, 512], BF16, st)
        rstd = K.sb("c1_rstd", [128, 512], F32, st)
        sgs = [K.sb(f"c1_sg{j}", [128, 512], BF16, st) for j in range(3)]
        tt_ = [K.sb(f"c1_t{j}", [128, 512], F32, st) for j in range(3)]
        psS = K.ps("c1_psS", [128, 512], F32, st)
        psG = [K.ps(f"c1_psG{j}", [128, 512], F32, st) for j in range(3)]
        psB = [K.ps(f"c1_psB{j}", [128, 512], F32, st) for j in range(3)]
        psO = K.ps("c1_psO", [128, 512], F32, st)
        tiles = tiles_of(512, last)

        def load_fn(i):
            t0, n, s = tiles[i]
            K.dma("sp", xts[i % 2][:, :, :n], C.xT[cur][:, t0:t0 + n].rearrange("(c p) t -> p c t", p=128), [C.xT[cur]], xts[i % 2])
            K.dma("sp", obs[i % 2][:, :, :n], C.obr[:, t0:t0 + n].rearrange("(c p) t -> p c t", p=128), [C.obr], obs[i % 2])

        def h_fn(i):
            t0, n, s = tiles[i]
            return compute_h_g(C, xts[i % 2], wk, hTs[i % 2], psS, rstd, 0, li, s, n)

        def main_fn(i):
            t0, n, s = tiles[i]
            xt, hT, ob = xts[i % 2], hTs[i % 2], obs[i % 2]
            for mo in range(8):
                for b in range(3):
                    pg, pb = psG[b], psB[b]
                    gc = (b * 8 + mo) * 128
                    for kc in range(8):
                        K.ins("pe", [wG, hT], [pg], nc.tensor.matmul, pg[:, :n], lhsT=wG[:, kc, gc:gc + 128], rhs=hT[:, kc, :n], start=(kc == 0), stop=(kc == 7))
                    k0, nk = BR_K[b]
                    for kk in range(nk):
                        K.ins("pe", [wB, ob], [pb], nc.tensor.matmul, pb[:, :n], lhsT=wB[:, k0 + kk, mo * 128:(mo + 1) * 128], rhs=ob[:, k0 + kk, :n], start=(kk == 0), stop=(kk == nk - 1))
                    K.ins("act", [pg], [sgs[b]], nc.scalar.activation, out=sgs[b][:, :n], in_=pg[:, :n], func=AF.Sigmoid)
                    K.ins("dve", [pb, sgs[b]], [tt_[b]], nc.vector.tensor_tensor, out=tt_[b][:, :n], in0=pb[:, :n], in1=sgs[b][:, :n], op=ALU.mult)
                    yield
                K.ins("pool", [tt_[0], tt_[1]], [tt_[0]], nc.gpsimd.tensor_tensor, out=tt_[0][:, :n], in0=tt_[0][:, :n], in1=tt_[1][:, :n], op=ALU.add)
                K.ins("pool", [tt_[0], tt_[2]], [mT], nc.gpsimd.tensor_tensor, out=mT[:, mo, :n], in0=tt_[0][:, :n], in1=tt_[2][:, :n], op=ALU.add)
            for mo in range(8):
                for kc in range(8):
                    K.ins("pe", [wO, mT], [psO], nc.tensor.matmul, psO[:, :n], lhsT=wO[:, kc, mo * 128:(mo + 1) * 128], rhs=mT[:, kc, :n], start=(kc == 0), stop=(kc == 7))
                K.ins("dve", [psO, C.mod, xt], [xt], nc.vector.scalar_tensor_tensor, out=xt[:, mo, :n], in0=psO[:, :n], scalar=C.mod[:, li, 16 + mo, s:s + 1], in1=xt[:, mo, :n], op0=ALU.mult, op1=ALU.add)
                yield
            K.dma("act", C.xT[cur ^ 1][:, t0:t0 + n].rearrange("(c p) t -> p c t", p=128), xt[:, :, :n], [xt], C.xT[cur ^ 1])

        pipeline_tiles(len(tiles), load_fn, h_fn, main_fn)


def phase_C2(C, li, cur):
    K, nc, I = C.K, C.nc, C.I
    last = li == 1
    N = 256
    with contextlib.ExitStack() as st:
        w1 = K.sb("c2_w1", [128, 8, 4096], BF16, st)
        w2 = K.sb("c2_w2", [128, 32, 1024], BF16, st)
        load_w_bf16(C, w1, lambda kc: I["mlp_w1"][li, kc * 128:(kc + 1) * 128, :], 8, I["mlp_w1"])
        load_w_bf16(C, w2, lambda kc: I["mlp_w2"][li, kc * 128:(kc + 1) * 128, :], 32, I["mlp_w2"])
        xts = [K.sb(f"c2_x{j}", [128, 8, N], F32, st) for j in range(2)]
        wk = K.sb("c2_wk", [128, 8, N], F32, st)
        hTs = [K.sb(f"c2_h{j}", [128, 8, N], BF16, st) for j in range(2)]
        uT = K.sb("c2_u", [128, 32, N], BF16, st)
        rstd = K.sb("c2_rstd", [128, N], F32, st)
        rrb = [K.sb(f"c2_r{j}", [128, N], BF16, st) for j in range(3)]
        ostg = [K.sb(f"c2_os{j}", [128, 1024], F32, st) for j in range(2)] if last else None
        psS = K.ps("c2_psS", [128, 512], F32, st)
        ps1 = [K.ps(f"c2_p1{j}", [128, 512], F32, st) for j in range(3)]
        ps2 = [K.ps(f"c2_p2{j}", [128, 512], F32, st) for j in range(2)]
        psT = [K.ps(f"c2_pT{j}", [128, 512], F32, st) for j in range(2)] if last else None
        tiles = tiles_of(N, last)
        cn = {"n1": 0, "n2": 0, "nt": 0}

        def load_fn(i):
            t0, n, s = tiles[i]
            K.dma("sp", xts[i % 2][:, :, :n], C.xT[cur][:, t0:t0 + n].rearrange("(c p) t -> p c t", p=128), [C.xT[cur]], xts[i % 2])

        def h_fn(i):
            t0, n, s = tiles[i]
            return compute_h_g(C, xts[i % 2], wk, hTs[i % 2], psS, rstd, 1, li, s, n)

        def main_fn(i):
            t0, n, s = tiles[i]
            xt, hT = xts[i % 2], hTs[i % 2]
            for f in range(32):
                p = ps1[cn["n1"] % 3]
                r = rrb[cn["n1"] % 3]
                cn["n1"] += 1
                for kc in range(8):
                    K.ins("pe", [w1, hT], [p], nc.tensor.matmul, p[:, :n], lhsT=w1[:, kc, f * 128:(f + 1) * 128], rhs=hT[:, kc, :n], start=(kc == 0), stop=(kc == 7))
                K.ins("act", [p], [r], nc.scalar.activation, out=r[:, :n], in_=p[:, :n], func=AF.Relu)
                K.ins("dve", [r], [uT], nc.vector.tensor_tensor, out=uT[:, f, :n], in0=r[:, :n], in1=r[:, :n], op=ALU.mult)
                if f % 2 == 1:
                    yield
            for mo in range(8):
                p = ps2[cn["n2"] % 2]
                cn["n2"] += 1
                for f in range(32):
                    K.ins("pe", [w2, uT], [p], nc.tensor.matmul, p[:, :n], lhsT=w2[:, f, mo * 128:(mo + 1) * 128], rhs=uT[:, f, :n], start=(f == 0), stop=(f == 31))
                K.ins("dve", [p, C.mod, xt], [xt], nc.vector.scalar_tensor_tensor, out=xt[:, mo, :n], in0=p[:, :n], scalar=C.mod[:, li, 40 + mo, s:s + 1], in1=xt[:, mo, :n], op0=ALU.mult, op1=ALU.add)
                yield
            if not last:
                K.dma("act", C.xT[cur ^ 1][:, t0:t0 + n].rearrange("(c p) t -> p c t", p=128), xt[:, :, :n], [xt], C.xT[cur ^ 1])
            else:
                for j in range(n // 128):
                    og = ostg[cn["nt"] % 2]
                    for half in range(2):
                        p = psT[half]
                        for c4 in range(4):
                            c = half * 4 + c4
                            K.ins("pe", [xt, C.ident], [p], nc.tensor.transpose, out=p[:, c4 * 128:(c4 + 1) * 128], in_=xt[:, c, j * 128:(j + 1) * 128], identity=C.ident[:])
                        if half == 0:
                            K.ins("dve", [p], [og], nc.vector.tensor_copy, out=og[:, 0:512], in_=p[:, :])
                        else:
                            K.ins("act", [p], [og], nc.scalar.copy, out=og[:, 512:1024], in_=p[:, :])
                    cn["nt"] += 1
                    r0 = t0 - LC + j * 128
                    K.dma("act", C.out[r0:r0 + 128, :], og[:, :], [og], C.out)
                    yield

        pipeline_tiles(len(tiles), load_fn, h_fn, main_fn)


def host_inputs(inputs, b):
    f = lambda a: np.ascontiguousarray(a, dtype=np.float32)
    x, c, ctx, c_ctx = inputs["x"], inputs["c"], inputs["ctx"], inputs["c_ctx"]
    m = {}
    m["xin"] = f(np.concatenate([ctx[b], x[b]], axis=0))
    cc = np.stack([c[b], c_ctx], axis=-1)
    m["ccT"] = f(cc.reshape(8, 128, 2).transpose(1, 0, 2).reshape(128, 16))
    m["w_ada"] = f(inputs["w_ada"])
    m["badaT"] = f(inputs["b_ada"].reshape(2, 48, 128).transpose(2, 0, 1).reshape(128, 96))
    ng = np.stack([inputs["norm1_g"], inputs["norm2_g"]], axis=0)
    m["ng"] = f(ng.reshape(2, 2, 8, 128).transpose(3, 0, 1, 2).reshape(128, 32))
    m["w_in"] = f(inputs["w_in"])
    m["w_br"] = f(np.concatenate([inputs["w_br_pool"], inputs["w_br_mla"], inputs["w_br_rwkv"]], axis=1))
    m["w_o"] = f(inputs["w_o"])
    m["mlp_w1"] = f(inputs["mlp_w1"])
    m["mlp_w2"] = f(inputs["mlp_w2"])
    m["ident"] = np.eye(128, dtype=np.float32)
    m.update(host_consts())
    pw = inputs["pool_w"]
    pwbd = np.zeros((2, 2, 128, 128), np.float32)
    for g in range(4):
        a, gi = divmod(g, 2)
        pwbd[:, a, gi * 64:(gi + 1) * 64, gi * 64:(gi + 1) * 64] = pw[:, g]
    m["pw_bd"] = pwbd
    m["pscale"] = f(inputs["pool_scale"].reshape(2, 2, 128).transpose(2, 0, 1))
    m["w_uq"] = f(inputs["mla_w_uq"])
    wkv = inputs["mla_w_ukv"].reshape(2, 256, 8, 128)
    m["w_ukv_k"] = f(wkv[:, :, :, :64].reshape(2, 256, 512))
    m["w_ukv_v"] = f(wkv[:, :, :, 64:].reshape(2, 256, 512))
    g96 = np.zeros((128, 2), np.float32); g96[:96] = inputs["qk_gain_q"].T; m["gq96"] = g96
    g96 = np.zeros((128, 2), np.float32); g96[:96] = inputs["qk_gain_k"].T; m["gk96"] = g96
    m["qng"] = f(inputs["mla_q_norm"].reshape(2, 3, 128).transpose(2, 0, 1))
    m["kvng"] = f(inputs["mla_kv_norm"].reshape(2, 2, 128).transpose(2, 0, 1))
    m["r_mu"] = f(inputs["rwkv_mu"].reshape(2, 2, 9, 128).transpose(3, 0, 2, 1))
    m["r_kk"] = f(inputs["rwkv_kk"].reshape(2, 2, 128).transpose(2, 0, 1))
    for nm in ("w0", "a0", "ka"):
        m["r_" + nm] = f(inputs["rwkv_" + nm].reshape(2, 2, 2, 128).transpose(3, 0, 1, 2))
    m["r_rk"] = f(inputs["rwkv_rk"].reshape(2, 2, 128).transpose(2, 0, 1))
    for nm in ("w2", "a2"):
        z_ = np.zeros((2, 2, 128, 256), np.float32)
        for d_ in range(2):
            z_[:, d_, d_ * 64:(d_ + 1) * 64, :] = inputs["rwkv_" + nm][:, d_]
        m["r_" + nm] = z_
    m["r_g2"] = f(inputs["rwkv_g2"])
    m["r_lnwb"] = f(np.broadcast_to(inputs["rwkv_ln_w"][:, None, :], (2, 128, 256)))
    m["r_lnbb"] = f(np.broadcast_to(inputs["rwkv_ln_b"][:, None, :], (2, 128, 256)))
    return m


_HC = {}


def host_consts():
    if _HC:
        return _HC
    m = _HC
    bands = np.zeros((128, 20, 128), np.float32)
    for g, win in enumerate((2, 4, 8, 16)):
        Ls = 384
        B = np.zeros((Ls, Ls), np.float64)
        for t in range(Ls):
            lo, hi = max(t - win // 2, 0), min(t + win // 2, Ls)
            B[lo:hi, t] = 1.0 / (hi - lo)
            B[t, t] -= 1.0
        bands[:, g * 5 + 0] = B[0:128, 0:128]
        bands[:, g * 5 + 1] = B[128:256, 128:256]
        bands[:, g * 5 + 2] = B[256:384, 256:384]
        bands[:, g * 5 + 3] = B[0:128, 128:256]
        bands[:, g * 5 + 4] = B[256:384, 128:256]
    m["bands"] = bands
    Cq = np.zeros((128, 128), np.float32); Cq[0:64, 0:64] = 1.0 / 64; Cq[64:96, 64:96] = 1.0 / 32
    m["Cq"] = Cq
    Pr = np.zeros((128, 128), np.float32)
    for i in range(16):
        Pr[64 + 16 + i, 64 + i] = 1.0
        Pr[64 + i, 64 + 16 + i] = 1.0
    m["Prot"] = Pr
    tpos = np.arange(4096)
    row = (tpos // 64).astype(np.float32); col = (tpos % 64).astype(np.float32)
    inv = np.power(np.float32(10000.0), -np.arange(8, dtype=np.float32) / np.float32(8)).astype(np.float32)
    ang = np.concatenate([row[:, None] * inv, col[:, None] * inv], axis=-1).astype(np.float32)
    cs = np.zeros((32, 2, 4096), np.float32)
    cs[0:16, 0] = np.cos(ang).T; cs[16:32, 0] = np.cos(ang).T
    cs[0:16, 1] = -np.sin(ang).T; cs[16:32, 1] = np.sin(ang).T
    m["cs"] = cs
    blk = np.zeros((128, 128), np.float32); blk[0:64, 0:64] = 1; blk[64:, 64:] = 1
    m["blk64"] = blk
    ind2 = np.zeros((128, 2), np.float32); ind2[0:64, 0] = 1; ind2[64:, 1] = 1
    m["ind2"] = ind2
    rr, cc_ = np.meshgrid(np.arange(128), np.arange(128), indexing="ij")
    masks = np.stack([rr < cc_, rr <= cc_, rr > cc_, rr >= cc_], axis=1).astype(np.float32)
    m["masks"] = np.ascontiguousarray(masks)
    lv = np.zeros((128, 14, 128), np.float32)
    ii = np.arange(128)
    for li_, mlev in enumerate((1, 2, 4, 8, 16, 32, 64)):
        same = (ii[:, None] // (2 * mlev)) == (ii[None, :] // (2 * mlev))
        MA = same & ((ii[:, None] % (2 * mlev)) >= mlev) & ((ii[None, :] % (2 * mlev)) < mlev)
        lv[:, li_, :] = MA
        lv[:, 7 + li_, :] = MA.T
    m["lvmask"] = lv
    rmask = np.ones((128, 512), np.float32); rmask[:, ::128] = 0
    m["rmask"] = rmask
    return m


def kernel(**inputs):
    nc = bass.Bass("TRN2", target_bir_lowering=False)
    with contextlib.ExitStack() as es:
        build(nc, es)
    in_maps = [host_inputs(inputs, b) for b in range(8)]
    res = run_bass_kernel_spmd(nc, in_maps, core_ids=list(range(8)))
    return np.stack([np.asarray(r["out"], dtype=np.float32) for r in res.results], axis=0)


_PE_SIG = {}


def mm(C, out_tk, out_ap, l_tk, l_ap, r_tk, r_ap, start=True, stop=True):
    sig = (l_ap.base_partition(), l_ap.partition_size())
    prev = _PE_SIG.get(id(out_tk))
    sw = None
    if prev is not None and prev[0] is out_tk and prev[1] != sig and prev[1][1] < 128 and sig[1] < 128:
        sw = prev[2]
    tok = C.K.ins("pe", [l_tk, r_tk], [out_tk], C.nc.tensor.matmul, out_ap, lhsT=l_ap, rhs=r_ap, start=start, stop=stop, _selfwait=sw)
    _PE_SIG[id(out_tk)] = (out_tk, sig, tok)


def cp(C, eng, out_tk, out_ap, in_tk, in_ap):
    if eng == "act":
        C.K.ins("act", [in_tk], [out_tk], C.nc.scalar.copy, out=out_ap, in_=in_ap)
    elif eng == "dve":
        C.K.ins("dve", [in_tk], [out_tk], C.nc.vector.tensor_copy, out=out_ap, in_=in_ap)
    else:
        C.K.ins("pool", [in_tk], [out_tk], C.nc.gpsimd.tensor_copy, out=out_ap, in_=in_ap)


def tt(C, eng, out_tk, out_ap, a_tk, a_ap, b_tk, b_ap, op):
    f = C.nc.vector.tensor_tensor if eng == "dve" else C.nc.gpsimd.tensor_tensor
    C.K.ins(eng, [a_tk, b_tk], [out_tk], f, out=out_ap, in0=a_ap, in1=b_ap, op=op)


def stt(C, out_tk, out_ap, a_tk, a_ap, sc_tk, sc, b_tk, b_ap, op0, op1):
    rd = [a_tk, b_tk] + ([sc_tk] if sc_tk is not None else [])
    C.K.ins("dve", rd, [out_tk], C.nc.vector.scalar_tensor_tensor, out=out_ap, in0=a_ap, scalar=sc, in1=b_ap, op0=op0, op1=op1)


def act(C, out_tk, out_ap, in_tk, in_ap, func, extra=(), **kw):
    C.K.ins("act", [in_tk] + list(extra), [out_tk], C.nc.scalar.activation, out=out_ap, in_=in_ap, func=func, **kw)


def seq_tiles(last_layer, n=512):
    return tiles_of(n, last_layer)


def phase_P(C, li):
    K, nc, I = C.K, C.nc, C.I
    with contextlib.ExitStack() as st:
        uall = K.sb("p_u", [128, 34, 256], BF16, st)
        bands = K.sb("p_bands", [128, 20, 128], BF16, st)
        pw = K.sb("p_pw", [128, 2, 128], BF16, st)
        psc = K.sb("p_sc", [128, 2], F32, st)
        pooled = K.sb("p_pooled", [128, 2, 512], BF16, st)
        ost = K.sb("p_ost", [128, 2, 512], BF16, st)
        psg = [K.ps(f"p_psg{j}", [128, 512], F32, st) for j in range(4)]
        psy = [K.ps(f"p_psy{j}", [128, 512], F32, st) for j in range(2)]
        K.dma("sp", uall[:], C.u_d[:, :].rearrange("(j p) c -> p j c", p=128), [C.u_d], uall)
        K.dma("pool", bands[:], I["bands"][:, :, :], [I["bands"]], bands)
        K.dma("pool", pw[:], I["pw_bd"][li].rearrange("a p d -> p a d"), [I["pw_bd"]], pw)
        K.dma("sp", psc[:], I["pscale"][:, li, :], [I["pscale"]], psc)
        for (t0, n, s) in tiles_of(512):
            j0 = t0 // 128
            first, lastj = (0, 1) if s else (2, 33)
            for a in range(2):
                for gi in range(2):
                    g = 2 * a + gi
                    pg = psg[g]
                    for jj in range(n // 128):
                        j = j0 + jj
                        srcs = []
                        if j > first:
                            srcs.append((j - 1, 3))
                        srcs.append((j, 0 if j == first else (2 if j == lastj else 1)))
                        if j < lastj:
                            srcs.append((j + 1, 4))
                        for si, (sj, kind) in enumerate(srcs):
                            mm(C, pg, pg[:, jj * 128:(jj + 1) * 128], uall, uall[:, sj, a * 128:(a + 1) * 128], bands, bands[:, g * 5 + kind, :], start=(si == 0), stop=(si == len(srcs) - 1))
                    cp(C, "dve" if gi == 0 else "act", pooled, pooled[gi * 64:(gi + 1) * 64, a, :n], pg, pg[gi * 64:(gi + 1) * 64, :n])
                py = psy[a]
                mm(C, py, py[:, :n], pw, pw[:, a, :], pooled, pooled[:, a, :n])
                K.ins("dve", [py, psc], [ost], nc.vector.tensor_scalar, out=ost[:, a, :n], in0=py[:, :n], scalar1=psc[:, a:a + 1], scalar2=None, op0=ALU.mult)
            K.dma("sp", C.obr[0:256, t0:t0 + n].rearrange("(a p) t -> p a t", p=128), ost[:, :, :n], [ost], C.obr)


def rms_feat(C, x, ncn, n, gain, out, psS, rstd, wk, dim):
    K, nc = C.K, C.nc
    act(C, wk, wk[:, :ncn, :n], x, x[:, :ncn, :n], AF.Square)
    for c in range(ncn):
        mm(C, psS, psS[:, :n], C.ones, C.ones[:], wk, wk[:, c, :n], start=(c == 0), stop=(c == ncn - 1))
    act(C, rstd, rstd[:, :n], psS, psS[:, :n], AF.Ln, extra=[C.epsb], scale=1.0 / dim, bias=C.epsb[:, 0:1])
    act(C, rstd, rstd[:, :n], rstd, rstd[:, :n], AF.Exp, scale=-0.5)
    for c in range(ncn):
        stt(C, out, out[:, c, :n], x, x[:, c, :n], gain, gain[:, c:c + 1], rstd, rstd[:, :n], ALU.mult, ALU.mult)


def run_pool(factories, slots, extra=(), extra_steps=1):
    free = list(slots)
    active = []
    extra = list(extra)
    pend = list(factories)
    while pend or active or extra:
        while pend and free:
            sl = free.pop(0)
            active.append((pend.pop(0)(sl), sl))
        for item in list(active):
            try:
                next(item[0])
            except StopIteration:
                active.remove(item)
                free.append(item[1])
        for g in list(extra):
            try:
                for _ in range(extra_steps):
                    next(g)
            except StopIteration:
                extra.remove(g)


def phase_MLA(C, li):
    K, nc, I = C.K, C.nc, C.I
    last = li == 1
    SC = 96.0 ** -0.5
    with contextlib.ExitStack() as st:
        KT = K.sb("l_KT", [128, 8, T_ALL], BF16, st)
        KTh = [Tk(f"l_KT_h{h}", KT.t) for h in range(8)]
        V = K.sb("l_V", [128, 34, 8, 65], BF16, st)
        wuq = K.sb("l_wuq", [128, 3, 768], BF16, st)
        wk_ = K.sb("l_wukvk", [128, 2, 512], BF16, st)
        wv_ = K.sb("l_wukvv", [128, 2, 512], BF16, st)
        Cq = K.sb("l_Cq", [128, 128], F32, st)
        Prot = K.sb("l_Prot", [128, 128], BF16, st)
        gq = K.sb("l_gq", [128, 1], F32, st)
        gk = K.sb("l_gk", [128, 1], F32, st)
        qng = K.sb("l_qng", [128, 3], F32, st)
        kvng = K.sb("l_kvng", [128, 2], F32, st)
        load_w_bf16(C, wuq, lambda kc: I["w_uq"][li, kc * 128:(kc + 1) * 128, :], 3, I["w_uq"])
        load_w_bf16(C, wk_, lambda kc: I["w_ukv_k"][li, kc * 128:(kc + 1) * 128, :], 2, I["w_ukv_k"])
        load_w_bf16(C, wv_, lambda kc: I["w_ukv_v"][li, kc * 128:(kc + 1) * 128, :], 2, I["w_ukv_v"])
        K.dma("sp", Cq[:], I["Cq"][:, :], [I["Cq"]], Cq)
        K.dma("pool", Prot[:], I["Prot"][:, :], [I["Prot"]], Prot)
        gqf = K.sb("l_gqf", [128, 2], F32, st)
        gkf = K.sb("l_gkf", [128, 2], F32, st)
        K.dma("sp", gqf[:], I["gq96"][:, :], [I["gq96"]], gqf)
        K.dma("sp", gkf[:], I["gk96"][:, :], [I["gk96"]], gkf)
        cp(C, "dve", gq, gq[:], gqf, gqf[:, li:li + 1])
        cp(C, "dve", gk, gk[:], gkf, gkf[:, li:li + 1])
        K.dma("sp", qng[:], I["qng"][:, li, :], [I["qng"]], qng)
        K.dma("sp", kvng[:], I["kvng"][:, li, :], [I["kvng"]], kvng)
        xin = K.sb("l_xin", [128, 3, 512], F32, st)
        xkr = K.sb("l_xkr", [128, 512], F32, st)
        xn = K.sb("l_xn", [128, 3, 512], BF16, st)
        wk = K.sb("l_wk", [128, 3, 512], F32, st)
        rstd = K.sb("l_rstd", [128, 512], F32, st)
        cs = K.sb("l_cs", [128, 2, 512], F32, st)
        QTs = [K.sb(f"l_QT{j}", [128, 8, 512], BF16, st) for j in range(2)]
        QTh = [[Tk(f"l_QT{j}_h{h}", QTs[j].t) for h in range(8)] for j in range(2)]
        PT = [K.sb(f"l_PT{j}", [128, 512], BF16, st) for j in range(4)]
        rsb = K.sb("l_rsb", [128, 512], F32, st)
        bcs = K.sb("l_bcs", [128, 512], F32, st)
        ost = K.sb("l_ost", [128, 8, 512], BF16, st)
        psS = K.ps("l_psS", [128, 512], F32, st)
        psSc = [K.ps(f"l_psSc{j}", [128, 512], F32, st) for j in range(3)]
        psO = [K.ps(f"l_psO{j}", [128, 512], F32, st) for j in range(2)]
        psF = psS
        slots = []
        for j in range(2):
            slots.append(dict(
                raw=K.sb(f"l_raw{j}", [128, 512], F32, st), sq=K.sb(f"l_sq{j}", [128, 512], F32, st),
                r2=K.sb(f"l_r2{j}", [128, 512], F32, st), qn=K.sb(f"l_qn{j}", [128, 512], F32, st),
                qnb=K.sb(f"l_qnb{j}", [128, 512], BF16, st), tmp=K.sb(f"l_tmp{j}", [128, 512], F32, st),
                ps=K.ps(f"l_psH{j}", [128, 512], F32, st)))
        K.ins("dve", [], [V], nc.vector.memset, V[:, :, :, 64:65], 1.0)
        R96 = slice(64, 96)

        def head_norm_g(S_, rows, src_tk, src_ap, gain, n, lhs_ap):
            sq, r2, qn, ps = S_["sq"], S_["r2"], S_["qn"], S_["ps"]
            tt(C, "pool", sq, sq[rows, :n], src_tk, src_ap, src_tk, src_ap, ALU.mult)
            yield
            mm(C, ps, ps[0:96, :n], Cq, lhs_ap, sq, sq[rows, :n])
            yield
            act(C, r2, r2[rows, :n], ps, ps[rows, :n], AF.Ln, extra=[C.epsb], scale=1.0, bias=C.epsb[rows, 0:1])
            yield
            act(C, r2, r2[rows, :n], r2, r2[rows, :n], AF.Exp, scale=-0.5)
            yield
            stt(C, qn, qn[rows, :n], src_tk, src_ap, gain, gain[rows, 0:1], r2, r2[rows, :n], ALU.mult, ALU.mult)
            yield

        def rope_g(S_, rows, n, out_tk, out_ap):
            qn, qnb, tmp, ps = S_["qn"], S_["qnb"], S_["tmp"], S_["ps"]
            cp(C, "dve", qnb, qnb[rows, :n], qn, qn[rows, :n])
            yield
            mm(C, ps, ps[0:96, :n], Prot, Prot[rows, 0:96], qnb, qnb[rows, :n])
            yield
            tt(C, "dve", tmp, tmp[rows, :n], ps, ps[rows, :n], cs, cs[rows, 1, :n], ALU.mult)
            tt(C, "pool", qn, qn[rows, :n], qn, qn[rows, :n], cs, cs[rows, 0, :n], ALU.mult)
            yield
            tt(C, "dve", out_tk, out_ap, qn, qn[rows, :n], tmp, tmp[rows, :n], ALU.add)
            yield

        def k_head_f(h, t0, n):
            def g(S_):
                ps, raw, qn = S_["ps"], S_["raw"], S_["qn"]
                for kc in range(2):
                    mm(C, ps, ps[0:64, :n], wk_, wk_[:, kc, h * 64:(h + 1) * 64], xn, xn[:, kc, :n], start=(kc == 0), stop=(kc == 1))
                yield
                cp(C, "dve", raw, raw[0:64, :n], ps, ps[0:64, :n])
                yield
                yield from head_norm_g(S_, slice(0, 64), raw, raw[0:64, :n], gk, n, Cq[0:64, 0:96])
                cp(C, "pool", KTh[h], KT[0:64, h, t0:t0 + n], qn, qn[0:64, :n])
                yield
            return g

        def k_kr_f(t0, n, s):
            def g(S_):
                qn = S_["qn"]
                yield from head_norm_g(S_, R96, xkr, xkr[R96, :n], gk, n, Cq[64:96, 0:96])
                if not s:
                    yield from rope_g(S_, R96, n, qn, qn[R96, :n])
                K.ins("dve", [qn], KTh, nc.vector.tensor_copy, out=KT[64:96, :, t0:t0 + n], in_=qn[64:96, :n].unsqueeze(1).broadcast_to([32, 8, n]))
                yield
            return g

        for (t0, n, s) in tiles_of(512):
            K.dma("sp", xin[:, 0:2, :n], C.zT[3:5, :, t0:t0 + n].rearrange("m p t -> p m t"), [C.zT], xin)
            K.dma("sp", xkr[64:96, :n], C.zT[5, 0:32, t0:t0 + n], [C.zT], xkr)
            if not s:
                K.dma("sp", cs[64:96, :, :n], I["cs"][:, :, t0 - LC:t0 - LC + n], [I["cs"]], cs)
            rms_feat(C, xin, 2, n, kvng, xn, psS, rstd, wk, 256.0)
            for jj in range(n // 128):
                pa = psSc[jj % 2]
                for kc in range(2):
                    mm(C, pa, pa[:, :], xn, xn[:, kc, jj * 128:(jj + 1) * 128], wv_, wv_[:, kc, :], start=(kc == 0), stop=(kc == 1))
                cp(C, "act", V, V[:, t0 // 128 + jj, :, 0:64], pa, pa[:, :].rearrange("p (h d) -> p h d", d=64))
            run_pool([k_kr_f(t0, n, s)] + [k_head_f(h, t0, n) for h in range(8)], slots)

        def q_head_f(h, n, s, qb):
            def g(S_):
                ps, raw, qn = S_["ps"], S_["raw"], S_["qn"]
                QTt, QT = QTh[qb][h], QTs[qb]
                for kc in range(3):
                    mm(C, ps, ps[0:96, :n], wuq, wuq[:, kc, h * 96:(h + 1) * 96], xn, xn[:, kc, :n], start=(kc == 0), stop=(kc == 2))
                yield
                cp(C, "dve", raw, raw[0:96, :n], ps, ps[0:96, :n])
                yield
                yield from head_norm_g(S_, slice(0, 96), raw, raw[0:96, :n], gq, n, Cq[0:96, 0:96])
                cp(C, "pool", QTt, QT[0:64, h, :n], qn, qn[0:64, :n])
                if not s:
                    yield from rope_g(S_, R96, n, QTt, QT[R96, h, :n])
                else:
                    cp(C, "pool", QTt, QT[R96, h, :n], qn, qn[R96, :n])
                yield
            return g

        def q_prep(t0, n, s, qb):
            K.dma("sp", xin[:, 0:3, :n], C.zT[0:3, :, t0:t0 + n].rearrange("m p t -> p m t"), [C.zT], xin)
            if not s:
                K.dma("sp", cs[64:96, :, :n], I["cs"][:, :, t0 - LC:t0 - LC + n], [I["cs"]], cs)
            rms_feat(C, xin, 3, n, qng, xn, psS, rstd, wk, 384.0)
            return [q_head_f(h, n, s, qb) for h in range(8)]

        cnt = {"npt": 0}

        def attn_g(t0, n, s, qb):
            QT = QTs[qb]
            kts = [0, 1] if s else list(range(34))
            steps = [(h, ki, kt) for h in range(8) for ki, kt in enumerate(kts)]
            slot = {}

            def emit_S(i):
                h, ki, kt = steps[i]
                psc_ = psSc[cnt["npt"] % 3]
                pt = PT[cnt["npt"] % 4]
                cnt["npt"] += 1
                slot[i] = pt
                mm(C, psc_, psc_[:, :n], KTh[h], KT[0:96, h, kt * 128:(kt + 1) * 128], QTh[qb][h], QT[0:96, h, :n])
                act(C, pt, pt[:, :n], psc_, psc_[:, :n], AF.Exp, scale=SC)

            emit_S(0)
            if len(steps) > 1:
                emit_S(1)
            for i, (h, ki, kt) in enumerate(steps):
                if i + 2 < len(steps):
                    emit_S(i + 2)
                po = psO[h % 2]
                pt = slot.pop(i)
                mm(C, po, po[0:65, :n], V, V[:, kt, h, :], pt, pt[:, :n], start=(ki == 0), stop=(ki == len(kts) - 1))
                if ki == len(kts) - 1:
                    K.ins("dve", [po], [rsb], nc.vector.reciprocal, out=rsb[64:65, :n], in_=po[64:65, :n])
                    mm(C, psF, psF[0:64, :n], C.ones, C.ones[64:65, 0:64], rsb, rsb[64:65, :n])
                    cp(C, "dve", bcs, bcs[0:64, :n], psF, psF[0:64, :n])
                    tt(C, "dve", ost, ost[0:64, h, :n], po, po[0:64, :n], bcs, bcs[0:64, :n], ALU.mult)
                yield
            K.dma("sp", C.obr[256:768, t0:t0 + n].rearrange("(h p) t -> p h t", p=64), ost[0:64, :, :n], [ost], C.obr)

        qtiles = tiles_of(512, last)
        run_pool(q_prep(*qtiles[0], 0), slots)
        for i, (t0, n, s) in enumerate(qtiles):
            nxt = q_prep(*qtiles[i + 1], (i + 1) % 2) if i + 1 < len(qtiles) else []
            run_pool(nxt, slots, extra=[attn_g(t0, n, s, i % 2)], extra_steps=4)


def phase_R(C, li):
    K, nc, I = C.K, C.nc, C.I
    last = li == 1
    NCH = 34
    TN = 256
    MAXFLY = min(getattr(C, "maxfly", 6), 6)
    with contextlib.ExitStack() as st:
        C.dbgtk = getattr(C, "dbgtk", {})

        def sbt(name, shape, dt=F32):
            tk = K.sb(f"rw{li}_" + name, shape, dt, st)
            C.dbgtk[f"rw{li}_" + name] = tk
            return tk
        mu = sbt("mu", [128, 9, 2]); omm = sbt("omm", [128, 9])
        kkp = sbt("kkp", [128, 2]); w0 = sbt("w0", [128, 2, 2]); a0 = sbt("a0", [128, 2, 2]); ka = sbt("ka", [128, 2, 2]); omka = sbt("omka", [128, 2, 2])
        rk = sbt("rk", [128, 2]); w2 = sbt("w2", [128, 2, 256], BF16); a2 = sbt("a2", [128, 2, 256], BF16); zb7 = sbt("zb7", [128, TN], BF16)
        g2 = sbt("g2", [128, 256], BF16); lnw = sbt("lnw", [128, 256]); lnb = sbt("lnb", [128, 256])
        blk = sbt("blk", [128, 128]); ind2 = sbt("ind2", [128, 2]); masks = sbt("masks", [128, 4, 128]); rmask = sbt("rmask", [128, 512])
        lvm = sbt("lvm", [128, 14, 128], BF16)
        lvm2 = sbt("lvm2", [128, 14, 128], BF16)
        K.dma("sp", mu[:], I["r_mu"][:, li, :, :], [I["r_mu"]], mu)
        K.dma("sp", kkp[:], I["r_kk"][:, li, :], [I["r_kk"]], kkp)
        K.dma("sp", w0[:], I["r_w0"][:, li, :, :], [I["r_w0"]], w0)
        K.dma("sp", a0[:], I["r_a0"][:, li, :, :], [I["r_a0"]], a0)
        K.dma("sp", ka[:], I["r_ka"][:, li, :, :], [I["r_ka"]], ka)
        K.dma("sp", rk[:], I["r_rk"][:, li, :], [I["r_rk"]], rk)
        for d_ in range(2):
            K.dma("pool", w2[:, d_, :], I["r_w2"][li, d_], [I["r_w2"]], w2)
            K.dma("pool", a2[:, d_, :], I["r_a2"][li, d_], [I["r_a2"]], a2)
        K.dma("pool", g2[:], I["r_g2"][li], [I["r_g2"]], g2)
        K.dma("pool", lvm[:], I["lvmask"][:, :, :], [I["lvmask"]], lvm)
        K.dma("pool", lvm2[:, 0:7, :], I["lvmask"][:, 7:14, :], [I["lvmask"]], lvm2)
        K.dma("pool", lvm2[:, 7:14, :], I["lvmask"][:, 0:7, :], [I["lvmask"]], lvm2)
        K.dma("sp", lnw[:], I["r_lnwb"][li], [I["r_lnwb"]], lnw)
        K.dma("sp", lnb[:], I["r_lnbb"][li], [I["r_lnbb"]], lnb)
        K.dma("sp", blk[:], I["blk64"][:, :], [I["blk64"]], blk)
        K.dma("sp", ind2[:], I["ind2"][:, :], [I["ind2"]], ind2)
        K.dma("sp", masks[:], I["masks"][:, :, :], [I["masks"]], masks)
        K.dma("sp", rmask[:], I["rmask"][:, :], [I["rmask"]], rmask)
        tt(C, "dve", omm, omm[:], mu, mu[:, :, 0], mu, mu[:, :, 1], ALU.add)
        K.ins("dve", [omm], [omm], nc.vector.tensor_scalar, out=omm[:], in0=omm[:], scalar1=-1.0, scalar2=1.0, op0=ALU.mult, op1=ALU.add)
        K.ins("dve", [ka], [omka], nc.vector.tensor_scalar, out=omka[:], in0=ka[:], scalar1=-1.0, scalar2=1.0, op0=ALU.mult, op1=ALU.add)
        Yacc = sbt("Yacc", [128, NCH, 256]); Vtok = sbt("Vtok", [128, NCH, 256], BF16)
        VtokT = [Tk(f"Vtok_t{j}", Vtok.t) for j in range(NCH // 2)]
        bon = sbt("bon", [128, NCH, 4]); gst = sbt("gst", [128, 2, 256], BF16)
        S_ = [sbt(f"S{hp}", [128, 64]) for hp in range(2)]
        Sb_ = [sbt(f"Sb{hp}", [128, 64], BF16) for hp in range(2)]
        zr = sbt("zr", [128, 9, TN + 2]); zs = sbt("zs", [128, 9, TN])
        kkn = sbt("kkn", [128, 2, TN]); e1 = sbt("e1", [128, 2, TN]); e2 = sbt("e2", [128, 2, TN])
        xw = sbt("xw", [128, 2, TN]); aa = sbt("aa", [128, 2, TN]); kd = [sbt(f"kd{d}", [128, 2, TN]) for d in range(2)]
        bb = sbt("bb", [128, 2, TN]); cum = sbt("cum", [128, 2, TN]); tot = sbt("tot", [128, 2, 2])
        th = sbt("th", [128, TN], BF16); sgd = sbt("sgd", [128, TN], BF16)
        opsets = []
        for j in range(3):
            o = {nm: sbt(f"o{j}_" + nm, [128, 2, TN], BF16) for nm in ("at", "rt", "bt", "kt", "bh", "kh")}
            o["WC"] = sbt(f"o{j}_WC", [128, 2, 2])
            opsets.append(o)
        banks = [K.ps(f"r_bank{j}", [128, 512], F32, st) for j in range(8)]
        bsets = []
        for j in range(MAXFLY):
            b = dict(
                AT=sbt(f"u{j}_AT", [128, 2, 512], BF16), Lc=[sbt(f"u{j}_Lc{q}", [128, 2, 2, 128], BF16) for q in range(2)],
                QT=[sbt(f"u{j}_QT{q}", [128, 2, 2, 128], BF16) for q in range(2)], ZZ=sbt(f"u{j}_ZZ", [128, 2, 2, 128], BF16),
                P=sbt(f"u{j}_P", [128, 2, 128], BF16), tok3=sbt(f"u{j}_tok3", [128, 3, 128], BF16), rGH=sbt(f"u{j}_rGH", [128, 2, 128], BF16),
                NX=sbt(f"u{j}_NX", [128, 512], BF16), GHs=sbt(f"u{j}_GHs", [128, 2, 128], BF16), GT=sbt(f"u{j}_GT", [128, 128], BF16), Us=sbt(f"u{j}_Us", [128, 2, 64], BF16),
                ps=(banks[2 + j], banks[2 + j]))
            bsets.append(b)
        psW = [banks[0], banks[1]]
        psQ = [banks[0], banks[1]]
        ytmp = sbt("ytmp", [128, 256]); otok = sbt("otok", [128, 256]); st4 = sbt("st4", [128, 4]); st4b = sbt("st4b", [128, 4])
        obst = sbt("obst", [128, 2, 128], BF16); gl = sbt("gl", [128, 256], BF16)

        def prep(t0, n, s, d, first_pass, oset):
            s0, s1 = (0, LC) if s else (LC, T_ALL)
            lo, hi = max(t0 - 1, s0), min(t0 + n + 1, s1)
            if lo == t0 or hi == t0 + n:
                K.ins("pool", [], [zr], nc.gpsimd.memset, zr[:, :, :], 0.0)
            off = lo - (t0 - 1)
            K.dma("sp", zr[:, :, off:off + hi - lo], C.zT[6:15, :, lo:hi].rearrange("m p t -> p m t"), [C.zT], zr)
            yield
            for c in range(9):
                K.ins("dve", [zr, omm], [zs], nc.vector.tensor_scalar, out=zs[:, c, :n], in0=zr[:, c, 1:n + 1], scalar1=omm[:, c:c + 1], scalar2=None, op0=ALU.mult)
                stt(C, zs, zs[:, c, :n], zr, zr[:, c, 0:n], mu, mu[:, c, 0:1], zs, zs[:, c, :n], ALU.mult, ALU.add)
                stt(C, zs, zs[:, c, :n], zr, zr[:, c, 2:n + 2], mu, mu[:, c, 1:2], zs, zs[:, c, :n], ALU.mult, ALU.add)
                if c % 3 == 2:
                    yield
            r_, k_ = zs[:, 0:2, :n], zs[:, 2:4, :n]
            for hp in range(2):
                K.ins("dve", [zs, kkp], [kkn], nc.vector.tensor_scalar, out=kkn[:, hp, :n], in0=zs[:, 2 + hp, :n], scalar1=kkp[:, hp:hp + 1], scalar2=None, op0=ALU.mult)
            tt(C, "pool", e1, e1[:, :, :n], kkn, kkn[:, :, :n], kkn, kkn[:, :, :n], ALU.mult)
            yield
            for hp in range(2):
                pw_ = psW[hp]
                mm(C, pw_, pw_[:, :n], blk, blk[:], e1, e1[:, hp, :n])
                K.ins("dve", [pw_], [e2], nc.vector.tensor_scalar, out=e2[:, hp, :n], in0=pw_[:, :n], scalar1=1e-24, scalar2=None, op0=ALU.max)
            act(C, e2, e2[:, :, :n], e2, e2[:, :, :n], AF.Ln)
            yield
            act(C, e2, e2[:, :, :n], e2, e2[:, :, :n], AF.Exp, scale=-0.5)
            tt(C, "dve", kkn, kkn[:, :, :n], kkn, kkn[:, :, :n], e2, e2[:, :, :n], ALU.mult)
            yield
            dirs = (0, 1) if first_pass else (d,)
            cp(C, "act", zb7, zb7[:, :n], zs, zs[:, 7, :n])
            yield
            for dd in dirs:
                for hp in range(2):
                    pw_ = psW[hp]
                    mm(C, pw_, pw_[:, :n], a2, a2[:, dd, hp * 128:(hp + 1) * 128], zb7, zb7[:, :n])
                    K.ins("dve", [pw_, a0], [aa], nc.vector.tensor_scalar, out=aa[:, hp, :n], in0=pw_[:, :n], scalar1=a0[:, dd, hp:hp + 1], scalar2=None, op0=ALU.add)
                    act(C, aa, aa[:, hp, :n], aa, aa[:, hp, :n], AF.Sigmoid)
                    K.ins("dve", [aa, ka], [e1], nc.vector.tensor_scalar, out=e1[:, hp, :n], in0=aa[:, hp, :n], scalar1=ka[:, dd, hp:hp + 1], scalar2=None, op0=ALU.mult)
                    K.ins("dve", [e1, omka], [e1], nc.vector.tensor_scalar, out=e1[:, hp, :n], in0=e1[:, hp, :n], scalar1=omka[:, dd, hp:hp + 1], scalar2=None, op0=ALU.add)
                    yield
                tt(C, "dve", kd[dd], kd[dd][:, :, :n], e1, e1[:, :, :n], zs, k_, ALU.mult)
                if dd == d:
                    tt(C, "pool", bb, bb[:, :, :n], kkn, kkn[:, :, :n], aa, aa[:, :, :n], ALU.mult)
            if first_pass:
                tidx = t0 // TN
                tt(C, "pool", e1, e1[:, :, :n], kd[0], kd[0][:, :, :n], kd[1], kd[1][:, :, :n], ALU.add)
                tt(C, "dve", e1, e1[:, :, :n], e1, e1[:, :, :n], zs, r_, ALU.mult)
                for hp in range(2):
                    K.ins("dve", [e1, rk], [e1], nc.vector.tensor_scalar, out=e1[:, hp, :n], in0=e1[:, hp, :n], scalar1=rk[:, hp:hp + 1], scalar2=0.5, op0=ALU.mult, op1=ALU.mult)
                act(C, sgd, sgd[:, :n], zs, zs[:, 8, :n], AF.Sigmoid)
                yield
                for jj in range(n // 128):
                    cidx = t0 // 128 + jj
                    pq = psQ[jj % 2]
                    for hp in range(2):
                        mm(C, pq, pq[:, 256 + 2 * hp:258 + 2 * hp], e1, e1[:, hp, jj * 128:(jj + 1) * 128], ind2, ind2[:, :])
                    mm(C, pq, pq[:, 0:256], sgd, sgd[:, jj * 128:(jj + 1) * 128], g2, g2[:, :])
                    cp(C, "act", gst, gst[:, jj, :], pq, pq[:, 0:256])
                    cp(C, "dve", bon, bon[:, cidx, :], pq, pq[:, 256:260])
                    pv = psW[jj % 2]
                    for hp in range(2):
                        K.ins("pe", [zs, C.ident], [pv], nc.tensor.transpose, out=pv[:, hp * 128:(hp + 1) * 128], in_=zs[:, 4 + hp, jj * 128:(jj + 1) * 128], identity=C.ident[:])
                    cp(C, "act", VtokT[tidx], Vtok[:, cidx, :], pv, pv[:, 0:256])
                    yield
                K.dma("sp", C.gtok_d[t0:t0 + n, :].rearrange("(j p) c -> p j c", p=128), gst[:, :n // 128, :], [gst], C.gtok_d)
            act(C, th, th[:, :n], zs, zs[:, 6, :n], AF.Tanh)
            yield
            for hp in range(2):
                pw_ = psW[hp]
                mm(C, pw_, pw_[:, :n], w2, w2[:, d, hp * 128:(hp + 1) * 128], th, th[:, :n])
                K.ins("dve", [pw_, w0], [xw], nc.vector.tensor_scalar, out=xw[:, hp, :n], in0=pw_[:, :n], scalar1=w0[:, d, hp:hp + 1], scalar2=None, op0=ALU.add)
                act(C, xw, xw[:, hp, :n], xw, xw[:, hp, :n], AF.Sigmoid)
                yield
            K.ins("dve", [xw], [xw], nc.vector.tensor_scalar, out=xw[:, :, :n], in0=xw[:, :, :n], scalar1=-0.6065306597126334, scalar2=None, op0=ALU.mult)
            yield
            nchk = n // 128
            for hp in range(2):
                K.ins("dve", [rmask, xw], [cum], nc.vector.tensor_tensor_scan, out=cum[:, hp, :n], data0=rmask[:, :n], data1=xw[:, hp, :n], initial=0.0, op0=ALU.mult, op1=ALU.add)
            cum4 = cum[:, :, :n].rearrange("p h (c t) -> p h c t", t=128)
            cp(C, "dve", tot, tot[:, :, :nchk], cum, cum4[:, :, :, 127])
            yield
            totb = tot[:, :, :nchk].unsqueeze(3).broadcast_to([128, 2, nchk, 128])
            v4 = lambda tk: tk[:, :, :n].rearrange("p h (c t) -> p h c t", t=128)
            if d == 0:
                tt(C, "pool", e2, e2[:, :, :n], cum, cum[:, :, :n], xw, xw[:, :, :n], ALU.subtract)
                tt(C, "dve", e1, v4(e1), tot, totb, cum, cum4, ALU.subtract)
            else:
                tt(C, "dve", e2, v4(e2), tot, totb, cum, cum4, ALU.subtract)
                tt(C, "pool", e1, e1[:, :, :n], cum, cum[:, :, :n], xw, xw[:, :, :n], ALU.subtract)
                tt(C, "dve", cum, cum[:, :, :n], e2, e2[:, :, :n], xw, xw[:, :, :n], ALU.add)
            ci = cum[:, :, :n]
            WC = oset["WC"]
            act(C, WC, WC[:, :, :nchk], tot, tot[:, :, :nchk], AF.Exp)
            yield
            act(C, e2, e2[:, :, :n], e2, e2[:, :, :n], AF.Exp)
            stt(C, oset["at"], oset["at"][:, :, :n], kkn, kkn[:, :, :n], None, -1.0, e2, e2[:, :, :n], ALU.mult, ALU.mult)
            yield
            act(C, e2, e2[:, :, :n], cum, ci, AF.Exp)
            tt(C, "dve", oset["rt"], oset["rt"][:, :, :n], zs, r_, e2, e2[:, :, :n], ALU.mult)
            yield
            act(C, e2, e2[:, :, :n], cum, ci, AF.Exp, scale=-1.0)
            tt(C, "dve", oset["bt"], oset["bt"][:, :, :n], bb, bb[:, :, :n], e2, e2[:, :, :n], ALU.mult)
            tt(C, "pool", oset["kt"], oset["kt"][:, :, :n], kd[d], kd[d][:, :, :n], e2, e2[:, :, :n], ALU.mult)
            yield
            act(C, e1, e1[:, :, :n], e1, e1[:, :, :n], AF.Exp)
            tt(C, "dve", oset["bh"], oset["bh"][:, :, :n], bb, bb[:, :, :n], e1, e1[:, :, :n], ALU.mult)
            tt(C, "pool", oset["kh"], oset["kh"][:, :, :n], kd[d], kd[d][:, :, :n], e1, e1[:, :, :n], ALU.mult)

        done = {0: 0, 1: 0}

        free_sets = list(range(MAXFLY))

        def unit(cidx, jj, d, hp, emit_y, first_pass, oset, B, myn):
            if getattr(C, "rdbg", 9) < 2:
                done[hp] += 1
                return
            bidx = free_sets.pop(0)
            B = bsets[bidx]
            T_ = slice(jj * 128, (jj + 1) * 128)
            mS = 0 if d == 0 else 2
            at, rt, bt, kt, bh, kh, WC = (oset[nm] for nm in ("at", "rt", "bt", "kt", "bh", "kh", "WC"))
            AT, Lcs, QTs, ZZ, P, tok3, rGH, GHs, GT, Us = (B[nm] for nm in ("AT", "Lc", "QT", "ZZ", "P", "tok3", "rGH", "GHs", "GT", "Us"))
            pA, pB = B["ps"]
            if getattr(C, "onebank", False):
                pB = pA
            S, Sb = S_[hp], Sb_[hp]
            Vt = VtokT[cidx // 2]
            vsl = lambda hh: Vtok[:, cidx, hp * 128 + hh * 64: hp * 128 + hh * 64 + 64]
            mview = masks[:, mS:mS + 2, :].unsqueeze(1).broadcast_to([128, 2, 2, 128])
            for hh in range(2):
                R = slice(hh * 64, hh * 64 + 64)
                pa = pA if hh == 0 else pB
                mm(C, pa, pa[:, 0:128], bt, bt[R, hp, T_], at, at[R, hp, T_])
                mm(C, pa, pa[:, 128:256], bt, bt[R, hp, T_], rt, rt[R, hp, T_])
                mm(C, pa, pa[:, 256:384], kt, kt[R, hp, T_], at, at[R, hp, T_])
                mm(C, pa, pa[:, 384:512], kt, kt[R, hp, T_], rt, rt[R, hp, T_])
                tt(C, "dve", AT, AT[:, hh, :].rearrange("p (a m t) -> p a m t", a=2, m=2), pa, pa[:, :].rearrange("p (a m t) -> p a m t", a=2, m=2), masks, mview, ALU.mult)
            yield
            pn = pA
            for hh in range(2):
                R = slice(hh * 64, hh * 64 + 64)
                mm(C, pn, pn[:, hh * 128:(hh + 1) * 128], at, at[R, hp, T_], bt, bt[R, hp, T_])
                mm(C, pn, pn[:, 256 + hh * 128:384 + hh * 128], bt, bt[R, hp, T_], at, at[R, hp, T_])
            NX = B["NX"]
            cp(C, "act", NX, NX[:, :], pn, pn[:, :])
            yield
            lvD = lvm if d == 0 else lvm2
            nx4 = NX[:, :].rearrange("p (a h t) -> p a h t", a=2, h=2)
            lvl_no = [0]

            def make_L(lvl):
                Lc = Lcs[lvl % 2]
                mk = lvD[:, :, :].rearrange("p (a l) t -> p a l t", a=2)[:, :, lvl, :].unsqueeze(2).broadcast_to([128, 2, 2, 128])
                tt(C, "dve" if lvl % 2 == 0 else "pool", Lc, Lc[:, :, :, :], NX, nx4, lvD, mk, ALU.mult)
                return Lc

            idb = C.identb[:].unsqueeze(1).broadcast_to([128, 2, 128])
            Lc = make_L(0)
            QT = QTs[0]
            tt(C, "pool", QT, QT[:, :, 0, :], Lc, Lc[:, 1, :, :], C.identb, idb, ALU.add)
            tt(C, "pool", QT, QT[:, :, 1, :], Lc, Lc[:, 0, :, :], C.identb, idb, ALU.add)
            Lc = make_L(1)
            yield
            curq = 0
            pz, pq_ = pB, pA
            for lvl in range(1, 6):
                QT, QTn = QTs[curq], QTs[curq ^ 1]
                for hh in range(2):
                    mm(C, pz, pz[:, (hh * 2) * 128:(hh * 2 + 1) * 128], Lc, Lc[:, 0, hh, :], QT, QT[:, hh, 0, :])
                    mm(C, pz, pz[:, (hh * 2 + 1) * 128:(hh * 2 + 2) * 128], Lc, Lc[:, 1, hh, :], QT, QT[:, hh, 1, :])
                cp(C, "act", ZZ, ZZ[:, :, :, :], pz, pz[:, :].rearrange("p (h a t) -> p h a t", h=2, a=2))
                Lc = make_L(lvl + 1)
                yield
                for hh in range(2):
                    o0 = pq_[:, (hh * 2) * 128:(hh * 2 + 1) * 128]
                    o1 = pq_[:, (hh * 2 + 1) * 128:(hh * 2 + 2) * 128]
                    mm(C, pq_, o0, QT, QT[:, hh, 1, :], ZZ, ZZ[:, hh, 0, :], start=True, stop=False)
                    mm(C, pq_, o0, C.identb, C.identb[:], QT, QT[:, hh, 0, :], start=False, stop=True)
                    mm(C, pq_, o1, QT, QT[:, hh, 0, :], ZZ, ZZ[:, hh, 1, :], start=True, stop=False)
                    mm(C, pq_, o1, C.identb, C.identb[:], QT, QT[:, hh, 1, :], start=False, stop=True)
                cp(C, "act", QTn, QTn[:, :, :, :], pq_, pq_[:, :].rearrange("p (h a t) -> p h a t", h=2, a=2))
                curq ^= 1
                yield
            QT = QTs[curq]
            for hh in range(2):
                mm(C, pz, pz[:, hh * 128:(hh + 1) * 128], Lc, Lc[:, 0, hh, :], QT, QT[:, hh, 0, :])
            cp(C, "act", ZZ, ZZ[:, :, 0, :], pz, pz[:, 0:256].rearrange("p (h t) -> p h t", h=2))
            yield
            for hh in range(2):
                o0 = pq_[:, hh * 128:(hh + 1) * 128]
                mm(C, pq_, o0, QT, QT[:, hh, 1, :], ZZ, ZZ[:, hh, 0, :], start=True, stop=False)
                mm(C, pq_, o0, C.identb, C.identb[:], QT, QT[:, hh, 0, :], start=False, stop=True)
            cp(C, "act", P, P[:, :, :], pq_, pq_[:, 0:256].rearrange("p (h t) -> p h t", h=2))
            yield
            pt_ = pB
            for i3, src in enumerate((at, bh, kh)):
                mm(C, pt_, pt_[:, i3 * 128:(i3 + 1) * 128], src, src[:, hp, T_], C.identb, C.identb[:])
            cp(C, "act", tok3, tok3[:, :, :], pt_, pt_[:, 0:384].rearrange("p (i f) -> p i f", i=3))
            yield
            pzz = pA
            for hh in range(2):
                mm(C, pzz, pzz[:, hh * 64:(hh + 1) * 64], AT, AT[:, hh, 256:384], Vt, vsl(hh))
            for hh in range(2):
                cp(C, "pool", rGH, rGH[:, hh, 0:64], tok3, tok3[:, 0, hh * 64:(hh + 1) * 64])
            cp(C, "dve", rGH, rGH[:, :, 64:128], pzz, pzz[:, 0:128].rearrange("p (h v) -> p h v", h=2))
            yield
            pg = pB
            for hh in range(2):
                mm(C, pg, pg[:, hh * 128:(hh + 1) * 128], P, P[:, hh, :], rGH, rGH[:, hh, :])
                mm(C, pg, pg[hh * 64:(hh + 1) * 64, 256:384], tok3, tok3[:, 0, hh * 64:(hh + 1) * 64], P, P[:, hh, :])
            cp(C, "act", GHs, GHs[:, :, :], pg, pg[:, 0:256].rearrange("p (h f) -> p h f", h=2))
            cp(C, "dve", GT, GT[:, :], pg, pg[:, 256:384])
            yield
            while done[hp] < myn:
                yield
            pu = pA
            for hh in range(2):
                R = slice(hh * 64, hh * 64 + 64)
                mm(C, pu, pu[:, hh * 64:(hh + 1) * 64], C.identb, C.identb[:], GHs, GHs[:, hh, 64:128], start=True, stop=False)
                mm(C, pu, pu[:, hh * 64:(hh + 1) * 64], GT, GT[R, :], Sb, Sb[R, :], start=False, stop=True)
            cp(C, "act", Us, Us[:, :, :], pu, pu[:, 0:128].rearrange("p (h v) -> p h v", h=2))
            yield
            if emit_y:
                py = pB
                for hh in range(2):
                    R = slice(hh * 64, hh * 64 + 64)
                    o_ = py[:, hh * 64:(hh + 1) * 64]
                    mm(C, py, o_, rt, rt[R, hp, T_], Sb, Sb[R, :], start=True, stop=False)
                    mm(C, py, o_, AT, AT[:, hh, 128:256], Us, Us[:, hh, :], start=False, stop=False)
                    mm(C, py, o_, AT, AT[:, hh, 384:512], Vt, vsl(hh), start=False, stop=True)
                if first_pass:
                    cp(C, "dve", Yacc, Yacc[:, cidx, hp * 128:(hp + 1) * 128], py, py[:, 0:128])
                else:
                    tt(C, "dve", Yacc, Yacc[:, cidx, hp * 128:(hp + 1) * 128], py, py[:, 0:128], Yacc, Yacc[:, cidx, hp * 128:(hp + 1) * 128], ALU.add)
            ps_ = pA
            for hh in range(2):
                o_ = ps_[hh * 64:(hh + 1) * 64, 256:320]
                mm(C, ps_, o_, tok3, tok3[:, 1, hh * 64:(hh + 1) * 64], Us, Us[:, hh, :], start=True, stop=False)
                mm(C, ps_, o_, tok3, tok3[:, 2, hh * 64:(hh + 1) * 64], Vt, vsl(hh), start=False, stop=True)
            stt(C, S, S[:, :], S, S[:, :], WC, WC[:, hp, jj:jj + 1], ps_, ps_[:, 256:320], ALU.mult, ALU.add)
            cp(C, "act", Sb, Sb[:, :], S, S[:, :])
            done[hp] += 1
            free_sets.append(bidx)
            yield

        PSTEPS = getattr(C, "psteps", 1)
        PEVERY = getattr(C, "pevery", 1)

        def run_pass(d):
            tl = tiles_of(TN)
            order = tl if d == 0 else [tl[0]] + tl[:0:-1]
            done[0] = done[1] = 0
            seq = {0: 0, 1: 0}
            active = []
            state = {"prep": None}

            def step_all():
                for g in list(active):
                    try:
                        next(g)
                    except StopIteration:
                        active.remove(g)
                state["round"] = state.get("round", 0) + 1
                if state["prep"] is not None and state["round"] % PEVERY == 0:
                    try:
                        for _ in range(PSTEPS):
                            next(state["prep"])
                    except StopIteration:
                        state["prep"] = None

            def mk_prep(ti):
                t0, n, s = order[ti]
                return prep(t0, n, s, d, d == 0, opsets[ti % 3])

            for _ in mk_prep(0):
                pass
            for ti, (t0, n, s) in enumerate(order):
                if state["prep"] is not None:
                    for _ in state["prep"]:
                        pass
                state["prep"] = mk_prep(ti + 1) if ti + 1 < len(order) else None
                oset = opsets[ti % 3]
                jjs = list(range(n // 128))
                if d == 1:
                    jjs = jjs[::-1]
                for jj in jjs:
                    for hp in range(2):
                        active.append(unit(t0 // 128 + jj, jj, d, hp, not (last and s), d == 0, oset, None, seq[hp]))
                        seq[hp] += 1
                        while len(active) >= MAXFLY:
                            step_all()
            while active:
                step_all()

        for d in range(2):
            for hp in range(2):
                K.ins("dve", [], [S_[hp]], nc.vector.memset, S_[hp][:], 0.0)
                K.ins("dve", [], [Sb_[hp]], nc.vector.memset, Sb_[hp][:], 0.0)
            run_pass(d)
        for cidx in range(2 if last else 0, NCH if getattr(C, "rdbg", 9) >= 8 else 0):
            K.dma("sp", gl[:, :], C.gtok_d[cidx * 128:(cidx + 1) * 128, :], [C.gtok_d], gl)
            y4 = Yacc[:, cidx, :].rearrange("p (h v) -> p h v", h=4)
            K.ins("dve", [Yacc], [st4], nc.vector.tensor_reduce, out=st4[:, :], in_=y4, axis=AX.X, op=ALU.add)
            K.ins("dve", [st4], [st4], nc.vector.tensor_scalar, out=st4[:, :], in0=st4[:, :], scalar1=1.0 / 64.0, scalar2=None, op0=ALU.mult)
            yt4 = ytmp[:, :].rearrange("p (h v) -> p h v", h=4)
            tt(C, "dve", ytmp, yt4, Yacc, y4, st4, st4[:, :].unsqueeze(2).broadcast_to([128, 4, 64]), ALU.subtract)
            tt(C, "pool", otok, otok[:, :], ytmp, ytmp[:, :], ytmp, ytmp[:, :], ALU.mult)
            K.ins("dve", [otok], [st4b], nc.vector.tensor_reduce, out=st4b[:, :], in_=otok[:, :].rearrange("p (h v) -> p h v", h=4), axis=AX.X, op=ALU.add)
            act(C, st4b, st4b[:, :], st4b, st4b[:, :], AF.Sqrt, extra=[C.gneps], scale=1.0 / 64.0, bias=C.gneps[:, 0:1])
            K.ins("dve", [st4b], [st4b], nc.vector.reciprocal, out=st4b[:, :], in_=st4b[:, :])
            tt(C, "dve", ytmp, yt4, ytmp, yt4, st4b, st4b[:, :].unsqueeze(2).broadcast_to([128, 4, 64]), ALU.mult)
            tt(C, "pool", ytmp, ytmp[:, :], ytmp, ytmp[:, :], lnw, lnw[:, :], ALU.mult)
            tt(C, "dve", ytmp, ytmp[:, :], ytmp, ytmp[:, :], lnb, lnb[:, :], ALU.add)
            tt(C, "dve", otok, otok[:, :].rearrange("p (h v) -> p h v", h=4), Vtok, Vtok[:, cidx, :].rearrange("p (h v) -> p h v", h=4), bon, bon[:, cidx, :].unsqueeze(2).broadcast_to([128, 4, 64]), ALU.mult)
            tt(C, "pool", ytmp, ytmp[:, :], ytmp, ytmp[:, :], otok, otok[:, :], ALU.add)
            tt(C, "dve", otok, otok[:, :], ytmp, ytmp[:, :], gl, gl[:, :], ALU.mult)
            po = psQ[cidx % 2]
            for hp in range(2):
                K.ins("pe", [otok, C.ident], [po], nc.tensor.transpose, out=po[:, hp * 128:(hp + 1) * 128], in_=otok[:, hp * 128:(hp + 1) * 128], identity=C.ident[:])
            cp(C, "act", obst, obst[:, :, :], po, po[:, 0:256].rearrange("p (h t) -> p h t", h=2))
            K.dma("sp", C.obr[768:1024, cidx * 128:(cidx + 1) * 128].rearrange("(h p) t -> p h t", p=128), obst[:, :, :], [obst], C.obr)
```
